# Optimizing a Trainium2 kernel written in Bass

```python
import math
import jax, jax.numpy as jnp
from jax import lax
import numpy as np

D_MODEL = 1024
BATCH = 32
SEQ = 2048
DEPTH = 4

ATTN_WIDTH = D_MODEL // 2
SSM_WIDTH = D_MODEL - ATTN_WIDTH
HEAD_DIM = 64
N_HEADS = ATTN_WIDTH // HEAD_DIM
DILATION_PAIRS = ((128, 1), (512, 4), (2048, 16))
ROPE_THETA = 10000.0
SSM_GROUP_DIM = 16
SSM_GROUPS = SSM_WIDTH // SSM_GROUP_DIM
SSM_STATE = 64
DT_MIN = 0.001
DT_MAX = 0.1
D_FF = 4 * D_MODEL
IN_WIDTH = 3 * ATTN_WIDTH + SSM_WIDTH
LN_EPS = 1e-5
RMS_EPS = 1e-6
NEG_INF = -1e30
DEEPNORM_ALPHA = (2.0 * DEPTH) ** 0.25
DEEPNORM_BETA = (8.0 * DEPTH) ** -0.25

kernel_name = "hymba_s5_dilated_attn_deepnorm_trunk"


def layer_norm(x, g, b):
    xf = x.astype(jnp.float32)
    mu = jnp.mean(xf, axis=-1, keepdims=True)
    var = jnp.mean(jnp.square(xf - mu), axis=-1, keepdims=True)
    y = (xf - mu) * lax.rsqrt(var + LN_EPS) * g.astype(jnp.float32) + b.astype(jnp.float32)
    return y.astype(x.dtype)


def rms_norm(x, g):
    xf = x.astype(jnp.float32)
    y = xf * lax.rsqrt(jnp.mean(jnp.square(xf), axis=-1, keepdims=True) + RMS_EPS)
    return y * g.astype(jnp.float32)


def rope(t, positions):
    half = t.shape[-1] // 2
    inv_freq = ROPE_THETA ** (-jnp.arange(half, dtype=jnp.float32) * 2.0 / t.shape[-1])
    ang = positions.astype(jnp.float32)[..., None] * inv_freq
    cos = jnp.cos(ang)[:, :, None, :]
    sin = jnp.sin(ang)[:, :, None, :]
    t1, t2 = t[..., :half], t[..., half:]
    return jnp.concatenate([t1 * cos - t2 * sin, t1 * sin + t2 * cos], axis=-1)


def dilated_branch(q, k, v, window, dilation):
    bsz, s, h, e = q.shape
    nk = window // dilation
    qb_size = nk
    sub_len = s // dilation
    nb = -(-sub_len // qb_size)
    padded = nb * qb_size
    pad = padded - sub_len

    def to_sub(t):
        return t.reshape(bsz, sub_len, dilation, h, e).transpose(0, 2, 3, 1, 4)

    qs = jnp.pad(to_sub(q), ((0, 0), (0, 0), (0, 0), (0, pad), (0, 0)))
    ks = jnp.pad(to_sub(k), ((0, 0), (0, 0), (0, 0), (qb_size, pad), (0, 0)))
    vs = jnp.pad(to_sub(v), ((0, 0), (0, 0), (0, 0), (qb_size, pad), (0, 0)))
    qblk = qs.reshape(bsz, dilation, h, nb, qb_size, e)
    kb = ks.reshape(bsz, dilation, h, nb + 1, qb_size, e)
    vb = vs.reshape(bsz, dilation, h, nb + 1, qb_size, e)
    keys = jnp.concatenate([kb[:, :, :, :-1], kb[:, :, :, 1:]], axis=4)
    vals = jnp.concatenate([vb[:, :, :, :-1], vb[:, :, :, 1:]], axis=4)

    scores = jnp.einsum('bdhnqe,bdhnke->bdhnqk', qblk, keys)
    qi = jnp.arange(qb_size)[:, None]
    kj = jnp.arange(2 * qb_size)[None, :]
    dist = qi + qb_size - kj
    key_idx = jnp.arange(nb)[:, None, None] * qb_size - qb_size + kj[None]
    valid = (dist >= 0)[None] & (dist <= nk)[None] & (key_idx >= 0)
    scores = jnp.where(valid, scores, NEG_INF)
    m = jnp.max(scores, axis=-1, keepdims=True)
    p = jnp.exp(scores - m)
    den = jnp.sum(p, axis=-1, keepdims=True)
    o = jnp.einsum('bdhnqk,bdhnke->bdhnqe', p, vals) / den
    lse = (m + jnp.log(den))[..., 0]

    o = o.reshape(bsz, dilation, h, padded, e)[:, :, :, :sub_len]
    o = o.transpose(0, 3, 1, 2, 4).reshape(bsz, s, h, e)
    lse = lse.reshape(bsz, dilation, h, padded)[:, :, :, :sub_len]
    lse = lse.transpose(0, 3, 1, 2).reshape(bsz, s, h)
    return o, lse


def dilated_attention(q, k, v, positions):
    bsz, s, _ = q.shape
    q = rope(q.astype(jnp.float32).reshape(bsz, s, N_HEADS, HEAD_DIM), positions)
    k = rope(k.astype(jnp.float32).reshape(bsz, s, N_HEADS, HEAD_DIM), positions)
    v = v.astype(jnp.float32).reshape(bsz, s, N_HEADS, HEAD_DIM)
    q = q * (HEAD_DIM ** -0.5)
    outs, lses = [], []
    for window, dilation in DILATION_PAIRS:
        o, lse = dilated_branch(q, k, v, window, dilation)
        outs.append(o)
        lses.append(lse)
    w = jax.nn.softmax(jnp.stack(lses, axis=0), axis=0)
    o = jnp.sum(w[..., None] * jnp.stack(outs, axis=0), axis=0)
    return o.reshape(bsz, s, ATTN_WIDTH)


def _ssm_combine(e1, e2):
    a1r, a1i, b1r, b1i = e1
    a2r, a2i, b2r, b2i = e2
    ar = a2r * a1r - a2i * a1i
    ai = a2r * a1i + a2i * a1r
    br = a2r * b1r - a2i * b1i + b2r
    bi = a2r * b1i + a2i * b1r + b2i
    return (ar, ai, br, bi)


def s5_mixer(u, a_re, a_im, log_dt, b_re, b_im, c_re, c_im, d_skip, w_glu, b_glu):
    bsz, s, _ = u.shape
    f32 = jnp.float32
    uf = u.astype(f32).reshape(bsz, s, SSM_GROUPS, SSM_GROUP_DIM)
    a_re = a_re.astype(f32)
    a_im = a_im.astype(f32)
    dt = jnp.exp(log_dt.astype(f32))[:, None]
    mag = jnp.exp(a_re * dt)
    ang = a_im * dt
    lb_re = mag * jnp.cos(ang)
    lb_im = mag * jnp.sin(ang)
    den = a_re * a_re + a_im * a_im
    nr = lb_re - 1.0
    ni = lb_im
    cr = (nr * a_re + ni * a_im) / den
    ci = (ni * a_re - nr * a_im) / den
    b_re = b_re.astype(f32)
    b_im = b_im.astype(f32)
    bb_re = cr[..., None] * b_re - ci[..., None] * b_im
    bb_im = cr[..., None] * b_im + ci[..., None] * b_re
    bu_re = jnp.einsum('bsgn,gpn->bsgp', uf, bb_re)
    bu_im = jnp.einsum('bsgn,gpn->bsgp', uf, bb_im)
    lam_re = jnp.broadcast_to(lb_re[None, None], (1, s, SSM_GROUPS, SSM_STATE))
    lam_im = jnp.broadcast_to(lb_im[None, None], (1, s, SSM_GROUPS, SSM_STATE))
    _, _, xr, xi = lax.associative_scan(_ssm_combine, (lam_re, lam_im, bu_re, bu_im), axis=1)
    y = (jnp.einsum('bsgp,gnp->bsgn', xr, c_re.astype(f32))
         - jnp.einsum('bsgp,gnp->bsgn', xi, c_im.astype(f32))
         + d_skip.astype(f32) * uf)
    y = jax.nn.gelu(y.reshape(bsz, s, SSM_WIDTH))
    y = y * jax.nn.sigmoid(y @ w_glu.astype(f32) + b_glu.astype(f32))
    return y


def setup_inputs(seed: int = 0) -> dict:
    key = jax.random.key(seed)
    ks = jax.random.split(key, 32)
    L = DEPTH
    nrm = jax.random.normal
    f32 = jnp.float32
    x = nrm(ks[0], (BATCH, SEQ, D_MODEL), f32)
    positions = (jax.random.randint(ks[1], (BATCH, 1), 0, 1024, dtype=jnp.int32)
                 + jnp.arange(SEQ, dtype=jnp.int32)[None, :])
    w_in = nrm(ks[2], (L, D_MODEL, IN_WIDTH), f32) * D_MODEL ** -0.5
    attn_gain = 1.0 + 0.02 * nrm(ks[3], (L, ATTN_WIDTH), f32)
    ssm_gain = 1.0 + 0.02 * nrm(ks[4], (L, SSM_WIDTH), f32)
    ssm_a_re = -0.5 + 0.01 * nrm(ks[5], (L, SSM_GROUPS, SSM_STATE), f32)
    ssm_a_im = (math.pi * jnp.arange(SSM_STATE, dtype=f32)[None, None, :]
                + 0.01 * nrm(ks[6], (L, SSM_GROUPS, SSM_STATE), f32))
    ssm_log_dt = jax.random.uniform(ks[7], (L, SSM_GROUPS), f32,
                                    math.log(DT_MIN), math.log(DT_MAX))
    bs = (2.0 * SSM_GROUP_DIM) ** -0.5
    cs = (2.0 * SSM_STATE) ** -0.5
    ssm_b_re = nrm(ks[8], (L, SSM_GROUPS, SSM_STATE, SSM_GROUP_DIM), f32) * bs
    ssm_b_im = nrm(ks[9], (L, SSM_GROUPS, SSM_STATE, SSM_GROUP_DIM), f32) * bs
    ssm_c_re = nrm(ks[10], (L, SSM_GROUPS, SSM_GROUP_DIM, SSM_STATE), f32) * cs
    ssm_c_im = nrm(ks[11], (L, SSM_GROUPS, SSM_GROUP_DIM, SSM_STATE), f32) * cs
    ssm_d = nrm(ks[12], (L, SSM_GROUPS, SSM_GROUP_DIM), f32)
    w_glu = nrm(ks[13], (L, SSM_WIDTH, SSM_WIDTH), f32) * SSM_WIDTH ** -0.5
    b_glu = 0.01 * nrm(ks[14], (L, SSM_WIDTH), f32)
    w_out = nrm(ks[15], (L, D_MODEL, D_MODEL), f32) * D_MODEL ** -0.5 * DEEPNORM_BETA
    b_out = 0.01 * nrm(ks[16], (L, D_MODEL), f32)
    ln1_g = 1.0 + 0.02 * nrm(ks[17], (L, D_MODEL), f32)
    ln1_b = 0.01 * nrm(ks[18], (L, D_MODEL), f32)
    w_ff1 = nrm(ks[19], (L, D_MODEL, D_FF), f32) * D_MODEL ** -0.5
    b_ff1 = 0.01 * nrm(ks[20], (L, D_FF), f32)
    w_ff2 = nrm(ks[21], (L, D_FF, D_MODEL), f32) * D_FF ** -0.5 * DEEPNORM_BETA
    b_ff2 = 0.01 * nrm(ks[22], (L, D_MODEL), f32)
    ln2_g = 1.0 + 0.02 * nrm(ks[23], (L, D_MODEL), f32)
    ln2_b = 0.01 * nrm(ks[24], (L, D_MODEL), f32)
    return {"x": x, "positions": positions, "w_in": w_in, "attn_gain": attn_gain,
            "ssm_gain": ssm_gain, "ssm_a_re": ssm_a_re, "ssm_a_im": ssm_a_im,
            "ssm_log_dt": ssm_log_dt, "ssm_b_re": ssm_b_re, "ssm_b_im": ssm_b_im,
            "ssm_c_re": ssm_c_re, "ssm_c_im": ssm_c_im, "ssm_d": ssm_d,
            "w_glu": w_glu, "b_glu": b_glu, "w_out": w_out, "b_out": b_out,
            "ln1_g": ln1_g, "ln1_b": ln1_b, "w_ff1": w_ff1, "b_ff1": b_ff1,
            "w_ff2": w_ff2, "b_ff2": b_ff2, "ln2_g": ln2_g, "ln2_b": ln2_b}


def reference(x, positions, w_in, attn_gain, ssm_gain, ssm_a_re, ssm_a_im, ssm_log_dt,
              ssm_b_re, ssm_b_im, ssm_c_re, ssm_c_im, ssm_d, w_glu, b_glu, w_out, b_out,
              ln1_g, ln1_b, w_ff1, b_ff1, w_ff2, b_ff2, ln2_g, ln2_b):
    h = x
    for l in range(DEPTH):
        proj = h @ w_in[l]
        q = proj[..., :ATTN_WIDTH]
        k = proj[..., ATTN_WIDTH:2 * ATTN_WIDTH]
        v = proj[..., 2 * ATTN_WIDTH:3 * ATTN_WIDTH]
        u = proj[..., 3 * ATTN_WIDTH:]
        attn = dilated_attention(q, k, v, positions)
        ssm = s5_mixer(u, ssm_a_re[l], ssm_a_im[l], ssm_log_dt[l], ssm_b_re[l], ssm_b_im[l],
                       ssm_c_re[l], ssm_c_im[l], ssm_d[l], w_glu[l], b_glu[l])
        mixed = jnp.concatenate([rms_norm(attn, attn_gain[l]), rms_norm(ssm, ssm_gain[l])],
                                axis=-1).astype(h.dtype)
        mix_out = mixed @ w_out[l] + b_out[l]
        h = layer_norm(DEEPNORM_ALPHA * h + mix_out, ln1_g[l], ln1_b[l])
        ff = jnp.square(jax.nn.relu(h @ w_ff1[l] + b_ff1[l])) @ w_ff2[l] + b_ff2[l]
        h = layer_norm(DEEPNORM_ALPHA * h + ff, ln2_g[l], ln2_b[l])
    return h
```

```python
import math
import os
STAGE = int(os.environ.get("KSTAGE", "9"))
KSKIP = set(os.environ.get("KSKIP", "").split(","))
from contextlib import ExitStack
import numpy as np
import concourse.bass as bass
import concourse.mybir as mybir
from concourse.bass_utils import run_bass_kernel_spmd

F32 = mybir.dt.float32
BF16 = mybir.dt.bfloat16
I32 = mybir.dt.int32
ALU = mybir.AluOpType
AF = mybir.ActivationFunctionType
ENGS = ("pe", "act", "dve", "pool", "sp")

D = 1024
S = 2048
DEPTH = 4
NCORES = 8
ALPHA = (2.0 * DEPTH) ** 0.25
LN_EPS = 1e-5
RMS_EPS = 1e-6
TWO_PI = 2.0 * math.pi


class Buf:
    __slots__ = ("name", "w", "r", "excl")

    def __init__(self, name, excl=False):
        self.name = name
        self.w = None
        self.r = {}
        self.excl = excl


class Prog:
    def __init__(self, nc, es):
        self.nc = nc
        self.es = es
        self.q = {e: [] for e in ENGS}
        self.sems = {}
        self.cnt = {}
        self.cur = {}
        self.dead = set()
        self.epoch = -1
        self.waited = {e: {} for e in ENGS}
        self.dma_cnt = {}
        self.dma_keys = []
        self.dma_rr = 0
        self.n_inst = 0
        self.new_epoch()

    def new_epoch(self):
        for k in self.cur.values():
            self.dead.add(k)
        self.epoch += 1
        for e in ENGS:
            k = "%s#%d" % (e, self.epoch)
            self.cur[e] = k
            self.cnt[k] = 0
            self.sems[k] = self.es.enter_context(self.nc.semaphore("s_%s_%d" % (e, self.epoch)))

    def _need(self, e, deps):
        out = []
        ce = self.cur[e]
        for (k, v) in deps:
            if k in self.dead:
                continue
            if k == ce and (e == "pe" or v > self.cnt[ce]):
                continue
            if self.waited[e].get(k, 0) >= v:
                continue
            self.waited[e][k] = v
            out.append((k, v))
        return out

    def _deps(self, reads, writes):
        deps = set()
        for b in reads:
            if b.w is not None:
                deps.add(b.w)
            if b.excl:
                for kv in b.r.items():
                    deps.add(kv)
        for b in writes:
            if b.w is not None:
                deps.add(b.w)
            for kv in b.r.items():
                deps.add(kv)
        return deps

    def op(self, e, fn, reads=(), writes=(), sig=True):
        waits = self._need(e, self._deps(reads, writes))
        ce = self.cur[e]
        if sig:
            self.cnt[ce] += 1
        tick = self.cnt[ce] if sig else self.cnt[ce] + 1
        sem = self.sems[ce]
        sems = self.sems

        def emit(h):
            for (k, v) in waits:
                h.wait_ge(sems[k], v)
            ins = fn(h)
            if sig:
                ins.then_inc(sem, 1)
        self.q[e].append(emit)
        self.n_inst += 1
        for b in reads:
            if b.excl:
                b.w = (ce, tick)
                b.r = {}
            elif b.r.get(ce, 0) < tick:
                b.r[ce] = tick
        for b in writes:
            b.w = (ce, tick)
            b.r = {}
        return tick

    def dma(self, qe, semkey, fn, reads=(), writes=()):
        semkey = self.dma_keys[self.dma_rr % len(self.dma_keys)]
        self.dma_rr += 1
        deps = self._deps(reads, writes)
        prev = self.dma_cnt.get(semkey, 0)
        if prev > 0:
            deps.add((semkey, prev))
        waits = self._need(qe, deps)
        self.dma_cnt[semkey] = prev + 16
        val = self.dma_cnt[semkey]
        sems = self.sems
        sem = sems[semkey]

        def emit(h):
            for (k, v) in waits:
                h.wait_ge(sems[k], v)
            fn(h).then_inc(sem, 16)
        self.q[qe].append(emit)
        self.n_inst += 1
        for b in reads:
            if b.r.get(semkey, 0) < val:
                b.r[semkey] = val
        for b in writes:
            b.w = (semkey, val)
            b.r = {}
        return val

    def wait_all(self, e):
        deps = [(self.cur[k], self.cnt[self.cur[k]]) for k in ENGS if self.cnt[self.cur[k]] > 0]
        deps += list(self.dma_cnt.items())
        waits = self._need(e, deps)
        sems = self.sems

        def emit(h):
            for (k, v) in waits:
                h.wait_ge(sems[k], v)
        self.q[e].append(emit)

    def barrier(self):
        for e in ENGS:
            self.wait_all(e)

    def emit_all(self, block):
        prog = self

        @block.tensor
        def _(h):
            for f in prog.q["pe"]:
                f(h)

        @block.scalar
        def _(h):
            for f in prog.q["act"]:
                f(h)

        @block.vector
        def _(h):
            for f in prog.q["dve"]:
                f(h)

        @block.gpsimd
        def _(h):
            for f in prog.q["pool"]:
                f(h)

        @block.sync
        def _(h):
            for f in prog.q["sp"]:
                f(h)


def build(NL, NSEQ, dbg=None):
    nc = bass.Bass("TRN2", target_bir_lowering=False)
    es = ExitStack()

    MINIO = os.environ.get("KMINIO", "")

    def din(name, shape, dt=F32):
        if MINIO and name not in ("x", "positions", "c_ident", "c_mask", "c_invf"):
            return None
        return nc.dram_tensor(name, list(shape), dt, kind="ExternalInput").ap()

    x_d = din("x", [NSEQ, S, D])
    pos_d = din("positions", [NSEQ, S], I32)
    w_in_d = din("w_in", [NL, D, 2048])
    attn_gain_d = din("attn_gain", [NL, 512])
    ssm_gain_d = din("ssm_gain", [NL, 512])
    a_re_d = din("ssm_a_re", [NL, 32, 64])
    a_im_d = din("ssm_a_im", [NL, 32, 64])
    log_dt_d = din("ssm_log_dt", [NL, 32])
    b_re_d = din("ssm_b_re", [NL, 32, 64, 16])
    b_im_d = din("ssm_b_im", [NL, 32, 64, 16])
    c_re_d = din("ssm_c_re", [NL, 32, 16, 64])
    c_im_d = din("ssm_c_im", [NL, 32, 16, 64])
    d_d = din("ssm_d", [NL, 32, 16])
    w_glu_d = din("w_glu", [NL, 512, 512])
    b_glu_d = din("b_glu", [NL, 512])
    w_out_d = din("w_out", [NL, D, D])
    b_out_d = din("b_out", [NL, D])
    ln1_g_d = din("ln1_g", [NL, D])
    ln1_b_d = din("ln1_b", [NL, D])
    w_ff1_d = din("w_ff1", [NL, D, 4096])
    b_ff1_d = din("b_ff1", [NL, 4096])
    w_ff2_d = din("w_ff2", [NL, 4096, D])
    b_ff2_d = din("b_ff2", [NL, D])
    ln2_g_d = din("ln2_g", [NL, D])
    ln2_b_d = din("ln2_b", [NL, D])
    ident_d = din("c_ident", [128, 128])
    mask_d = din("c_mask", [128, 256])
    invf_d = din("c_invf", [128, 32])
    out_d = nc.dram_tensor("out", [NSEQ, S, D], F32, kind="ExternalOutput").ap()
    vscr = None if MINIO else nc.dram_tensor("vscr", [S, 512], BF16, kind="ExternalOutput").ap()
    dbg_d = None
    if dbg:
        dbg_d = nc.dram_tensor("dbg", [128, 8, S], F32, kind="ExternalOutput").ap()

    def sb(name, shape, dt):
        return es.enter_context(nc.sbuf_tensor(name, list(shape), dt))

    P = Prog(nc, es)
    for i in range(int(os.environ.get("KNDMA", "8"))):
        k = "dma%d" % i
        P.sems[k] = es.enter_context(nc.semaphore("d_" + k))
        P.dma_keys.append(k)

    h32 = sb("h32", [128, 8, S], F32)
    hb = sb("hb", [128, 8, S], BF16)
    R1 = sb("R1", [128, 8192], F32)
    R1b = R1.bitcast(BF16)
    qT = R1b[:, 0:8192].rearrange("p (c n) -> p c n", c=4)
    kT = R1b[:, 8192:16384].rearrange("p (c n) -> p c n", c=4)
    X0 = R1[:, 0:4096].rearrange("p (c n) -> p c n", c=2)
    X1 = R1[:, 4096:8192].rearrange("p (c n) -> p c n", c=2)
    W1q = R1b[:, 0:8192].rearrange("p (c n) -> p c n", c=8)
    W2q = R1b[:, 8192:16384].rearrange("p (c n) -> p c n", c=8)
    uT = sb("uT", [128, 4, S], BF16)
    R2 = sb("R2", [128, 2048], F32)
    Oacc = R2
    R2b = R2.bitcast(BF16)
    XB = R2b[:, 0:4096].rearrange("p (c n) -> p c n", c=2)
    aT = R2b[:, 0:4096].rearrange("p (c n) -> p c n", c=8)
    vh = sb("vh", [128, 1, 16, 128], BF16)
    PT = [sb("PT%d" % i, [128, 256], BF16) for i in range(2)]
    tmpA = sb("tmpA", [128, 2048], F32)
    tmpB = sb("tmpB", [128, 512], F32)
    tmpC = sb("tmpC", [128, 512], F32)
    tbf2 = sb("tbf2", [128, 512], BF16)
    wst = sb("wst", [128, 1024], F32)
    wb = sb("wb", [128, 1, 4096], BF16)
    xtm = tmpA[:, 0:1024]
    qk_tm = sb("qk_tm", [128, 512], BF16)
    v_tm = sb("v_tm", [128, 512], BF16)
    identf = sb("identf", [128, 128], F32)
    identb = sb("identb", [128, 128], BF16)
    onesb = sb("onesb", [128, 128], BF16)
    maskb = sb("maskb", [128, 256], BF16)
    invf = sb("invf", [128, 32], F32)
    posf = sb("posf", [128, 16], F32)
    posi = sb("posi", [128, 16], I32)
    cosT = sb("cosT", [128, 16, 32], F32)
    sinT = sb("sinT", [128, 16, 32], F32)
    prm = sb("prm", [128, 512], F32)
    prm_stage = sb("prm_stage", [128, 128], F32)
    sA = sb("sA", [128, 16, 8], F32)
    LPr = sb("LPr", [128, 11, 16], F32)
    LPi = sb("LPi", [128, 11, 16], F32)
    LPn = sb("LPn", [128, 11, 16], F32)
    Bre = sb("Bre", [128, 16, 16], F32)
    Bim = sb("Bim", [128, 16, 16], F32)
    bbr = sb("bbr", [128, 16, 16], F32)
    bbi = sb("bbi", [128, 16, 16], F32)
    CT = sb("CT", [128, 2, 4, 128], BF16)
    Zf = sb("Zf", [128, 2, 128], F32)
    BbT = sb("BbT", [128, 2, 128], BF16)
    Cm = sb("Cm", [128, 2, 128], BF16)
    Dd = sb("Dd", [128, 4, 128], BF16)
    cstage = sb("cstage", [128, 64], F32)

    ps = [es.enter_context(nc.psum_tensor("ps%d" % i, [128, 512], F32)) for i in range(8)]
    psb = [p.bitcast(BF16) for p in ps]
    psB = [Buf("ps%d" % i, excl=True) for i in range(8)]
    bank_ctr = [0]
    bank_pool = [list(range(8))]

    def bank():
        pool = bank_pool[0]
        i = pool[bank_ctr[0] % len(pool)]
        bank_ctr[0] += 1
        return i

    B = {}

    def bf(name):
        if name not in B:
            B[name] = Buf(name)
        return B[name]

    hB = [[bf("h%d_%d" % (c, t)) for t in range(4)] for c in range(8)]
    hbB = [[bf("hb%d_%d" % (c, t)) for t in range(4)] for c in range(8)]
    R1B = bf("R1")
    uTB = [[bf("uT%d_%d" % (c, t)) for t in range(4)] for c in range(4)]
    R2B = bf("R2")

    def allhb():
        return [b for row in hbB for b in row]

    def allh():
        return [b for row in hB for b in row]

    def alluT():
        return [b for row in uTB for b in row]

    def dve(fn, reads, writes):
        return P.op("dve", fn, reads, writes)

    def act(fn, reads, writes):
        return P.op("act", fn, reads, writes)

    def pool(fn, reads, writes):
        return P.op("pool", fn, reads, writes)

    def mm(out, lhsT, rhs, start, stop, reads, writes, sig=None):
        if sig is None:
            sig = stop
        return P.op("pe", lambda h: h.matmul(out, lhsT=lhsT, rhs=rhs, start=start, stop=stop),
                    reads, writes, sig=sig)

    def tr(out, in_, ident, reads, writes):
        if os.environ.get("KTR", "") == "mm" and in_.dtype == F32:
            return P.op("pe", lambda h: h.matmul(out, lhsT=in_, rhs=ident, start=True, stop=True), reads, writes)
        return P.op("pe", lambda h: h.transpose(out=out, in_=in_, identity=ident), reads, writes)

    def load_w(dst, src, dstB, nparts=128):
        a, n = dst.shape[1], dst.shape[2]
        per = max(1, 1024 // n)
        for a0 in range(0, a, per):
            a1 = min(a, a0 + per)
            st = wst[:, 0:(a1 - a0) * n].rearrange("p (a n) -> p a n", a=a1 - a0)
            P.dma("sp", "w", lambda h, st=st, a0=a0, a1=a1: h.dma_start(out=st, in_=src[:, a0:a1, :]),
                  writes=[bf("wst")])
            pool(lambda h, st=st, a0=a0, a1=a1: h.tensor_copy(out=dst[:, a0:a1, :], in_=st),
                 [bf("wst")], [dstB])

    P.dma("sp", "misc", lambda h: h.dma_start(out=identf[:], in_=ident_d), writes=[bf("identf")])
    P.dma("sp", "misc", lambda h: h.dma_start(out=tmpB[:, 0:256], in_=mask_d), writes=[bf("tmpB")])
    P.dma("sp", "misc", lambda h: h.dma_start(out=invf[:], in_=invf_d), writes=[bf("invf")])
    dve(lambda h: h.tensor_copy(out=identb[:], in_=identf[:]), [bf("identf")], [bf("identb")])
    dve(lambda h: h.tensor_copy(out=maskb[:], in_=tmpB[:, 0:256]), [bf("tmpB")], [bf("maskb")])
    dve(lambda h: h.memset(onesb[:], 1.0), [], [bf("onesb")])
    if "vh" not in KSKIP:
        dve(lambda h: h.memset(vh[:, 0, :, :], 1.0), [], [bf("vh0")])

    prm_cols = {}
    col = [0]

    def load_T(name, src2d, R):
        if "prm" in KSKIP:
            prm_cols[name] = col[0]; col[0] += R
            return
        c0 = col[0]
        col[0] += R
        prm_cols[name] = c0
        P.dma("sp", "misc", lambda h: h.dma_start(out=prm_stage[0:R, :], in_=src2d), writes=[bf("prm_stage")])
        b = bank()
        tr(ps[b][:, 0:R], prm_stage[0:R, :], identf[0:R, 0:R], [bf("prm_stage"), bf("identf")], [psB[b]])
        dve(lambda h: h.tensor_copy(out=prm[:, c0:c0 + R], in_=ps[b][:, 0:R]), [psB[b]], [bf("prm")])
        return c0

    if not MINIO:
        load_T("b_ff1", b_ff1_d.rearrange("l (c p) -> (l c) p", p=128), NL * 32)
        for nm, ap_ in [("b_out", b_out_d), ("ln1_g", ln1_g_d), ("ln1_b", ln1_b_d), ("b_ff2", b_ff2_d),
                        ("ln2_g", ln2_g_d), ("ln2_b", ln2_b_d)]:
            load_T(nm, ap_.rearrange("l (c p) -> (l c) p", p=128), NL * 8)
        for nm, ap_ in [("attn_gain", attn_gain_d), ("ssm_gain", ssm_gain_d), ("b_glu", b_glu_d)]:
            load_T(nm, ap_.rearrange("l (c p) -> (l c) p", p=128), NL * 4)
        load_T("ssm_d", d_d.rearrange("l (c g) n -> (l c) (g n)", g=8), NL * 4)
        for nm, ap_ in [("a_re", a_re_d), ("a_im", a_im_d)]:
            load_T(nm, ap_.rearrange("l (gp two) p -> (l gp) (two p)", two=2), NL * 16)
    assert col[0] <= 512

    def pcol(name, idx):
        c = prm_cols[name] + idx
        return prm[:, c:c + 1]

    def sin_of(dst, src, add, n, rd, wr):
        ti = tmpB.bitcast(I32)[:, 0:n]
        tf = tmpC[:, 0:n]
        tg = tmpB[:, 0:n] if False else None
        if len(dst.shape) == 3:
            a_, b_ = dst.shape[1], dst.shape[2]
            ti = ti.rearrange("p (a b) -> p a b", a=a_)
            tf = tf.rearrange("p (a b) -> p a b", a=a_)
        T = [bf("tmpB"), bf("tmpC")]
        dve(lambda h: h.tensor_scalar(out=dst, in0=src, scalar1=float(add), scalar2=None, op0=ALU.add), rd, wr)
        dve(lambda h: h.tensor_scalar(out=tf, in0=dst, scalar1=1.0 / TWO_PI, scalar2=None, op0=ALU.mult), wr, T)
        dve(lambda h: h.tensor_copy(out=ti, in_=tf), T, T)
        dve(lambda h: h.tensor_copy(out=tf, in_=ti), T, T)
        dve(lambda h: h.scalar_tensor_tensor(out=dst, in0=tf, scalar=-TWO_PI, in1=dst, op0=ALU.mult, op1=ALU.add), T + wr, wr)
        dve(lambda h: h.tensor_scalar(out=tf, in0=dst, scalar1=-math.pi, scalar2=1e30, op0=ALU.add, op1=ALU.mult), wr, T)
        dve(lambda h: h.tensor_scalar(out=tf, in0=tf, scalar1=0.0, scalar2=1.0, op0=ALU.max, op1=ALU.min), T, T)
        dve(lambda h: h.scalar_tensor_tensor(out=dst, in0=tf, scalar=-TWO_PI, in1=dst, op0=ALU.mult, op1=ALU.add), T + wr, wr)
        act(lambda h: h.activation(out=dst, in_=dst, func=AF.Sin), wr, wr)

    def ssm_setup(l):
        T_ = bf("ssmtab")
        for gpar in range(2):
            src = bass.AP(log_dt_d.tensor, l * 32 + gpar, [[0, 64], [2, 16], [1, 1]])
            P.dma("sp", "misc", lambda h, src=src, gpar=gpar: h.dma_start(
                out=sA[gpar * 64:(gpar + 1) * 64, :, 0:1], in_=src), writes=[T_])
        are = prm[:, prm_cols["a_re"] + l * 16: prm_cols["a_re"] + l * 16 + 16]
        aim = prm[:, prm_cols["a_im"] + l * 16: prm_cols["a_im"] + l * 16 + 16]
        dt_ = sA[:, :, 0]
        mag = sA[:, :, 1]
        ang = sA[:, :, 2]
        t3 = sA[:, :, 3]
        t4 = sA[:, :, 4]
        t5 = sA[:, :, 5]
        t6 = sA[:, :, 6]
        t7 = sA[:, :, 7]
        rd = [T_, bf("prm")]
        act(lambda h: h.activation(out=dt_, in_=dt_, func=AF.Exp), rd, [T_])
        dve(lambda h: h.tensor_tensor(out=mag, in0=are, in1=dt_, op=ALU.mult), rd, [T_])
        act(lambda h: h.activation(out=mag, in_=mag, func=AF.Exp), rd, [T_])
        dve(lambda h: h.tensor_tensor(out=ang, in0=aim, in1=dt_, op=ALU.mult), rd, [T_])
        sin_of(t3, ang, 0.0, 16, rd, [T_])
        sin_of(t4, ang, 0.5 * math.pi, 16, rd, [T_])
        lr = LPr[:, 0, :]
        li = LPi[:, 0, :]
        dve(lambda h: h.tensor_tensor(out=lr, in0=mag, in1=t4, op=ALU.mult), rd, [T_])
        dve(lambda h: h.tensor_tensor(out=li, in0=mag, in1=t3, op=ALU.mult), rd, [T_])
        for k in range(1, 11):
            pr, pi_, nr_, ni_ = LPr[:, k - 1, :], LPi[:, k - 1, :], LPr[:, k, :], LPi[:, k, :]
            dve(lambda h, pr=pr, pi_=pi_: h.tensor_tensor(out=t5, in0=pr, in1=pr, op=ALU.mult), rd, [T_])
            dve(lambda h, pr=pr, pi_=pi_: h.tensor_tensor(out=t6, in0=pi_, in1=pi_, op=ALU.mult), rd, [T_])
            dve(lambda h, nr_=nr_: h.tensor_tensor(out=nr_, in0=t5, in1=t6, op=ALU.subtract), rd, [T_])
            dve(lambda h, pr=pr, pi_=pi_: h.tensor_tensor(out=t5, in0=pr, in1=pi_, op=ALU.mult), rd, [T_])
            dve(lambda h, ni_=ni_: h.tensor_scalar(out=ni_, in0=t5, scalar1=2.0, scalar2=None, op0=ALU.mult), rd, [T_])
        dve(lambda h: h.tensor_scalar(out=LPn[:], in0=LPi[:], scalar1=-1.0, scalar2=None, op0=ALU.mult), rd, [T_])
        dve(lambda h: h.tensor_scalar(out=t3, in0=lr, scalar1=-1.0, scalar2=None, op0=ALU.add), rd, [T_])
        dve(lambda h: h.tensor_tensor(out=t4, in0=are, in1=are, op=ALU.mult), rd, [T_])
        dve(lambda h: h.tensor_tensor(out=t5, in0=aim, in1=aim, op=ALU.mult), rd, [T_])
        dve(lambda h: h.tensor_tensor(out=t4, in0=t4, in1=t5, op=ALU.add), rd, [T_])
        dve(lambda h: h.reciprocal(out=t4, in_=t4), rd, [T_])
        dve(lambda h: h.tensor_tensor(out=t5, in0=t3, in1=are, op=ALU.mult), rd, [T_])
        dve(lambda h: h.tensor_tensor(out=t6, in0=li, in1=aim, op=ALU.mult), rd, [T_])
        dve(lambda h: h.tensor_tensor(out=t5, in0=t5, in1=t6, op=ALU.add), rd, [T_])
        dve(lambda h: h.tensor_tensor(out=t5, in0=t5, in1=t4, op=ALU.mult), rd, [T_])
        dve(lambda h: h.tensor_tensor(out=t6, in0=li, in1=are, op=ALU.mult), rd, [T_])
        dve(lambda h: h.tensor_tensor(out=t7, in0=t3, in1=aim, op=ALU.mult), rd, [T_])
        dve(lambda h: h.tensor_tensor(out=t6, in0=t6, in1=t7, op=ALU.subtract), rd, [T_])
        dve(lambda h: h.tensor_tensor(out=t6, in0=t6, in1=t4, op=ALU.mult), rd, [T_])
        for (dst, src_d) in [(Bre, b_re_d), (Bim, b_im_d)]:
            src = src_d[l].rearrange("(gp two) p n -> (two p) gp n", two=2)
            P.dma("sp", "misc", lambda h, dst=dst, src=src: h.dma_start(out=dst[:], in_=src), writes=[T_])
        crb = bass.AP(sA, 5, [[128, 128], [8, 16], [0, 16]])
        cib = bass.AP(sA, 6, [[128, 128], [8, 16], [0, 16]])
        dve(lambda h: h.tensor_tensor(out=bbr[:], in0=Bre[:], in1=crb, op=ALU.mult), rd, [T_])
        dve(lambda h: h.tensor_tensor(out=bbi[:], in0=Bim[:], in1=cib, op=ALU.mult), rd, [T_])
        dve(lambda h: h.tensor_tensor(out=bbr[:], in0=bbr[:], in1=bbi[:], op=ALU.subtract), rd, [T_])
        dve(lambda h: h.tensor_tensor(out=bbi[:], in0=Bim[:], in1=crb, op=ALU.mult), rd, [T_])
        dve(lambda h: h.tensor_tensor(out=Bre[:], in0=Bre[:], in1=cib, op=ALU.mult), rd, [T_])
        dve(lambda h: h.tensor_tensor(out=bbi[:], in0=bbi[:], in1=Bre[:], op=ALU.add), rd, [T_])
        for ri, src_d in enumerate([c_re_d, c_im_d]):
            for j in range(4):
                src = src_d[l, 8 * j:8 * j + 8].rearrange("g n p -> (g n) p")
                P.dma("sp", "misc", lambda h, src=src: h.dma_start(out=cstage[:], in_=src), writes=[bf("cstage")])
                b = bank()
                tr(ps[b][0:64, 0:128], cstage[:, :], identf[:], [bf("cstage"), bf("identf")], [psB[b]])
                if ri == 0:
                    dve(lambda h, b=b, j=j: h.tensor_copy(out=CT[0:64, 0, j, :], in_=ps[b][0:64, 0:128]), [psB[b]], [T_])
                else:
                    dve(lambda h, b=b, j=j: h.tensor_scalar(out=CT[0:64, 1, j, :], in0=ps[b][0:64, 0:128],
                                                           scalar1=-1.0, scalar2=None, op0=ALU.mult), [psB[b]], [T_])
        for j in range(4):
            dc = pcol("ssm_d", l * 4 + j)
            dve(lambda h, j=j, dc=dc: h.tensor_scalar(out=Dd[:, j, :], in0=identf[:], scalar1=dc, scalar2=None,
                                                    op0=ALU.mult), [bf("identf"), bf("prm")], [T_])

    def layer_norm(l, gname, bname):
        for t in range(4):
            sl = slice(t * 512, (t + 1) * 512)
            b1 = bank()
            b2 = bank()
            for c in range(8):
                dve(lambda h, c=c, sl=sl: h.tensor_copy(out=tbf2[:], in_=h32[:, c, sl]), [hB[c][t]], [bf("tbf2")])
                mm(ps[b1][:], onesb[:], tbf2[:], c == 0, c == 7, [bf("tbf2"), bf("onesb")], [psB[b1]], sig=True)
                dve(lambda h, c=c, sl=sl: h.tensor_tensor(out=tbf2[:], in0=h32[:, c, sl], in1=h32[:, c, sl], op=ALU.mult),
                    [hB[c][t]], [bf("tbf2")])
                mm(ps[b2][:], onesb[:], tbf2[:], c == 0, c == 7, [bf("tbf2"), bf("onesb")], [psB[b2]], sig=True)
            act(lambda h, b1=b1: h.activation(out=tmpB[:], in_=ps[b1][:], func=AF.Identity, scale=1.0 / D), [psB[b1]], [bf("tmpB")])
            dve(lambda h: h.tensor_tensor(out=tmpC[:], in0=tmpB[:], in1=tmpB[:], op=ALU.mult), [bf("tmpB")], [bf("tmpC")])
            dve(lambda h, b2=b2: h.scalar_tensor_tensor(out=tmpC[:], in0=ps[b2][:], scalar=1.0 / D, in1=tmpC[:],
                                                 op0=ALU.mult, op1=ALU.subtract), [psB[b2], bf("tmpC")], [bf("tmpC")])
            act(lambda h: h.activation(out=tmpC[:], in_=tmpC[:], func=AF.Sqrt, bias=LN_EPS, scale=1.0), [bf("tmpC")], [bf("tmpC")])
            dve(lambda h: h.reciprocal(out=tmpC[:], in_=tmpC[:]), [bf("tmpC")], [bf("tmpC")])
            for c in range(8):
                g_ = pcol(gname, l * 8 + c)
                b_ = pcol(bname, l * 8 + c)
                dve(lambda h, c=c, sl=sl: h.tensor_tensor(out=h32[:, c, sl], in0=h32[:, c, sl], in1=tmpB[:], op=ALU.subtract),
                    [hB[c][t], bf("tmpB")], [hB[c][t]])
                dve(lambda h, c=c, sl=sl: h.tensor_tensor(out=h32[:, c, sl], in0=h32[:, c, sl], in1=tmpC[:], op=ALU.mult),
                    [hB[c][t], bf("tmpC")], [hB[c][t]])
                dve(lambda h, c=c, g_=g_, b_=b_, sl=sl: h.tensor_scalar(out=h32[:, c, sl], in0=h32[:, c, sl], scalar1=g_, scalar2=b_,
                                                                  op0=ALU.mult, op1=ALU.add), [hB[c][t], bf("prm")], [hB[c][t]])
                act(lambda h, c=c, sl=sl: h.copy(out=hb[:, c, sl], in_=h32[:, c, sl]), [hB[c][t]], [hbB[c][t]])

    def rms_half(l, c0, gname):
        bs = [bank() for _ in range(4)]
        for ci in range(4):
            c = c0 + ci
            for t in range(4):
                sl = slice(t * 512, (t + 1) * 512)
                dve(lambda h, c=c, sl=sl: h.tensor_tensor(out=tbf2[:], in0=hb[:, c, sl], in1=hb[:, c, sl], op=ALU.mult),
                    [hbB[c][t]], [bf("tbf2")])
                mm(ps[bs[t]][:], onesb[:], tbf2[:], ci == 0, ci == 3, [bf("tbf2"), bf("onesb")], [psB[bs[t]]], sig=True)
        for t in range(4):
            act(lambda h, t=t: h.activation(out=tmpA[:, t * 512:(t + 1) * 512], in_=ps[bs[t]][:], func=AF.Sqrt,
                                            bias=RMS_EPS, scale=1.0 / 512), [psB[bs[t]]], [bf("tmpA")])
        dve(lambda h: h.reciprocal(out=tmpA[:], in_=tmpA[:]), [bf("tmpA")], [bf("tmpA")])
        for ci in range(4):
            c = c0 + ci
            g_ = pcol(gname, l * 4 + ci)
            rd = [hbB[c][t] for t in range(4)]
            dve(lambda h, c=c, g_=g_: h.scalar_tensor_tensor(out=hb[:, c, :], in0=hb[:, c, :], scalar=g_, in1=tmpA[:],
                                                             op0=ALU.mult, op1=ALU.mult), rd + [bf("tmpA"), bf("prm")], rd)

    def block(s, l, first):
        if STAGE < 1:
            return
        for piece in range(2):
            slot = 0
            load_w(wb[:, slot, :].rearrange("p (a n) -> p a n", a=8),
                   w_in_d[l][:, piece * 512:(piece + 1) * 512].rearrange("(a p) n -> p a n", p=128), bf("wb%d" % slot))
            wv = wb[:, slot, :].rearrange("p (a n) -> p a n", a=8)
            dstT = qT if piece == 0 else kT
            for tt in range(16):
                t4 = tt // 4
                b = bank()
                for dk in range(8):
                    mm(ps[b][:], hb[:, dk, tt * 128:(tt + 1) * 128], wv[:, dk, :], dk == 0, dk == 7,
                       [hbB[dk][t4], bf("wb%d" % slot)], [psB[b]])
                pv = ps[b][:].rearrange("p (h e) -> p h e", h=8)
                q1, q2 = pv[:, :, 0:32], pv[:, :, 32:64]
                cb = bass.AP(cosT, tt * 32, [[512, 128], [0, 8], [1, 32]])
                sbb = bass.AP(sinT, tt * 32, [[512, 128], [0, 8], [1, 32]])
                ta = tmpB[:, 0:256].rearrange("p (h e) -> p h e", h=8)
                tb_ = tmpB[:, 256:512].rearrange("p (h e) -> p h e", h=8)
                ov = qk_tm[:].rearrange("p (h e) -> p h e", h=8)
                rdc = [psB[b], bf("rope"), bf("ropes")]
                dve(lambda h, q1=q1, cb=cb, ta=ta: h.tensor_tensor(out=ta, in0=q1, in1=cb, op=ALU.mult), rdc, [bf("tmpB")])
                dve(lambda h, q2=q2, sbb=sbb, tb_=tb_: h.tensor_tensor(out=tb_, in0=q2, in1=sbb, op=ALU.mult), rdc, [bf("tmpB")])
                dve(lambda h, ta=ta, tb_=tb_, ov=ov: h.tensor_tensor(out=ov[:, :, 0:32], in0=ta, in1=tb_, op=ALU.subtract),
                    [bf("tmpB")], [bf("qk_tm")])
                dve(lambda h, q1=q1, sbb=sbb, ta=ta: h.tensor_tensor(out=ta, in0=q1, in1=sbb, op=ALU.mult), rdc, [bf("tmpB")])
                dve(lambda h, q2=q2, cb=cb, tb_=tb_: h.tensor_tensor(out=tb_, in0=q2, in1=cb, op=ALU.mult), rdc, [bf("tmpB")])
                dve(lambda h, ta=ta, tb_=tb_, ov=ov: h.tensor_tensor(out=ov[:, :, 32:64], in0=ta, in1=tb_, op=ALU.add),
                    [bf("tmpB")], [bf("qk_tm")])
                b2 = bank()
                for c in range(4):
                    tr(psb[b2][:, c * 128:(c + 1) * 128], qk_tm[:, c * 128:(c + 1) * 128], identb[:],
                       [bf("qk_tm"), bf("identb")], [psB[b2]])
                act(lambda h, b2=b2, tt=tt, dstT=dstT: h.copy(
                    out=dstT[:, :, tt * 128:(tt + 1) * 128],
                    in_=psb[b2][:, 0:512].rearrange("p (c n) -> p c n", c=4)), [psB[b2]], [R1B])
        load_w(wb[:, 0, :].rearrange("p (a n) -> p a n", a=8),
               w_in_d[l][:, 1024:1536].rearrange("(a p) n -> p a n", p=128), bf("wb0"))
        wv = wb[:, 0, :].rearrange("p (a n) -> p a n", a=8)
        for tt in range(16):
            t4 = tt // 4
            b = bank()
            for dk in range(8):
                mm(ps[b][:], hb[:, dk, tt * 128:(tt + 1) * 128], wv[:, dk, :], dk == 0, dk == 7,
                   [hbB[dk][t4], bf("wb0")], [psB[b]])
            act(lambda h, b=b: h.copy(out=v_tm[:], in_=ps[b][:]), [psB[b]], [bf("v_tm")])
            P.dma("sp", "v", lambda h, tt=tt: h.dma_start(out=vscr[tt * 128:(tt + 1) * 128, :], in_=v_tm[:]),
                  reads=[bf("v_tm")], writes=[bf("vscr")])
        load_w(wb[:, 0, :].rearrange("p (a n) -> p a n", a=8),
               w_in_d[l][:, 1536:2048].rearrange("(a p) n -> p a n", p=128), bf("wb0"))
        wv = wb[:, 0, :].rearrange("p (a n) -> p a n", a=8)
        for c in range(4):
            for t in range(4):
                b = bank()
                for dk in range(8):
                    mm(ps[b][:], wv[:, dk, c * 128:(c + 1) * 128], hb[:, dk, t * 512:(t + 1) * 512], dk == 0, dk == 7,
                       [hbB[dk][t], bf("wb0")], [psB[b]])
                act(lambda h, b=b, c=c, t=t: h.copy(out=uT[:, c, t * 512:(t + 1) * 512], in_=ps[b][:]), [psB[b]], [uTB[c][t]])

        if STAGE < 2:
            return
        def tokset(br, i):
            if br == 0:
                return slice(128 * i, 128 * i + 128)
            if br == 1:
                r, J = i // 4, i % 4
                return slice(512 * J + r, 512 * J + 512, 4)
            return slice(i, S, 16)

        def vslot(br, i):
            if br == 0:
                return i
            if br == 1:
                r, J = i // 4, i % 4
                return 4 * J + r
            return i

        subseqs = [
            (0, [list(range(16))]),
            (1, [[r * 4 + J for J in range(4)] for r in range(4)]),
            (2, [[r] for r in range(16)]),
        ]
        for hh in range(8):
            hs = slice((hh % 2) * 64, (hh % 2) * 64 + 64)
            cch = hh // 2
            vcol = vscr[:, 64 * hh:64 * hh + 64]
            srcs = [vcol.rearrange("(j p) e -> p j e", p=128),
                    vcol.rearrange("(J p r) e -> p J r e", J=4, r=4),
                    vcol.rearrange("(p r) e -> p r e", r=16)]
            ptc = 0
            for (br, seqs) in subseqs:
                if br == 1:
                    pairs = [(vh[:, 0, 4 * J:4 * J + 4, 0:64], srcs[1][:, J]) for J in range(4)]
                else:
                    pairs = [(vh[:, 0, 8 * hf:8 * hf + 8, 0:64], srcs[br][:, 8 * hf:8 * hf + 8]) for hf in range(2)]
                for (dst, src) in pairs:
                    P.dma("sp", "vh", lambda h, dst=dst, src=src: h.dma_start(out=dst, in_=src),
                          reads=[bf("vscr")], writes=[bf("vh0")])
                for seq in seqs:
                    n = len(seq)
                    prev_pt = None
                    for ii, i in enumerate(seq):
                        ks = tokset(br, i)
                        b = bank()
                        has_next = ii + 1 < n
                        mm(ps[b][:, 0:128], kT[hs, cch, ks], qT[hs, cch, ks], True, True, [R1B], [psB[b]], sig=not has_next)
                        if has_next:
                            mm(ps[b][:, 128:256], kT[hs, cch, ks], qT[hs, cch, tokset(br, seq[ii + 1])], True, True,
                               [R1B], [psB[b]], sig=True)
                        w_ = 256 if has_next else 128
                        pt = PT[ptc % 2]
                        ptB = bf("PT%d" % (ptc % 2))
                        ptc += 1
                        act(lambda h, b=b, pt=pt, w_=w_: h.activation(out=pt[:, 0:w_], in_=ps[b][:, 0:w_], func=AF.Exp, scale=0.125),
                            [psB[b]], [ptB])
                        dve(lambda h, pt=pt, w_=w_: h.tensor_tensor(out=pt[:, 0:w_], in0=pt[:, 0:w_], in1=maskb[:, 0:w_], op=ALU.mult),
                            [ptB, bf("maskb")], [ptB])
                        bo = bank()
                        if prev_pt is not None:
                            ppt, pptB, pi_ = prev_pt
                            mm(ps[bo][:, 0:128], vh[:, 0, vslot(br, pi_), :], ppt[:, 128:256], True, False,
                               [bf("vh0"), pptB], [psB[bo]], sig=False)
                        mm(ps[bo][:, 0:128], vh[:, 0, vslot(br, i), :], pt[:, 0:128], prev_pt is None, True,
                           [bf("vh0"), ptB], [psB[bo]], sig=True)
                        if br == 0:
                            dve(lambda h, bo=bo, ks=ks: h.tensor_copy(out=Oacc[:, ks], in_=ps[bo][:, 0:128]), [psB[bo]], [R2B])
                        else:
                            dve(lambda h, bo=bo, ks=ks: h.tensor_tensor(out=Oacc[:, ks], in0=Oacc[:, ks], in1=ps[bo][:, 0:128], op=ALU.add),
                                [psB[bo], R2B], [R2B])
                        prev_pt = (pt, ptB, i)
            dve(lambda h: h.tensor_copy(out=tmpA[0:64, :], in_=Oacc[64:128, :]), [R2B], [bf("tmpA")])
            dve(lambda h: h.reciprocal(out=tmpA[0:64, :], in_=tmpA[0:64, :]), [bf("tmpA")], [bf("tmpA")])
            wr = [hbB[cch][t] for t in range(4)]
            dve(lambda h, hs=hs, cch=cch: h.tensor_tensor(out=hb[hs, cch, :], in0=Oacc[0:64, :], in1=tmpA[0:64, :], op=ALU.mult),
                [R2B, bf("tmpA")], wr)
        rms_half(l, 0, "attn_gain")

        if STAGE < 3:
            return
        ssm_setup(l)
        T_ = bf("ssmtab")
        bank_pool[0] = [4, 5, 6, 7]
        for gp in range(16):
            j = gp // 4
            gl0 = (2 * gp) % 8
            for ri, srcB in enumerate([bbr, bbi]):
                dve(lambda h: h.memset(Zf[:, 0, :], 0.0), [], [bf("Zf")])
                dve(lambda h, srcB=srcB, gp=gp, gl0=gl0: h.tensor_copy(out=Zf[0:64, 0, 16 * gl0:16 * gl0 + 16], in_=srcB[0:64, gp, :]), [T_], [bf("Zf")])
                dve(lambda h, srcB=srcB, gp=gp, gl0=gl0: h.tensor_copy(out=Zf[64:128, 0, 16 * gl0 + 16:16 * gl0 + 32], in_=srcB[64:128, gp, :]), [T_], [bf("Zf")])
                b = bank()
                tr(ps[b][:, 0:128], Zf[:, 0, :], identf[:], [bf("Zf"), bf("identf")], [psB[b]])
                act(lambda h, b=b, ri=ri: h.copy(out=BbT[:, ri, :], in_=ps[b][:, 0:128]), [psB[b]], [bf("BbT")])
            dve(lambda h: h.memset(Zf[:], 0.0), [], [bf("Zf")])
            for ri in range(2):
                dve(lambda h, ri=ri, gl0=gl0, j=j: h.tensor_copy(out=Zf[0:64, ri, 16 * gl0:16 * gl0 + 16], in_=CT[0:64, ri, j, 16 * gl0:16 * gl0 + 16]), [T_], [bf("Zf")])
                dve(lambda h, ri=ri, gl0=gl0, j=j: h.tensor_copy(out=Zf[64:128, ri, 16 * gl0 + 16:16 * gl0 + 32], in_=CT[0:64, ri, j, 16 * gl0 + 16:16 * gl0 + 32]), [T_], [bf("Zf")])
            dve(lambda h: h.tensor_copy(out=Cm[:], in_=Zf[:]), [bf("Zf")], [bf("Cm")])
            for t in range(4):
                sl = slice(t * 512, (t + 1) * 512)
                for ri in range(2):
                    b = bank()
                    mm(ps[b][:], BbT[:, ri, :], uT[:, j, sl], True, True, [bf("BbT"), uTB[j][t]], [psB[b]])
                    if ri == 0:
                        act(lambda h, b=b, sl=sl: h.copy(out=X0[:, 0, sl], in_=ps[b][:]), [psB[b]], [R1B])
                    else:
                        dve(lambda h, b=b, sl=sl: h.tensor_copy(out=X0[:, 1, sl], in_=ps[b][:]), [psB[b]], [R1B])
            src, dst = X0, X1
            for k in range(11):
                d = 1 << k
                ar = LPr[:, k, gp:gp + 1]
                ai = LPi[:, k, gp:gp + 1]
                an = LPn[:, k, gp:gp + 1]
                rdx = [R1B, T_]
                dve(lambda h, src=src, dst=dst, d=d: h.tensor_copy(out=dst[:, :, 0:d], in_=src[:, :, 0:d]), rdx, [R1B])
                dve(lambda h, src=src, dst=dst, d=d, ar=ar: h.scalar_tensor_tensor(
                    out=dst[:, 0, d:S], in0=src[:, 0, 0:S - d], scalar=ar, in1=src[:, 0, d:S], op0=ALU.mult, op1=ALU.add), rdx, [R1B])
                dve(lambda h, src=src, dst=dst, d=d, an=an: h.scalar_tensor_tensor(
                    out=dst[:, 0, d:S], in0=src[:, 1, 0:S - d], scalar=an, in1=dst[:, 0, d:S], op0=ALU.mult, op1=ALU.add), rdx, [R1B])
                dve(lambda h, src=src, dst=dst, d=d, ar=ar: h.scalar_tensor_tensor(
                    out=dst[:, 1, d:S], in0=src[:, 1, 0:S - d], scalar=ar, in1=src[:, 1, d:S], op0=ALU.mult, op1=ALU.add), rdx, [R1B])
                dve(lambda h, src=src, dst=dst, d=d, ai=ai: h.scalar_tensor_tensor(
                    out=dst[:, 1, d:S], in0=src[:, 0, 0:S - d], scalar=ai, in1=dst[:, 1, d:S], op0=ALU.mult, op1=ALU.add), rdx, [R1B])
                src, dst = dst, src
            act(lambda h, src=src: h.copy(out=XB[:], in_=src[:]), [R1B], [R2B])
            for t in range(4):
                sl = slice(t * 512, (t + 1) * 512)
                first_gp = (gp % 4 == 0)
                last_gp = (gp % 4 == 3)
                mm(ps[t][:], Cm[:, 0, :], XB[:, 0, sl], first_gp, False, [bf("Cm"), R2B], [psB[t]], sig=False)
                mm(ps[t][:], Cm[:, 1, :], XB[:, 1, sl], False, False, [bf("Cm"), R2B], [psB[t]], sig=not last_gp)
                if last_gp:
                    mm(ps[t][:], Dd[:, j, :], uT[:, j, sl], False, True, [T_, uTB[j][t]], [psB[t]], sig=True)
                    act(lambda h, t=t: h.activation(out=tmpB[:], in_=ps[t][:], func=AF.Square), [psB[t]], [bf("tmpB")])
                    dve(lambda h: h.tensor_scalar(out=tmpB[:], in0=tmpB[:], scalar1=0.044715, scalar2=1.0, op0=ALU.mult, op1=ALU.add),
                        [bf("tmpB")], [bf("tmpB")])
                    dve(lambda h, t=t: h.tensor_tensor(out=tmpB[:], in0=tmpB[:], in1=ps[t][:], op=ALU.mult), [bf("tmpB"), psB[t]], [bf("tmpB")])
                    act(lambda h: h.activation(out=tmpB[:], in_=tmpB[:], func=AF.Sigmoid, scale=1.5957691216057308), [bf("tmpB")], [bf("tmpB")])
                    dve(lambda h, t=t, sl=sl, j=j: h.tensor_tensor(out=uT[:, j, sl], in0=tmpB[:], in1=ps[t][:], op=ALU.mult),
                        [bf("tmpB"), psB[t]], [uTB[j][t]])
        bank_pool[0] = list(range(8))
        load_w(wb[:, 0, 0:2048].rearrange("p (a n) -> p a n", a=4),
               w_glu_d[l].rearrange("(a p) n -> p a n", p=128), bf("wb0"))
        wg = wb[:, 0, 0:2048].rearrange("p (a n) -> p a n", a=4)
        for co in range(4):
            bg = pcol("b_glu", l * 4 + co)
            for t in range(4):
                sl = slice(t * 512, (t + 1) * 512)
                b = bank()
                for ci in range(4):
                    mm(ps[b][:], wg[:, ci, co * 128:(co + 1) * 128], uT[:, ci, sl], ci == 0, ci == 3, [bf("wb0"), uTB[ci][t]], [psB[b]])
                act(lambda h, b=b, bg=bg: h.activation(out=tmpB[:], in_=ps[b][:], func=AF.Sigmoid, bias=bg, scale=1.0),
                    [psB[b], bf("prm")], [bf("tmpB")])
                dve(lambda h, co=co, sl=sl: h.tensor_tensor(out=hb[:, 4 + co, sl], in0=uT[:, co, sl], in1=tmpB[:], op=ALU.mult),
                    [bf("tmpB"), uTB[co][t]], [hbB[4 + co][t]])
        rms_half(l, 4, "ssm_gain")

        if dbg == "mixed" and s == 0 and l == 0:
            for c in range(8):
                dve(lambda h, c=c: h.tensor_copy(out=tmpA[:], in_=hb[:, c, :]), [hbB[c][t] for t in range(4)], [bf("tmpA")])
                P.dma("sp", "st", lambda h, c=c: h.dma_start(out=dbg_d[:, c, :], in_=tmpA[:]), reads=[bf("tmpA")])

        if STAGE < 4:
            return
        for co in range(8):
            if co % 4 == 0:
                half = co // 4
                load_w(wb[:, 0, :].rearrange("p (a n) -> p a n", a=8),
                       w_out_d[l][:, half * 512:(half + 1) * 512].rearrange("(a p) n -> p a n", p=128), bf("wb0"))
            wv = wb[:, 0, :].rearrange("p (a n) -> p a n", a=8)
            cl = co % 4
            bo_ = pcol("b_out", l * 8 + co)
            for t in range(4):
                sl = slice(t * 512, (t + 1) * 512)
                b = bank()
                for dk in range(8):
                    mm(ps[b][:], wv[:, dk, cl * 128:(cl + 1) * 128], hb[:, dk, sl], dk == 0, dk == 7,
                       [bf("wb0"), hbB[dk][t]], [psB[b]])
                act(lambda h, b=b, bo_=bo_: h.activation(out=tmpB[:], in_=ps[b][:], func=AF.Identity, bias=bo_, scale=1.0),
                    [psB[b], bf("prm")], [bf("tmpB")])
                dve(lambda h, co=co, sl=sl: h.scalar_tensor_tensor(out=h32[:, co, sl], in0=h32[:, co, sl], scalar=ALPHA, in1=tmpB[:],
                                                                 op0=ALU.mult, op1=ALU.add), [hB[co][t], bf("tmpB")], [hB[co][t]])
        P.barrier()
        layer_norm(l, "ln1_g", "ln1_b")
        if dbg == "ln1" and s == 0 and l == 0:
            for c in range(8):
                P.dma("sp", "st", lambda h, c=c: h.dma_start(out=dbg_d[:, c, :], in_=h32[:, c, :]), reads=[hB[c][t] for t in range(4)])

        if STAGE < 5:
            return
        P.barrier()
        for qp in range(4):
            load_w(W1q, w_ff1_d[l][:, qp * 1024:(qp + 1) * 1024].rearrange("(a p) n -> p a n", p=128), bf("W1q"))
            load_w(W2q, w_ff2_d[l][qp * 1024:(qp + 1) * 1024, :].rearrange("(a p) n -> p a n", p=128), bf("W2q"))
            for t in range(4):
                sl = slice(t * 512, (t + 1) * 512)
                for fc in range(8):
                    b = bank()
                    b1_ = pcol("b_ff1", l * 32 + qp * 8 + fc)
                    for dk in range(8):
                        mm(ps[b][:], W1q[:, dk, fc * 128:(fc + 1) * 128], hb[:, dk, sl], dk == 0, dk == 7,
                           [bf("W1q"), hbB[dk][t]], [psB[b]])
                    act(lambda h, b=b, b1_=b1_: h.activation(out=tbf2[:], in_=ps[b][:], func=AF.Relu, bias=b1_, scale=1.0),
                        [psB[b], bf("prm")], [bf("tbf2")])
                    dve(lambda h, fc=fc: h.tensor_tensor(out=aT[:, fc, :], in0=tbf2[:], in1=tbf2[:], op=ALU.mult),
                        [bf("tbf2")], [bf("aT%d" % fc)])
                for co in range(8):
                    b = bank()
                    for fc in range(8):
                        mm(ps[b][:], W2q[:, fc, co * 128:(co + 1) * 128], aT[:, fc, :], fc == 0, fc == 7,
                           [bf("W2q"), bf("aT%d" % fc)], [psB[b]])
                    if qp == 0:
                        b2_ = pcol("b_ff2", l * 8 + co)
                        act(lambda h, b=b, b2_=b2_: h.activation(out=tmpB[:], in_=ps[b][:], func=AF.Identity, bias=b2_, scale=1.0),
                            [psB[b], bf("prm")], [bf("tmpB")])
                        dve(lambda h, co=co, sl=sl: h.scalar_tensor_tensor(out=h32[:, co, sl], in0=h32[:, co, sl], scalar=ALPHA, in1=tmpB[:],
                                                                         op0=ALU.mult, op1=ALU.add), [hB[co][t], bf("tmpB")], [hB[co][t]])
                    else:
                        dve(lambda h, b=b, co=co, sl=sl: h.tensor_tensor(out=h32[:, co, sl], in0=h32[:, co, sl], in1=ps[b][:], op=ALU.add),
                            [hB[co][t], psB[b]], [hB[co][t]])
        P.barrier()
        layer_norm(l, "ln2_g", "ln2_b")
        P.barrier()
        P.new_epoch()

    for s in range(NSEQ):
        if "rope" not in KSKIP: P.dma("sp", "misc", lambda h, s=s: h.dma_start(out=posi[:], in_=pos_d[s].rearrange("(j p) -> p j", p=128)),
              writes=[bf("posi")])
        if "rope" not in KSKIP:
            dve(lambda h: h.tensor_copy(out=posf[:], in_=posi[:]), [bf("posi")], [bf("posf")])
        for tt in (range(16) if "rope" not in KSKIP else []):
            pc = posf[:, tt:tt + 1]
            dve(lambda h, tt=tt, pc=pc: h.tensor_scalar(out=cosT[:, tt, :], in0=invf[:], scalar1=pc, scalar2=None, op0=ALU.mult),
                [bf("posf"), bf("invf")], [bf("rope")])
        rr = [bf("rope")]
        if "rope" not in KSKIP:
            sin_of(sinT[:], cosT[:], 0.0, 512, rr, [bf("ropes")])
            sin_of(cosT[:], cosT[:], 0.5 * math.pi, 512, rr + [bf("ropes")], rr)
        for tt in range(int(os.environ.get("KXL", "16"))):
            t4 = tt // 4
            tq = 0 if os.environ.get("KXOFF") else tt
            P.dma("sp", "ld", lambda h, s=s, tt=tq: h.dma_start(out=xtm, in_=x_d[s, tt * 128:(tt + 1) * 128, :]),
                  writes=[bf("tmpA")])
            for g in range(2):
                b = bank()
                for c4 in range(4):
                    c = g * 4 + c4
                    tr(ps[b][:, c4 * 128:(c4 + 1) * 128], xtm[:, c * 128:(c + 1) * 128], identf[:], [bf("tmpA"), bf("identf")], [psB[b]])
                wr = [hB[g * 4 + c4][t4] for c4 in range(4)]
                wr2 = [hbB[g * 4 + c4][t4] for c4 in range(4)]
                pv = ps[b][:].rearrange("p (c n) -> p c n", c=4)
                if "h32copy" not in KSKIP:
                    dve(lambda h, pv=pv, g=g, tt=tt: h.tensor_copy(out=h32[:, g * 4:g * 4 + 4, tt * 128:(tt + 1) * 128], in_=pv), [psB[b]], wr)
                if "hbcopy" not in KSKIP:
                    act(lambda h, pv=pv, g=g, tt=tt: h.copy(out=hb[:, g * 4:g * 4 + 4, tt * 128:(tt + 1) * 128], in_=pv), [psB[b]], wr2)
        for l in range(NL):
            block(s, l, False)
        for tt in range(int(os.environ.get("KXS", "16"))):
            t4 = tt // 4
            for g in range(2):
                b = bank()
                for c4 in range(4):
                    c = g * 4 + c4
                    tr(ps[b][:, c4 * 128:(c4 + 1) * 128], h32[:, c, tt * 128:(tt + 1) * 128], identf[:], [hB[c][t4], bf("identf")], [psB[b]])
                dve(lambda h, b=b, g=g: h.tensor_copy(out=xtm[:, g * 512:(g + 1) * 512], in_=ps[b][:]), [psB[b]], [bf("tmpA")])
            P.dma("sp", "st", lambda h, s=s, tt=tt: h.dma_start(out=out_d[s, tt * 128:(tt + 1) * 128, :], in_=xtm),
                  reads=[bf("tmpA")])
        P.barrier()
    P.barrier()
    with nc.allow_non_contiguous_dma(reason="small strided param loads"), nc.Block() as blk:
        P.emit_all(blk)
    es.close()
    return nc, P


def consts():
    ident = np.eye(128, dtype=np.float32)
    k = np.arange(128)[:, None]
    q = np.arange(128)[None, :]
    mask = np.concatenate([(k <= q), (k >= q)], axis=1).astype(np.float32)
    invf = (10000.0 ** (-np.arange(32, dtype=np.float32) * 2.0 / 64.0)).astype(np.float32)
    invf = np.broadcast_to(invf[None, :], (128, 32)).copy()
    return {"c_ident": ident, "c_mask": mask, "c_invf": invf}


WNAMES = ["w_in", "attn_gain", "ssm_gain", "ssm_a_re", "ssm_a_im", "ssm_log_dt", "ssm_b_re", "ssm_b_im",
          "ssm_c_re", "ssm_c_im", "ssm_d", "w_glu", "b_glu", "w_out", "b_out", "ln1_g", "ln1_b",
          "w_ff1", "b_ff1", "w_ff2", "b_ff2", "ln2_g", "ln2_b"]

_CACHE = {}

LAUNCH_NL = 4
LAUNCH_NSEQ = 4


def kernel(**inputs):
    x = np.ascontiguousarray(inputs["x"], dtype=np.float32)
    pos = np.ascontiguousarray(inputs["positions"], dtype=np.int32)
    per_core = x.shape[0] // NCORES
    key = (LAUNCH_NL, LAUNCH_NSEQ)
    if key not in _CACHE:
        _CACHE[key] = build(LAUNCH_NL, LAUNCH_NSEQ)[0]
    nc = _CACHE[key]
    cs = consts()
    h = x
    for l0 in range(0, DEPTH, LAUNCH_NL):
        base = {n: np.ascontiguousarray(inputs[n][l0:l0 + LAUNCH_NL], dtype=np.float32) for n in WNAMES}
        base.update(cs)
        newh = np.empty_like(h)
        for s0 in range(0, per_core, LAUNCH_NSEQ):
            in_maps = []
            for c in range(NCORES):
                m = dict(base)
                lo = c * per_core + s0
                m["x"] = np.ascontiguousarray(h[lo:lo + LAUNCH_NSEQ])
                m["positions"] = np.ascontiguousarray(pos[lo:lo + LAUNCH_NSEQ])
                in_maps.append(m)
            res = run_bass_kernel_spmd(nc, in_maps, core_ids=list(range(NCORES)))
            for c in range(NCORES):
                lo = c * per_core + s0
                newh[lo:lo + LAUNCH_NSEQ] = res.results[c]["out"]
        h = newh
    return h
```

```python
import math
import os
STAGE = int(os.environ.get("KSTAGE", "9"))
KSKIP = set(os.environ.get("KSKIP", "").split(","))
from contextlib import ExitStack
import numpy as np
import concourse.bass as bass
import concourse.mybir as mybir
from concourse.bass_utils import run_bass_kernel_spmd

F32 = mybir.dt.float32
BF16 = mybir.dt.bfloat16
I32 = mybir.dt.int32
ALU = mybir.AluOpType
AF = mybir.ActivationFunctionType
ENGS = ("pe", "act", "dve", "pool", "sp")

D = 1024
S = 2048
DEPTH = 4
NCORES = 8
ALPHA = (2.0 * DEPTH) ** 0.25
LN_EPS = 1e-5
RMS_EPS = 1e-6
TWO_PI = 2.0 * math.pi


class Buf:
    __slots__ = ("name", "w", "r", "excl")

    def __init__(self, name, excl=False):
        self.name = name
        self.w = None
        self.r = {}
        self.excl = excl


class Prog:
    def __init__(self, nc, es):
        self.nc = nc
        self.es = es
        self.q = {e: [] for e in ENGS}
        self.sems = {}
        self.cnt = {}
        self.cur = {}
        self.dead = set()
        self.epoch = -1
        self.waited = {e: {} for e in ENGS}
        self.dma_cnt = {}
        self.dma_keys = []
        self.dma_rr = 0
        self.n_inst = 0
        self.new_epoch()

    def new_epoch(self):
        for k in self.cur.values():
            self.dead.add(k)
        self.epoch += 1
        for e in ENGS:
            k = "%s#%d" % (e, self.epoch)
            self.cur[e] = k
            self.cnt[k] = 0
            self.sems[k] = self.es.enter_context(self.nc.semaphore("s_%s_%d" % (e, self.epoch)))

    def _need(self, e, deps):
        out = []
        ce = self.cur[e]
        for (k, v) in deps:
            if k in self.dead:
                continue
            if k == ce and (e == "pe" or v > self.cnt[ce]):
                continue
            if self.waited[e].get(k, 0) >= v:
                continue
            self.waited[e][k] = v
            out.append((k, v))
        return out

    def _deps(self, reads, writes):
        deps = set()
        for b in reads:
            if b.w is not None:
                deps.add(b.w)
            if b.excl:
                for kv in b.r.items():
                    deps.add(kv)
        for b in writes:
            if b.w is not None:
                deps.add(b.w)
            for kv in b.r.items():
                deps.add(kv)
        return deps

    def op(self, e, fn, reads=(), writes=(), sig=True):
        waits = self._need(e, self._deps(reads, writes))
        ce = self.cur[e]
        if sig:
            self.cnt[ce] += 1
        tick = self.cnt[ce] if sig else self.cnt[ce] + 1
        sem = self.sems[ce]
        sems = self.sems

        def emit(h):
            for (k, v) in waits:
                h.wait_ge(sems[k], v)
            ins = fn(h)
            if sig:
                ins.then_inc(sem, 1)
        self.q[e].append(emit)
        self.n_inst += 1
        for b in reads:
            if b.excl:
                b.w = (ce, tick)
                b.r = {}
            elif b.r.get(ce, 0) < tick:
                b.r[ce] = tick
        for b in writes:
            b.w = (ce, tick)
            b.r = {}
        return tick

    def dma(self, qe, semkey, fn, reads=(), writes=()):
        semkey = self.dma_keys[self.dma_rr % len(self.dma_keys)]
        self.dma_rr += 1
        deps = self._deps(reads, writes)
        prev = self.dma_cnt.get(semkey, 0)
        if prev > 0:
            deps.add((semkey, prev))
        waits = self._need(qe, deps)
        self.dma_cnt[semkey] = prev + 16
        val = self.dma_cnt[semkey]
        sems = self.sems
        sem = sems[semkey]

        def emit(h):
            for (k, v) in waits:
                h.wait_ge(sems[k], v)
            fn(h).then_inc(sem, 16)
        self.q[qe].append(emit)
        self.n_inst += 1
        for b in reads:
            if b.r.get(semkey, 0) < val:
                b.r[semkey] = val
        for b in writes:
            b.w = (semkey, val)
            b.r = {}
        return val

    def wait_all(self, e):
        deps = [(self.cur[k], self.cnt[self.cur[k]]) for k in ENGS if self.cnt[self.cur[k]] > 0]
        deps += list(self.dma_cnt.items())
        waits = self._need(e, deps)
        sems = self.sems

        def emit(h):
            for (k, v) in waits:
                h.wait_ge(sems[k], v)
        self.q[e].append(emit)

    def barrier(self):
        for e in ENGS:
            self.wait_all(e)

    def emit_all(self, block):
        prog = self

        @block.tensor
        def _(h):
            for f in prog.q["pe"]:
                f(h)

        @block.scalar
        def _(h):
            for f in prog.q["act"]:
                f(h)

        @block.vector
        def _(h):
            for f in prog.q["dve"]:
                f(h)

        @block.gpsimd
        def _(h):
            for f in prog.q["pool"]:
                f(h)

        @block.sync
        def _(h):
            for f in prog.q["sp"]:
                f(h)


def build(NL, NSEQ, dbg=None):
    nc = bass.Bass("TRN2", target_bir_lowering=False)
    es = ExitStack()

    MINIO = os.environ.get("KMINIO", "")

    def din(name, shape, dt=F32):
        if MINIO and name not in ("x", "positions", "c_ident", "c_mask", "c_invf"):
            return None
        return nc.dram_tensor(name, list(shape), dt, kind="ExternalInput").ap()

    x_d = din("x", [NSEQ, S, D])
    pos_d = din("positions", [NSEQ, S], I32)
    w_in_d = din("w_in", [NL, D, 2048])
    attn_gain_d = din("attn_gain", [NL, 512])
    ssm_gain_d = din("ssm_gain", [NL, 512])
    a_re_d = din("ssm_a_re", [NL, 32, 64])
    a_im_d = din("ssm_a_im", [NL, 32, 64])
    log_dt_d = din("ssm_log_dt", [NL, 32])
    b_re_d = din("ssm_b_re", [NL, 32, 64, 16])
    b_im_d = din("ssm_b_im", [NL, 32, 64, 16])
    c_re_d = din("ssm_c_re", [NL, 32, 16, 64])
    c_im_d = din("ssm_c_im", [NL, 32, 16, 64])
    d_d = din("ssm_d", [NL, 32, 16])
    w_glu_d = din("w_glu", [NL, 512, 512])
    b_glu_d = din("b_glu", [NL, 512])
    w_out_d = din("w_out", [NL, D, D])
    b_out_d = din("b_out", [NL, D])
    ln1_g_d = din("ln1_g", [NL, D])
    ln1_b_d = din("ln1_b", [NL, D])
    w_ff1_d = din("w_ff1", [NL, D, 4096])
    b_ff1_d = din("b_ff1", [NL, 4096])
    w_ff2_d = din("w_ff2", [NL, 4096, D])
    b_ff2_d = din("b_ff2", [NL, D])
    ln2_g_d = din("ln2_g", [NL, D])
    ln2_b_d = din("ln2_b", [NL, D])
    ident_d = din("c_ident", [128, 128])
    mask_d = din("c_mask", [128, 256])
    invf_d = din("c_invf", [128, 32])
    out_d = nc.dram_tensor("out", [NSEQ, S, D], F32, kind="ExternalOutput").ap()
    vscr = None if MINIO else nc.dram_tensor("vscr", [S, 512], BF16, kind="ExternalOutput").ap()
    dbg_d = None
    if dbg:
        dbg_d = nc.dram_tensor("dbg", [128, 8, S], F32, kind="ExternalOutput").ap()

    def sb(name, shape, dt):
        return es.enter_context(nc.sbuf_tensor(name, list(shape), dt))

    P = Prog(nc, es)
    for i in range(int(os.environ.get("KNDMA", "8"))):
        k = "dma%d" % i
        P.sems[k] = es.enter_context(nc.semaphore("d_" + k))
        P.dma_keys.append(k)

    h32 = sb("h32", [128, 8, S], F32)
    hb = sb("hb", [128, 8, S], BF16)
    R1 = sb("R1", [128, 8192], F32)
    R1b = R1.bitcast(BF16)
    qT = R1b[:, 0:8192].rearrange("p (c n) -> p c n", c=4)
    kT = R1b[:, 8192:16384].rearrange("p (c n) -> p c n", c=4)
    X0 = R1[:, 0:4096].rearrange("p (c n) -> p c n", c=2)
    X1 = R1[:, 4096:8192].rearrange("p (c n) -> p c n", c=2)
    W1q = R1b[:, 0:8192].rearrange("p (c n) -> p c n", c=8)
    W2q = R1b[:, 8192:16384].rearrange("p (c n) -> p c n", c=8)
    uT = sb("uT", [128, 4, S], BF16)
    R2 = sb("R2", [128, 2048], F32)
    Oacc = R2
    R2b = R2.bitcast(BF16)
    XB = R2b[:, 0:4096].rearrange("p (c n) -> p c n", c=2)
    aT = R2b[:, 0:4096].rearrange("p (c n) -> p c n", c=8)
    vh = sb("vh", [128, 1, 16, 128], BF16)
    PT = [sb("PT%d" % i, [128, 256], BF16) for i in range(3)]
    tmpA = sb("tmpA", [128, 2048], F32)
    tmpB = sb("tmpB", [128, 512], F32)
    tmpC = sb("tmpC", [128, 512], F32)
    tbf2 = sb("tbf2", [128, 512], BF16)
    NWST = int(os.environ.get("KNWST", "2"))
    wsts = [sb("wst%d" % i, [128, 1024], F32) for i in range(NWST)]
    wst_ctr = [0]
    wb = sb("wb", [128, 1, 4096], BF16)
    xtm = tmpA[:, 0:1024]
    qk_tm = sb("qk_tm", [128, 512], BF16)
    v_tm = qk_tm
    identf = sb("identf", [128, 128], F32)
    identb = sb("identb", [128, 128], BF16)
    onesb = sb("onesb", [128, 128], BF16)
    maskb = sb("maskb", [128, 256], BF16)
    invf = sb("invf", [128, 32], F32)
    posf = sb("posf", [128, 16], F32)
    posi = sb("posi", [128, 16], I32)
    cosT = sb("cosT", [128, 16, 32], F32)
    sinT = sb("sinT", [128, 16, 32], F32)
    prm = sb("prm", [128, 512], F32)
    prm_stage = sb("prm_stage", [128, 128], F32)
    sA = sb("sA", [128, 16, 8], F32)
    LPr = sb("LPr", [128, 11, 16], F32)
    LPi = sb("LPi", [128, 11, 16], F32)
    LPn = sb("LPn", [128, 11, 16], F32)
    Bre = sb("Bre", [128, 16, 16], F32)
    Bim = sb("Bim", [128, 16, 16], F32)
    bbr = sb("bbr", [128, 16, 16], F32)
    bbi = sb("bbi", [128, 16, 16], F32)
    CT = sb("CT", [128, 2, 4, 128], BF16)
    Zf = sb("Zf", [128, 2, 128], F32)
    BbT = sb("BbT", [128, 2, 128], BF16)
    Cm = sb("Cm", [128, 2, 128], BF16)
    Dd = sb("Dd", [128, 4, 128], BF16)
    cstage = sb("cstage", [128, 64], F32)

    ps = [es.enter_context(nc.psum_tensor("ps%d" % i, [128, 512], F32)) for i in range(8)]
    psb = [p.bitcast(BF16) for p in ps]
    psB = [Buf("ps%d" % i, excl=True) for i in range(8)]
    bank_ctr = [0]
    bank_pool = [list(range(8))]

    def bank():
        pool = bank_pool[0]
        i = pool[bank_ctr[0] % len(pool)]
        bank_ctr[0] += 1
        return i

    B = {}

    def bf(name):
        if name not in B:
            B[name] = Buf(name)
        return B[name]

    hB = [[bf("h%d_%d" % (c, t)) for t in range(4)] for c in range(8)]
    hbB = [[bf("hb%d_%d" % (c, t)) for t in range(4)] for c in range(8)]
    R1B = bf("R1")
    uTB = [[bf("uT%d_%d" % (c, t)) for t in range(4)] for c in range(4)]
    R2B = bf("R2")

    def allhb():
        return [b for row in hbB for b in row]

    def allh():
        return [b for row in hB for b in row]

    def alluT():
        return [b for row in uTB for b in row]

    def dve(fn, reads, writes):
        return P.op("dve", fn, reads, writes)

    def act(fn, reads, writes):
        return P.op("act", fn, reads, writes)

    def pool(fn, reads, writes):
        return P.op("pool", fn, reads, writes)

    def mm(out, lhsT, rhs, start, stop, reads, writes, sig=None):
        if sig is None:
            sig = stop
        return P.op("pe", lambda h: h.matmul(out, lhsT=lhsT, rhs=rhs, start=start, stop=stop),
                    reads, writes, sig=sig)

    def tr(out, in_, ident, reads, writes):
        if os.environ.get("KTR", "") == "mm" and in_.dtype == F32:
            return P.op("pe", lambda h: h.matmul(out, lhsT=in_, rhs=ident, start=True, stop=True), reads, writes)
        return P.op("pe", lambda h: h.transpose(out=out, in_=in_, identity=ident), reads, writes)

    def load_w(dst, src, dstB, nparts=128):
        a, n = dst.shape[1], dst.shape[2]
        per = max(1, 1024 // n)
        for a0 in range(0, a, per):
            a1 = min(a, a0 + per)
            k = wst_ctr[0] % NWST
            wst_ctr[0] += 1
            wbuf = bf("wst%d" % k)
            st = wsts[k][:, 0:(a1 - a0) * n].rearrange("p (a n) -> p a n", a=a1 - a0)
            P.dma("sp", "w", lambda h, st=st, a0=a0, a1=a1: h.dma_start(out=st, in_=src[:, a0:a1, :]),
                  writes=[wbuf])
            pool(lambda h, st=st, a0=a0, a1=a1: h.tensor_copy(out=dst[:, a0:a1, :], in_=st),
                 [wbuf], [dstB])

    P.dma("sp", "misc", lambda h: h.dma_start(out=identf[:], in_=ident_d), writes=[bf("identf")])
    P.dma("sp", "misc", lambda h: h.dma_start(out=tmpB[:, 0:256], in_=mask_d), writes=[bf("tmpB")])
    P.dma("sp", "misc", lambda h: h.dma_start(out=invf[:], in_=invf_d), writes=[bf("invf")])
    dve(lambda h: h.tensor_copy(out=identb[:], in_=identf[:]), [bf("identf")], [bf("identb")])
    dve(lambda h: h.tensor_copy(out=maskb[:], in_=tmpB[:, 0:256]), [bf("tmpB")], [bf("maskb")])
    dve(lambda h: h.memset(onesb[:], 1.0), [], [bf("onesb")])
    if "vh" not in KSKIP:
        dve(lambda h: h.memset(vh[:, 0, :, :], 1.0), [], [bf("vh0")])

    prm_cols = {}
    col = [0]

    def load_T(name, src2d, R):
        if "prm" in KSKIP:
            prm_cols[name] = col[0]; col[0] += R
            return
        c0 = col[0]
        col[0] += R
        prm_cols[name] = c0
        P.dma("sp", "misc", lambda h: h.dma_start(out=prm_stage[0:R, :], in_=src2d), writes=[bf("prm_stage")])
        b = bank()
        tr(ps[b][:, 0:R], prm_stage[0:R, :], identf[0:R, 0:R], [bf("prm_stage"), bf("identf")], [psB[b]])
        dve(lambda h: h.tensor_copy(out=prm[:, c0:c0 + R], in_=ps[b][:, 0:R]), [psB[b]], [bf("prm")])
        return c0

    if not MINIO:
        load_T("b_ff1", b_ff1_d.rearrange("l (c p) -> (l c) p", p=128), NL * 32)
        for nm, ap_ in [("b_out", b_out_d), ("ln1_g", ln1_g_d), ("ln1_b", ln1_b_d), ("b_ff2", b_ff2_d),
                        ("ln2_g", ln2_g_d), ("ln2_b", ln2_b_d)]:
            load_T(nm, ap_.rearrange("l (c p) -> (l c) p", p=128), NL * 8)
        for nm, ap_ in [("attn_gain", attn_gain_d), ("ssm_gain", ssm_gain_d), ("b_glu", b_glu_d)]:
            load_T(nm, ap_.rearrange("l (c p) -> (l c) p", p=128), NL * 4)
        load_T("ssm_d", d_d.rearrange("l (c g) n -> (l c) (g n)", g=8), NL * 4)
        for nm, ap_ in [("a_re", a_re_d), ("a_im", a_im_d)]:
            load_T(nm, ap_.rearrange("l (gp two) p -> (l gp) (two p)", two=2), NL * 16)
    assert col[0] <= 512

    def pcol(name, idx):
        c = prm_cols[name] + idx
        return prm[:, c:c + 1]

    def sin_of(dst, src, add, n, rd, wr):
        ti = tmpB.bitcast(I32)[:, 0:n]
        tf = tmpC[:, 0:n]
        tg = tmpB[:, 0:n] if False else None
        if len(dst.shape) == 3:
            a_, b_ = dst.shape[1], dst.shape[2]
            ti = ti.rearrange("p (a b) -> p a b", a=a_)
            tf = tf.rearrange("p (a b) -> p a b", a=a_)
        T = [bf("tmpB"), bf("tmpC")]
        dve(lambda h: h.tensor_scalar(out=dst, in0=src, scalar1=float(add), scalar2=None, op0=ALU.add), rd, wr)
        dve(lambda h: h.tensor_scalar(out=tf, in0=dst, scalar1=1.0 / TWO_PI, scalar2=None, op0=ALU.mult), wr, T)
        dve(lambda h: h.tensor_copy(out=ti, in_=tf), T, T)
        dve(lambda h: h.tensor_copy(out=tf, in_=ti), T, T)
        dve(lambda h: h.scalar_tensor_tensor(out=dst, in0=tf, scalar=-TWO_PI, in1=dst, op0=ALU.mult, op1=ALU.add), T + wr, wr)
        dve(lambda h: h.tensor_scalar(out=tf, in0=dst, scalar1=-math.pi, scalar2=1e30, op0=ALU.add, op1=ALU.mult), wr, T)
        dve(lambda h: h.tensor_scalar(out=tf, in0=tf, scalar1=0.0, scalar2=1.0, op0=ALU.max, op1=ALU.min), T, T)
        dve(lambda h: h.scalar_tensor_tensor(out=dst, in0=tf, scalar=-TWO_PI, in1=dst, op0=ALU.mult, op1=ALU.add), T + wr, wr)
        act(lambda h: h.activation(out=dst, in_=dst, func=AF.Sin), wr, wr)

    def ssm_setup(l):
        T_ = bf("ssmtab")
        for gpar in range(2):
            src = bass.AP(log_dt_d.tensor, l * 32 + gpar, [[0, 64], [2, 16], [1, 1]])
            P.dma("sp", "misc", lambda h, src=src, gpar=gpar: h.dma_start(
                out=sA[gpar * 64:(gpar + 1) * 64, :, 0:1], in_=src), writes=[T_])
        are = prm[:, prm_cols["a_re"] + l * 16: prm_cols["a_re"] + l * 16 + 16]
        aim = prm[:, prm_cols["a_im"] + l * 16: prm_cols["a_im"] + l * 16 + 16]
        dt_ = sA[:, :, 0]
        mag = sA[:, :, 1]
        ang = sA[:, :, 2]
        t3 = sA[:, :, 3]
        t4 = sA[:, :, 4]
        t5 = sA[:, :, 5]
        t6 = sA[:, :, 6]
        t7 = sA[:, :, 7]
        rd = [T_, bf("prm")]
        act(lambda h: h.activation(out=dt_, in_=dt_, func=AF.Exp), rd, [T_])
        dve(lambda h: h.tensor_tensor(out=mag, in0=are, in1=dt_, op=ALU.mult), rd, [T_])
        act(lambda h: h.activation(out=mag, in_=mag, func=AF.Exp), rd, [T_])
        dve(lambda h: h.tensor_tensor(out=ang, in0=aim, in1=dt_, op=ALU.mult), rd, [T_])
        sin_of(t3, ang, 0.0, 16, rd, [T_])
        sin_of(t4, ang, 0.5 * math.pi, 16, rd, [T_])
        lr = LPr[:, 0, :]
        li = LPi[:, 0, :]
        dve(lambda h: h.tensor_tensor(out=lr, in0=mag, in1=t4, op=ALU.mult), rd, [T_])
        dve(lambda h: h.tensor_tensor(out=li, in0=mag, in1=t3, op=ALU.mult), rd, [T_])
        for k in range(1, 11):
            pr, pi_, nr_, ni_ = LPr[:, k - 1, :], LPi[:, k - 1, :], LPr[:, k, :], LPi[:, k, :]
            dve(lambda h, pr=pr, pi_=pi_: h.tensor_tensor(out=t5, in0=pr, in1=pr, op=ALU.mult), rd, [T_])
            dve(lambda h, pr=pr, pi_=pi_: h.tensor_tensor(out=t6, in0=pi_, in1=pi_, op=ALU.mult), rd, [T_])
            dve(lambda h, nr_=nr_: h.tensor_tensor(out=nr_, in0=t5, in1=t6, op=ALU.subtract), rd, [T_])
            dve(lambda h, pr=pr, pi_=pi_: h.tensor_tensor(out=t5, in0=pr, in1=pi_, op=ALU.mult), rd, [T_])
            dve(lambda h, ni_=ni_: h.tensor_scalar(out=ni_, in0=t5, scalar1=2.0, scalar2=None, op0=ALU.mult), rd, [T_])
        dve(lambda h: h.tensor_scalar(out=LPn[:], in0=LPi[:], scalar1=-1.0, scalar2=None, op0=ALU.mult), rd, [T_])
        dve(lambda h: h.tensor_scalar(out=t3, in0=lr, scalar1=-1.0, scalar2=None, op0=ALU.add), rd, [T_])
        dve(lambda h: h.tensor_tensor(out=t4, in0=are, in1=are, op=ALU.mult), rd, [T_])
        dve(lambda h: h.tensor_tensor(out=t5, in0=aim, in1=aim, op=ALU.mult), rd, [T_])
        dve(lambda h: h.tensor_tensor(out=t4, in0=t4, in1=t5, op=ALU.add), rd, [T_])
        dve(lambda h: h.reciprocal(out=t4, in_=t4), rd, [T_])
        dve(lambda h: h.tensor_tensor(out=t5, in0=t3, in1=are, op=ALU.mult), rd, [T_])
        dve(lambda h: h.tensor_tensor(out=t6, in0=li, in1=aim, op=ALU.mult), rd, [T_])
        dve(lambda h: h.tensor_tensor(out=t5, in0=t5, in1=t6, op=ALU.add), rd, [T_])
        dve(lambda h: h.tensor_tensor(out=t5, in0=t5, in1=t4, op=ALU.mult), rd, [T_])
        dve(lambda h: h.tensor_tensor(out=t6, in0=li, in1=are, op=ALU.mult), rd, [T_])
        dve(lambda h: h.tensor_tensor(out=t7, in0=t3, in1=aim, op=ALU.mult), rd, [T_])
        dve(lambda h: h.tensor_tensor(out=t6, in0=t6, in1=t7, op=ALU.subtract), rd, [T_])
        dve(lambda h: h.tensor_tensor(out=t6, in0=t6, in1=t4, op=ALU.mult), rd, [T_])
        for (dst, src_d) in [(Bre, b_re_d), (Bim, b_im_d)]:
            src = src_d[l].rearrange("(gp two) p n -> (two p) gp n", two=2)
            P.dma("sp", "misc", lambda h, dst=dst, src=src: h.dma_start(out=dst[:], in_=src), writes=[T_])
        crb = bass.AP(sA, 5, [[128, 128], [8, 16], [0, 16]])
        cib = bass.AP(sA, 6, [[128, 128], [8, 16], [0, 16]])
        dve(lambda h: h.tensor_tensor(out=bbr[:], in0=Bre[:], in1=crb, op=ALU.mult), rd, [T_])
        dve(lambda h: h.tensor_tensor(out=bbi[:], in0=Bim[:], in1=cib, op=ALU.mult), rd, [T_])
        dve(lambda h: h.tensor_tensor(out=bbr[:], in0=bbr[:], in1=bbi[:], op=ALU.subtract), rd, [T_])
        dve(lambda h: h.tensor_tensor(out=bbi[:], in0=Bim[:], in1=crb, op=ALU.mult), rd, [T_])
        dve(lambda h: h.tensor_tensor(out=Bre[:], in0=Bre[:], in1=cib, op=ALU.mult), rd, [T_])
        dve(lambda h: h.tensor_tensor(out=bbi[:], in0=bbi[:], in1=Bre[:], op=ALU.add), rd, [T_])
        for ri, src_d in enumerate([c_re_d, c_im_d]):
            for j in range(4):
                src = src_d[l, 8 * j:8 * j + 8].rearrange("g n p -> (g n) p")
                P.dma("sp", "misc", lambda h, src=src: h.dma_start(out=cstage[:], in_=src), writes=[bf("cstage")])
                b = bank()
                tr(ps[b][0:64, 0:128], cstage[:, :], identf[:], [bf("cstage"), bf("identf")], [psB[b]])
                if ri == 0:
                    dve(lambda h, b=b, j=j: h.tensor_copy(out=CT[0:64, 0, j, :], in_=ps[b][0:64, 0:128]), [psB[b]], [T_])
                else:
                    dve(lambda h, b=b, j=j: h.tensor_scalar(out=CT[0:64, 1, j, :], in0=ps[b][0:64, 0:128],
                                                           scalar1=-1.0, scalar2=None, op0=ALU.mult), [psB[b]], [T_])
        for j in range(4):
            dc = pcol("ssm_d", l * 4 + j)
            dve(lambda h, j=j, dc=dc: h.tensor_scalar(out=Dd[:, j, :], in0=identf[:], scalar1=dc, scalar2=None,
                                                    op0=ALU.mult), [bf("identf"), bf("prm")], [T_])

    def layer_norm(l, gname, bname):
        for t in range(4):
            sl = slice(t * 512, (t + 1) * 512)
            b1 = bank()
            b2 = bank()
            for c in range(8):
                dve(lambda h, c=c, sl=sl: h.tensor_copy(out=tbf2[:], in_=h32[:, c, sl]), [hB[c][t]], [bf("tbf2")])
                mm(ps[b1][:], onesb[:], tbf2[:], c == 0, c == 7, [bf("tbf2"), bf("onesb")], [psB[b1]], sig=True)
                dve(lambda h, c=c, sl=sl: h.tensor_tensor(out=tbf2[:], in0=h32[:, c, sl], in1=h32[:, c, sl], op=ALU.mult),
                    [hB[c][t]], [bf("tbf2")])
                mm(ps[b2][:], onesb[:], tbf2[:], c == 0, c == 7, [bf("tbf2"), bf("onesb")], [psB[b2]], sig=True)
            act(lambda h, b1=b1: h.activation(out=tmpB[:], in_=ps[b1][:], func=AF.Identity, scale=1.0 / D), [psB[b1]], [bf("tmpB")])
            dve(lambda h: h.tensor_tensor(out=tmpC[:], in0=tmpB[:], in1=tmpB[:], op=ALU.mult), [bf("tmpB")], [bf("tmpC")])
            dve(lambda h, b2=b2: h.scalar_tensor_tensor(out=tmpC[:], in0=ps[b2][:], scalar=1.0 / D, in1=tmpC[:],
                                                 op0=ALU.mult, op1=ALU.subtract), [psB[b2], bf("tmpC")], [bf("tmpC")])
            act(lambda h: h.activation(out=tmpC[:], in_=tmpC[:], func=AF.Sqrt, bias=LN_EPS, scale=1.0), [bf("tmpC")], [bf("tmpC")])
            dve(lambda h: h.reciprocal(out=tmpC[:], in_=tmpC[:]), [bf("tmpC")], [bf("tmpC")])
            for c in range(8):
                g_ = pcol(gname, l * 8 + c)
                b_ = pcol(bname, l * 8 + c)
                dve(lambda h, c=c, sl=sl: h.tensor_tensor(out=h32[:, c, sl], in0=h32[:, c, sl], in1=tmpB[:], op=ALU.subtract),
                    [hB[c][t], bf("tmpB")], [hB[c][t]])
                dve(lambda h, c=c, sl=sl: h.tensor_tensor(out=h32[:, c, sl], in0=h32[:, c, sl], in1=tmpC[:], op=ALU.mult),
                    [hB[c][t], bf("tmpC")], [hB[c][t]])
                dve(lambda h, c=c, g_=g_, b_=b_, sl=sl: h.tensor_scalar(out=h32[:, c, sl], in0=h32[:, c, sl], scalar1=g_, scalar2=b_,
                                                                  op0=ALU.mult, op1=ALU.add), [hB[c][t], bf("prm")], [hB[c][t]])
                act(lambda h, c=c, sl=sl: h.copy(out=hb[:, c, sl], in_=h32[:, c, sl]), [hB[c][t]], [hbB[c][t]])

    def rms_half(l, c0, gname):
        bs = [bank() for _ in range(4)]
        for ci in range(4):
            c = c0 + ci
            for t in range(4):
                sl = slice(t * 512, (t + 1) * 512)
                dve(lambda h, c=c, sl=sl: h.tensor_tensor(out=tbf2[:], in0=hb[:, c, sl], in1=hb[:, c, sl], op=ALU.mult),
                    [hbB[c][t]], [bf("tbf2")])
                mm(ps[bs[t]][:], onesb[:], tbf2[:], ci == 0, ci == 3, [bf("tbf2"), bf("onesb")], [psB[bs[t]]], sig=True)
        for t in range(4):
            act(lambda h, t=t: h.activation(out=tmpA[:, t * 512:(t + 1) * 512], in_=ps[bs[t]][:], func=AF.Sqrt,
                                            bias=RMS_EPS, scale=1.0 / 512), [psB[bs[t]]], [bf("tmpA")])
        dve(lambda h: h.reciprocal(out=tmpA[:], in_=tmpA[:]), [bf("tmpA")], [bf("tmpA")])
        for ci in range(4):
            c = c0 + ci
            g_ = pcol(gname, l * 4 + ci)
            rd = [hbB[c][t] for t in range(4)]
            dve(lambda h, c=c, g_=g_: h.scalar_tensor_tensor(out=hb[:, c, :], in0=hb[:, c, :], scalar=g_, in1=tmpA[:],
                                                             op0=ALU.mult, op1=ALU.mult), rd + [bf("tmpA"), bf("prm")], rd)

    def block(s, l, first):
        if STAGE < 1:
            return
        for piece in range(2):
            slot = 0
            load_w(wb[:, slot, :].rearrange("p (a n) -> p a n", a=8),
                   w_in_d[l][:, piece * 512:(piece + 1) * 512].rearrange("(a p) n -> p a n", p=128), bf("wb%d" % slot))
            wv = wb[:, slot, :].rearrange("p (a n) -> p a n", a=8)
            dstT = qT if piece == 0 else kT
            for tt in range(16):
                t4 = tt // 4
                b = bank()
                for dk in range(8):
                    mm(ps[b][:], hb[:, dk, tt * 128:(tt + 1) * 128], wv[:, dk, :], dk == 0, dk == 7,
                       [hbB[dk][t4], bf("wb%d" % slot)], [psB[b]])
                pv = ps[b][:].rearrange("p (h e) -> p h e", h=8)
                q1, q2 = pv[:, :, 0:32], pv[:, :, 32:64]
                cb = bass.AP(cosT, tt * 32, [[512, 128], [0, 8], [1, 32]])
                sbb = bass.AP(sinT, tt * 32, [[512, 128], [0, 8], [1, 32]])
                ta = tmpB[:, 0:256].rearrange("p (h e) -> p h e", h=8)
                tb_ = tmpB[:, 256:512].rearrange("p (h e) -> p h e", h=8)
                ov = qk_tm[:].rearrange("p (h e) -> p h e", h=8)
                rdc = [psB[b], bf("rope"), bf("ropes")]
                dve(lambda h, q1=q1, cb=cb, ta=ta: h.tensor_tensor(out=ta, in0=q1, in1=cb, op=ALU.mult), rdc, [bf("tmpB")])
                dve(lambda h, q2=q2, sbb=sbb, tb_=tb_: h.tensor_tensor(out=tb_, in0=q2, in1=sbb, op=ALU.mult), rdc, [bf("tmpB")])
                dve(lambda h, ta=ta, tb_=tb_, ov=ov: h.tensor_tensor(out=ov[:, :, 0:32], in0=ta, in1=tb_, op=ALU.subtract),
                    [bf("tmpB")], [bf("qk_tm")])
                dve(lambda h, q1=q1, sbb=sbb, ta=ta: h.tensor_tensor(out=ta, in0=q1, in1=sbb, op=ALU.mult), rdc, [bf("tmpB")])
                dve(lambda h, q2=q2, cb=cb, tb_=tb_: h.tensor_tensor(out=tb_, in0=q2, in1=cb, op=ALU.mult), rdc, [bf("tmpB")])
                dve(lambda h, ta=ta, tb_=tb_, ov=ov: h.tensor_tensor(out=ov[:, :, 32:64], in0=ta, in1=tb_, op=ALU.add),
                    [bf("tmpB")], [bf("qk_tm")])
                b2 = bank()
                for c in range(4):
                    tr(psb[b2][:, c * 128:(c + 1) * 128], qk_tm[:, c * 128:(c + 1) * 128], identb[:],
                       [bf("qk_tm"), bf("identb")], [psB[b2]])
                act(lambda h, b2=b2, tt=tt, dstT=dstT: h.copy(
                    out=dstT[:, :, tt * 128:(tt + 1) * 128],
                    in_=psb[b2][:, 0:512].rearrange("p (c n) -> p c n", c=4)), [psB[b2]], [R1B])
        load_w(wb[:, 0, :].rearrange("p (a n) -> p a n", a=8),
               w_in_d[l][:, 1024:1536].rearrange("(a p) n -> p a n", p=128), bf("wb0"))
        wv = wb[:, 0, :].rearrange("p (a n) -> p a n", a=8)
        for tt in range(16):
            t4 = tt // 4
            b = bank()
            for dk in range(8):
                mm(ps[b][:], hb[:, dk, tt * 128:(tt + 1) * 128], wv[:, dk, :], dk == 0, dk == 7,
                   [hbB[dk][t4], bf("wb0")], [psB[b]])
            act(lambda h, b=b: h.copy(out=v_tm[:], in_=ps[b][:]), [psB[b]], [bf("qk_tm")])
            P.dma("sp", "v", lambda h, tt=tt: h.dma_start(out=vscr[tt * 128:(tt + 1) * 128, :], in_=v_tm[:]),
                  reads=[bf("qk_tm")], writes=[bf("vscr")])
        load_w(wb[:, 0, :].rearrange("p (a n) -> p a n", a=8),
               w_in_d[l][:, 1536:2048].rearrange("(a p) n -> p a n", p=128), bf("wb0"))
        wv = wb[:, 0, :].rearrange("p (a n) -> p a n", a=8)
        for c in range(4):
            for t in range(4):
                b = bank()
                for dk in range(8):
                    mm(ps[b][:], wv[:, dk, c * 128:(c + 1) * 128], hb[:, dk, t * 512:(t + 1) * 512], dk == 0, dk == 7,
                       [hbB[dk][t], bf("wb0")], [psB[b]])
                act(lambda h, b=b, c=c, t=t: h.copy(out=uT[:, c, t * 512:(t + 1) * 512], in_=ps[b][:]), [psB[b]], [uTB[c][t]])

        if STAGE < 2:
            return
        def tokset(br, i):
            if br == 0:
                return slice(128 * i, 128 * i + 128)
            if br == 1:
                r, J = i // 4, i % 4
                return slice(512 * J + r, 512 * J + 512, 4)
            return slice(i, S, 16)

        def vslot(br, i):
            if br == 0:
                return i
            if br == 1:
                r, J = i // 4, i % 4
                return 4 * J + r
            return i

        subseqs = [
            (0, [list(range(16))]),
            (1, [[r * 4 + J for J in range(4)] for r in range(4)]),
            (2, [[r] for r in range(16)]),
        ]
        for hh in range(8):
            hs = slice((hh % 2) * 64, (hh % 2) * 64 + 64)
            cch = hh // 2
            vcol = vscr[:, 64 * hh:64 * hh + 64]
            srcs = [vcol.rearrange("(j p) e -> p j e", p=128),
                    vcol.rearrange("(J p r) e -> p J r e", J=4, r=4),
                    vcol.rearrange("(p r) e -> p r e", r=16)]
            items = []
            for (br, seqs) in subseqs:
                for seq in seqs:
                    for ii, i in enumerate(seq):
                        items.append((br, i, seq[ii + 1] if ii + 1 < len(seq) else None, ii > 0))
            st_pt = {}

            def stage_a(n):
                br, i, nxt, _ = items[n]
                ks = tokset(br, i)
                b = bank()
                mm(ps[b][:, 0:128], kT[hs, cch, ks], qT[hs, cch, ks], True, True, [R1B], [psB[b]], sig=nxt is None)
                if nxt is not None:
                    mm(ps[b][:, 128:256], kT[hs, cch, ks], qT[hs, cch, tokset(br, nxt)], True, True,
                       [R1B], [psB[b]], sig=True)
                w_ = 256 if nxt is not None else 128
                pt = PT[n % 3]
                ptB = bf("PT%d" % (n % 3))
                act(lambda h, b=b, pt=pt, w_=w_: h.activation(out=pt[:, 0:w_], in_=ps[b][:, 0:w_], func=AF.Exp, scale=0.125),
                    [psB[b]], [ptB])
                pool(lambda h, pt=pt, w_=w_: h.tensor_tensor(out=pt[:, 0:w_], in0=pt[:, 0:w_], in1=maskb[:, 0:w_], op=ALU.mult),
                     [ptB, bf("maskb")], [ptB])
                st_pt[n] = (pt, ptB)

            def stage_b(n):
                br, i, nxt, has_prev = items[n]
                if n == 0 or items[n - 1][0] != br:
                    if br == 1:
                        pairs = [(vh[:, 0, 4 * J:4 * J + 4, 0:64], srcs[1][:, J]) for J in range(4)]
                    else:
                        pairs = [(vh[:, 0, 8 * hf:8 * hf + 8, 0:64], srcs[br][:, 8 * hf:8 * hf + 8]) for hf in range(2)]
                    for (dst, src) in pairs:
                        P.dma("sp", "vh", lambda h, dst=dst, src=src: h.dma_start(out=dst, in_=src),
                              reads=[bf("vscr")], writes=[bf("vh0")])
                ks = tokset(br, i)
                pt, ptB = st_pt[n]
                bo = bank()
                if has_prev:
                    ppt, pptB = st_pt[n - 1]
                    mm(ps[bo][:, 0:128], vh[:, 0, vslot(br, items[n - 1][1]), :], ppt[:, 128:256], True, False,
                       [bf("vh0"), pptB], [psB[bo]], sig=False)
                mm(ps[bo][:, 0:128], vh[:, 0, vslot(br, i), :], pt[:, 0:128], not has_prev, True,
                   [bf("vh0"), ptB], [psB[bo]], sig=True)
                if br == 0:
                    dve(lambda h, bo=bo, ks=ks: h.tensor_copy(out=Oacc[:, ks], in_=ps[bo][:, 0:128]), [psB[bo]], [R2B])
                else:
                    dve(lambda h, bo=bo, ks=ks: h.tensor_tensor(out=Oacc[:, ks], in0=Oacc[:, ks], in1=ps[bo][:, 0:128], op=ALU.add),
                        [psB[bo], R2B], [R2B])

            stage_a(0)
            for n in range(len(items)):
                if n + 1 < len(items):
                    stage_a(n + 1)
                stage_b(n)
            dve(lambda h: h.tensor_copy(out=tmpA[0:64, :], in_=Oacc[64:128, :]), [R2B], [bf("tmpA")])
            dve(lambda h: h.reciprocal(out=tmpA[0:64, :], in_=tmpA[0:64, :]), [bf("tmpA")], [bf("tmpA")])
            wr = [hbB[cch][t] for t in range(4)]
            dve(lambda h, hs=hs, cch=cch: h.tensor_tensor(out=hb[hs, cch, :], in0=Oacc[0:64, :], in1=tmpA[0:64, :], op=ALU.mult),
                [R2B, bf("tmpA")], wr)
        rms_half(l, 0, "attn_gain")

        if STAGE < 3:
            return
        ssm_setup(l)
        T_ = bf("ssmtab")
        bank_pool[0] = [4, 5, 6, 7]
        for gp in range(16):
            j = gp // 4
            gl0 = (2 * gp) % 8
            for ri, srcB in enumerate([bbr, bbi]):
                dve(lambda h: h.memset(Zf[:, 0, :], 0.0), [], [bf("Zf")])
                dve(lambda h, srcB=srcB, gp=gp, gl0=gl0: h.tensor_copy(out=Zf[0:64, 0, 16 * gl0:16 * gl0 + 16], in_=srcB[0:64, gp, :]), [T_], [bf("Zf")])
                dve(lambda h, srcB=srcB, gp=gp, gl0=gl0: h.tensor_copy(out=Zf[64:128, 0, 16 * gl0 + 16:16 * gl0 + 32], in_=srcB[64:128, gp, :]), [T_], [bf("Zf")])
                b = bank()
                tr(ps[b][:, 0:128], Zf[:, 0, :], identf[:], [bf("Zf"), bf("identf")], [psB[b]])
                act(lambda h, b=b, ri=ri: h.copy(out=BbT[:, ri, :], in_=ps[b][:, 0:128]), [psB[b]], [bf("BbT")])
            dve(lambda h: h.memset(Zf[:], 0.0), [], [bf("Zf")])
            for ri in range(2):
                dve(lambda h, ri=ri, gl0=gl0, j=j: h.tensor_copy(out=Zf[0:64, ri, 16 * gl0:16 * gl0 + 16], in_=CT[0:64, ri, j, 16 * gl0:16 * gl0 + 16]), [T_], [bf("Zf")])
                dve(lambda h, ri=ri, gl0=gl0, j=j: h.tensor_copy(out=Zf[64:128, ri, 16 * gl0 + 16:16 * gl0 + 32], in_=CT[0:64, ri, j, 16 * gl0 + 16:16 * gl0 + 32]), [T_], [bf("Zf")])
            dve(lambda h: h.tensor_copy(out=Cm[:], in_=Zf[:]), [bf("Zf")], [bf("Cm")])
            for t in range(4):
                sl = slice(t * 512, (t + 1) * 512)
                for ri in range(2):
                    b = bank()
                    mm(ps[b][:], BbT[:, ri, :], uT[:, j, sl], True, True, [bf("BbT"), uTB[j][t]], [psB[b]])
                    if ri == 0:
                        act(lambda h, b=b, sl=sl: h.copy(out=X0[:, 0, sl], in_=ps[b][:]), [psB[b]], [R1B])
                    else:
                        act(lambda h, b=b, sl=sl: h.copy(out=X0[:, 1, sl], in_=ps[b][:]), [psB[b]], [R1B])
            src, dst = X0, X1
            for k in range(11):
                d = 1 << k
                ar = LPr[:, k, gp:gp + 1]
                ai = LPi[:, k, gp:gp + 1]
                an = LPn[:, k, gp:gp + 1]
                rdx = [R1B, T_]
                dve(lambda h, src=src, dst=dst, d=d: h.tensor_copy(out=dst[:, :, 0:d], in_=src[:, :, 0:d]), rdx, [R1B])
                dve(lambda h, src=src, dst=dst, d=d, ar=ar: h.scalar_tensor_tensor(
                    out=dst[:, 0, d:S], in0=src[:, 0, 0:S - d], scalar=ar, in1=src[:, 0, d:S], op0=ALU.mult, op1=ALU.add), rdx, [R1B])
                dve(lambda h, src=src, dst=dst, d=d, an=an: h.scalar_tensor_tensor(
                    out=dst[:, 0, d:S], in0=src[:, 1, 0:S - d], scalar=an, in1=dst[:, 0, d:S], op0=ALU.mult, op1=ALU.add), rdx, [R1B])
                dve(lambda h, src=src, dst=dst, d=d, ar=ar: h.scalar_tensor_tensor(
                    out=dst[:, 1, d:S], in0=src[:, 1, 0:S - d], scalar=ar, in1=src[:, 1, d:S], op0=ALU.mult, op1=ALU.add), rdx, [R1B])
                dve(lambda h, src=src, dst=dst, d=d, ai=ai: h.scalar_tensor_tensor(
                    out=dst[:, 1, d:S], in0=src[:, 0, 0:S - d], scalar=ai, in1=dst[:, 1, d:S], op0=ALU.mult, op1=ALU.add), rdx, [R1B])
                src, dst = dst, src
            act(lambda h, src=src: h.copy(out=XB[:], in_=src[:]), [R1B], [R2B])
            for t in range(4):
                sl = slice(t * 512, (t + 1) * 512)
                first_gp = (gp % 4 == 0)
                last_gp = (gp % 4 == 3)
                mm(ps[t][:], Cm[:, 0, :], XB[:, 0, sl], first_gp, False, [bf("Cm"), R2B], [psB[t]], sig=False)
                mm(ps[t][:], Cm[:, 1, :], XB[:, 1, sl], False, False, [bf("Cm"), R2B], [psB[t]], sig=not last_gp)
                if last_gp:
                    mm(ps[t][:], Dd[:, j, :], uT[:, j, sl], False, True, [T_, uTB[j][t]], [psB[t]], sig=True)
                    act(lambda h, t=t: h.activation(out=tmpB[:], in_=ps[t][:], func=AF.Square), [psB[t]], [bf("tmpB")])
                    dve(lambda h: h.tensor_scalar(out=tmpB[:], in0=tmpB[:], scalar1=0.044715, scalar2=1.0, op0=ALU.mult, op1=ALU.add),
                        [bf("tmpB")], [bf("tmpB")])
                    dve(lambda h, t=t: h.tensor_tensor(out=tmpB[:], in0=tmpB[:], in1=ps[t][:], op=ALU.mult), [bf("tmpB"), psB[t]], [bf("tmpB")])
                    act(lambda h: h.activation(out=tmpB[:], in_=tmpB[:], func=AF.Sigmoid, scale=1.5957691216057308), [bf("tmpB")], [bf("tmpB")])
                    dve(lambda h, t=t, sl=sl, j=j: h.tensor_tensor(out=uT[:, j, sl], in0=tmpB[:], in1=ps[t][:], op=ALU.mult),
                        [bf("tmpB"), psB[t]], [uTB[j][t]])
        bank_pool[0] = list(range(8))
        load_w(wb[:, 0, 0:2048].rearrange("p (a n) -> p a n", a=4),
               w_glu_d[l].rearrange("(a p) n -> p a n", p=128), bf("wb0"))
        wg = wb[:, 0, 0:2048].rearrange("p (a n) -> p a n", a=4)
        for co in range(4):
            bg = pcol("b_glu", l * 4 + co)
            for t in range(4):
                sl = slice(t * 512, (t + 1) * 512)
                b = bank()
                for ci in range(4):
                    mm(ps[b][:], wg[:, ci, co * 128:(co + 1) * 128], uT[:, ci, sl], ci == 0, ci == 3, [bf("wb0"), uTB[ci][t]], [psB[b]])
                act(lambda h, b=b, bg=bg: h.activation(out=tmpB[:], in_=ps[b][:], func=AF.Sigmoid, bias=bg, scale=1.0),
                    [psB[b], bf("prm")], [bf("tmpB")])
                dve(lambda h, co=co, sl=sl: h.tensor_tensor(out=hb[:, 4 + co, sl], in0=uT[:, co, sl], in1=tmpB[:], op=ALU.mult),
                    [bf("tmpB"), uTB[co][t]], [hbB[4 + co][t]])
        rms_half(l, 4, "ssm_gain")

        if dbg == "mixed" and s == 0 and l == 0:
            for c in range(8):
                dve(lambda h, c=c: h.tensor_copy(out=tmpA[:], in_=hb[:, c, :]), [hbB[c][t] for t in range(4)], [bf("tmpA")])
                P.dma("sp", "st", lambda h, c=c: h.dma_start(out=dbg_d[:, c, :], in_=tmpA[:]), reads=[bf("tmpA")])

        if STAGE < 4:
            return
        for co in range(8):
            if co % 4 == 0:
                half = co // 4
                load_w(wb[:, 0, :].rearrange("p (a n) -> p a n", a=8),
                       w_out_d[l][:, half * 512:(half + 1) * 512].rearrange("(a p) n -> p a n", p=128), bf("wb0"))
            wv = wb[:, 0, :].rearrange("p (a n) -> p a n", a=8)
            cl = co % 4
            bo_ = pcol("b_out", l * 8 + co)
            for t in range(4):
                sl = slice(t * 512, (t + 1) * 512)
                b = bank()
                for dk in range(8):
                    mm(ps[b][:], wv[:, dk, cl * 128:(cl + 1) * 128], hb[:, dk, sl], dk == 0, dk == 7,
                       [bf("wb0"), hbB[dk][t]], [psB[b]])
                act(lambda h, b=b, bo_=bo_: h.activation(out=tmpB[:], in_=ps[b][:], func=AF.Identity, bias=bo_, scale=1.0),
                    [psB[b], bf("prm")], [bf("tmpB")])
                dve(lambda h, co=co, sl=sl: h.scalar_tensor_tensor(out=h32[:, co, sl], in0=h32[:, co, sl], scalar=ALPHA, in1=tmpB[:],
                                                                 op0=ALU.mult, op1=ALU.add), [hB[co][t], bf("tmpB")], [hB[co][t]])
        P.barrier()
        layer_norm(l, "ln1_g", "ln1_b")
        if dbg == "ln1" and s == 0 and l == 0:
            for c in range(8):
                P.dma("sp", "st", lambda h, c=c: h.dma_start(out=dbg_d[:, c, :], in_=h32[:, c, :]), reads=[hB[c][t] for t in range(4)])

        if STAGE < 5:
            return
        P.barrier()
        W1e = [R1b[:, 0:4096].rearrange("p (c n) -> p c n", c=8), R1b[:, 4096:8192].rearrange("p (c n) -> p c n", c=8)]
        W2e = [R1b[:, 8192:12288].rearrange("p (c n) -> p c n", c=4), R1b[:, 12288:16384].rearrange("p (c n) -> p c n", c=4)]

        def load_pass(pp):
            bi = pp % 2
            load_w(W1e[bi], w_ff1_d[l][:, pp * 512:(pp + 1) * 512].rearrange("(a p) n -> p a n", p=128), bf("W1e%d" % bi))
            load_w(W2e[bi], w_ff2_d[l][pp * 512:(pp + 1) * 512, :].rearrange("(a p) n -> p a n", p=128), bf("W2e%d" % bi))

        load_pass(0)
        tile_ctr = 0
        for pp in range(8):
            if pp + 1 < 8:
                load_pass(pp + 1)
            bi = pp % 2
            for t in range(4):
                sl = slice(t * 512, (t + 1) * 512)
                ab = (tile_ctr % 2) * 4
                tile_ctr += 1
                for fc in range(4):
                    b = bank()
                    b1_ = pcol("b_ff1", l * 32 + pp * 4 + fc)
                    for dk in range(8):
                        mm(ps[b][:], W1e[bi][:, dk, fc * 128:(fc + 1) * 128], hb[:, dk, sl], dk == 0, dk == 7,
                           [bf("W1e%d" % bi), hbB[dk][t]], [psB[b]])
                    act(lambda h, b=b, b1_=b1_: h.activation(out=tbf2[:], in_=ps[b][:], func=AF.Relu, bias=b1_, scale=1.0),
                        [psB[b], bf("prm")], [bf("tbf2")])
                    dve(lambda h, fc=fc, ab=ab: h.tensor_tensor(out=aT[:, ab + fc, :], in0=tbf2[:], in1=tbf2[:], op=ALU.mult),
                        [bf("tbf2")], [bf("aT%d" % (ab + fc))])
                for co in range(8):
                    b = bank()
                    for fc in range(4):
                        mm(ps[b][:], W2e[bi][:, fc, co * 128:(co + 1) * 128], aT[:, ab + fc, :], fc == 0, fc == 3,
                           [bf("W2e%d" % bi), bf("aT%d" % (ab + fc))], [psB[b]])
                    if pp == 0:
                        b2_ = pcol("b_ff2", l * 8 + co)
                        act(lambda h, b=b, b2_=b2_: h.activation(out=tmpB[:], in_=ps[b][:], func=AF.Identity, bias=b2_, scale=1.0),
                            [psB[b], bf("prm")], [bf("tmpB")])
                        dve(lambda h, co=co, sl=sl: h.scalar_tensor_tensor(out=h32[:, co, sl], in0=h32[:, co, sl], scalar=ALPHA, in1=tmpB[:],
                                                                         op0=ALU.mult, op1=ALU.add), [hB[co][t], bf("tmpB")], [hB[co][t]])
                    else:
                        dve(lambda h, b=b, co=co, sl=sl: h.tensor_tensor(out=h32[:, co, sl], in0=h32[:, co, sl], in1=ps[b][:], op=ALU.add),
                            [hB[co][t], psB[b]], [hB[co][t]])
        P.barrier()
        layer_norm(l, "ln2_g", "ln2_b")
        P.barrier()
        P.new_epoch()

    for s in range(NSEQ):
        if "rope" not in KSKIP: P.dma("sp", "misc", lambda h, s=s: h.dma_start(out=posi[:], in_=pos_d[s].rearrange("(j p) -> p j", p=128)),
              writes=[bf("posi")])
        if "rope" not in KSKIP:
            dve(lambda h: h.tensor_copy(out=posf[:], in_=posi[:]), [bf("posi")], [bf("posf")])
        for tt in (range(16) if "rope" not in KSKIP else []):
            pc = posf[:, tt:tt + 1]
            dve(lambda h, tt=tt, pc=pc: h.tensor_scalar(out=cosT[:, tt, :], in0=invf[:], scalar1=pc, scalar2=None, op0=ALU.mult),
                [bf("posf"), bf("invf")], [bf("rope")])
        rr = [bf("rope")]
        if "rope" not in KSKIP:
            sin_of(sinT[:], cosT[:], 0.0, 512, rr, [bf("ropes")])
            sin_of(cosT[:], cosT[:], 0.5 * math.pi, 512, rr + [bf("ropes")], rr)
        for tt in range(int(os.environ.get("KXL", "16"))):
            t4 = tt // 4
            tq = 0 if os.environ.get("KXOFF") else tt
            P.dma("sp", "ld", lambda h, s=s, tt=tq: h.dma_start(out=xtm, in_=x_d[s, tt * 128:(tt + 1) * 128, :]),
                  writes=[bf("tmpA")])
            for g in range(2):
                b = bank()
                for c4 in range(4):
                    c = g * 4 + c4
                    tr(ps[b][:, c4 * 128:(c4 + 1) * 128], xtm[:, c * 128:(c + 1) * 128], identf[:], [bf("tmpA"), bf("identf")], [psB[b]])
                wr = [hB[g * 4 + c4][t4] for c4 in range(4)]
                wr2 = [hbB[g * 4 + c4][t4] for c4 in range(4)]
                pv = ps[b][:].rearrange("p (c n) -> p c n", c=4)
                if "h32copy" not in KSKIP:
                    dve(lambda h, pv=pv, g=g, tt=tt: h.tensor_copy(out=h32[:, g * 4:g * 4 + 4, tt * 128:(tt + 1) * 128], in_=pv), [psB[b]], wr)
                if "hbcopy" not in KSKIP:
                    act(lambda h, pv=pv, g=g, tt=tt: h.copy(out=hb[:, g * 4:g * 4 + 4, tt * 128:(tt + 1) * 128], in_=pv), [psB[b]], wr2)
        for l in range(NL):
            block(s, l, False)
        for tt in range(int(os.environ.get("KXS", "16"))):
            t4 = tt // 4
            for g in range(2):
                b = bank()
                for c4 in range(4):
                    c = g * 4 + c4
                    tr(ps[b][:, c4 * 128:(c4 + 1) * 128], h32[:, c, tt * 128:(tt + 1) * 128], identf[:], [hB[c][t4], bf("identf")], [psB[b]])
                dve(lambda h, b=b, g=g: h.tensor_copy(out=xtm[:, g * 512:(g + 1) * 512], in_=ps[b][:]), [psB[b]], [bf("tmpA")])
            P.dma("sp", "st", lambda h, s=s, tt=tt: h.dma_start(out=out_d[s, tt * 128:(tt + 1) * 128, :], in_=xtm),
                  reads=[bf("tmpA")])
        P.barrier()
    P.barrier()
    with nc.allow_non_contiguous_dma(reason="small strided param loads"), nc.Block() as blk:
        P.emit_all(blk)
    es.close()
    return nc, P


def consts():
    ident = np.eye(128, dtype=np.float32)
    k = np.arange(128)[:, None]
    q = np.arange(128)[None, :]
    mask = np.concatenate([(k <= q), (k >= q)], axis=1).astype(np.float32)
    invf = (10000.0 ** (-np.arange(32, dtype=np.float32) * 2.0 / 64.0)).astype(np.float32)
    invf = np.broadcast_to(invf[None, :], (128, 32)).copy()
    return {"c_ident": ident, "c_mask": mask, "c_invf": invf}


WNAMES = ["w_in", "attn_gain", "ssm_gain", "ssm_a_re", "ssm_a_im", "ssm_log_dt", "ssm_b_re", "ssm_b_im",
          "ssm_c_re", "ssm_c_im", "ssm_d", "w_glu", "b_glu", "w_out", "b_out", "ln1_g", "ln1_b",
          "w_ff1", "b_ff1", "w_ff2", "b_ff2", "ln2_g", "ln2_b"]

_CACHE = {}

LAUNCH_NL = 4
LAUNCH_NSEQ = 4


def kernel(**inputs):
    x = np.ascontiguousarray(inputs["x"], dtype=np.float32)
    pos = np.ascontiguousarray(inputs["positions"], dtype=np.int32)
    per_core = x.shape[0] // NCORES
    key = (LAUNCH_NL, LAUNCH_NSEQ)
    if key not in _CACHE:
        _CACHE[key] = build(LAUNCH_NL, LAUNCH_NSEQ)[0]
    nc = _CACHE[key]
    cs = consts()
    h = x
    for l0 in range(0, DEPTH, LAUNCH_NL):
        base = {n: np.ascontiguousarray(inputs[n][l0:l0 + LAUNCH_NL], dtype=np.float32) for n in WNAMES}
        base.update(cs)
        newh = np.empty_like(h)
        for s0 in range(0, per_core, LAUNCH_NSEQ):
            in_maps = []
            for c in range(NCORES):
                m = dict(base)
                lo = c * per_core + s0
                m["x"] = np.ascontiguousarray(h[lo:lo + LAUNCH_NSEQ])
                m["positions"] = np.ascontiguousarray(pos[lo:lo + LAUNCH_NSEQ])
                in_maps.append(m)
            res = run_bass_kernel_spmd(nc, in_maps, core_ids=list(range(NCORES)))
            for c in range(NCORES):
                lo = c * per_core + s0
                newh[lo:lo + LAUNCH_NSEQ] = res.results[c]["out"]
        h = newh
    return h
```

```python
import math
import os
STAGE = int(os.environ.get("KSTAGE", "9"))
KSKIP = set(os.environ.get("KSKIP", "").split(","))
from contextlib import ExitStack
import numpy as np
import concourse.bass as bass
import concourse.mybir as mybir
from concourse.bass_utils import run_bass_kernel_spmd

F32 = mybir.dt.float32
BF16 = mybir.dt.bfloat16
I32 = mybir.dt.int32
ALU = mybir.AluOpType
AF = mybir.ActivationFunctionType
ENGS = ("pe", "act", "dve", "pool", "sp")

D = 1024
S = 2048
DEPTH = 4
NCORES = 8
ALPHA = (2.0 * DEPTH) ** 0.25
LN_EPS = 1e-5
RMS_EPS = 1e-6
TWO_PI = 2.0 * math.pi


class Buf:
    __slots__ = ("name", "w", "r", "excl")

    def __init__(self, name, excl=False):
        self.name = name
        self.w = None
        self.r = {}
        self.excl = excl


class Prog:
    def __init__(self, nc, es):
        self.nc = nc
        self.es = es
        self.q = {e: [] for e in ENGS}
        self.sems = {}
        self.cnt = {}
        self.cur = {}
        self.dead = set()
        self.epoch = -1
        self.waited = {e: {} for e in ENGS}
        self.dma_cnt = {}
        self.dma_keys = []
        self.dma_rr = 0
        self.n_inst = 0
        self.new_epoch()

    def new_epoch(self):
        for k in self.cur.values():
            self.dead.add(k)
        self.epoch += 1
        for e in ENGS:
            k = "%s#%d" % (e, self.epoch)
            self.cur[e] = k
            self.cnt[k] = 0
            self.sems[k] = self.es.enter_context(self.nc.semaphore("s_%s_%d" % (e, self.epoch)))

    def _need(self, e, deps):
        out = []
        ce = self.cur[e]
        for (k, v) in deps:
            if k in self.dead:
                continue
            if k == ce and (e == "pe" or v > self.cnt[ce]):
                continue
            if self.waited[e].get(k, 0) >= v:
                continue
            self.waited[e][k] = v
            out.append((k, v))
        return out

    def _deps(self, reads, writes):
        deps = set()
        for b in reads:
            if b.w is not None:
                deps.add(b.w)
            if b.excl:
                for kv in b.r.items():
                    deps.add(kv)
        for b in writes:
            if b.w is not None:
                deps.add(b.w)
            for kv in b.r.items():
                deps.add(kv)
        return deps

    def op(self, e, fn, reads=(), writes=(), sig=True):
        waits = self._need(e, self._deps(reads, writes))
        ce = self.cur[e]
        if sig:
            self.cnt[ce] += 1
        tick = self.cnt[ce] if sig else self.cnt[ce] + 1
        sem = self.sems[ce]
        sems = self.sems

        def emit(h):
            for (k, v) in waits:
                h.wait_ge(sems[k], v)
            ins = fn(h)
            if sig:
                ins.then_inc(sem, 1)
        self.q[e].append(emit)
        self.n_inst += 1
        for b in reads:
            if b.excl:
                b.w = (ce, tick)
                b.r = {}
            elif b.r.get(ce, 0) < tick:
                b.r[ce] = tick
        for b in writes:
            b.w = (ce, tick)
            b.r = {}
        return tick

    def dma(self, qe, semkey, fn, reads=(), writes=()):
        semkey = self.dma_keys[self.dma_rr % len(self.dma_keys)]
        self.dma_rr += 1
        deps = self._deps(reads, writes)
        prev = self.dma_cnt.get(semkey, 0)
        if prev > 0:
            deps.add((semkey, prev))
        waits = self._need(qe, deps)
        self.dma_cnt[semkey] = prev + 16
        val = self.dma_cnt[semkey]
        sems = self.sems
        sem = sems[semkey]

        def emit(h):
            for (k, v) in waits:
                h.wait_ge(sems[k], v)
            fn(h).then_inc(sem, 16)
        self.q[qe].append(emit)
        self.n_inst += 1
        for b in reads:
            if b.r.get(semkey, 0) < val:
                b.r[semkey] = val
        for b in writes:
            b.w = (semkey, val)
            b.r = {}
        return val

    def wait_all(self, e):
        deps = [(self.cur[k], self.cnt[self.cur[k]]) for k in ENGS if self.cnt[self.cur[k]] > 0]
        deps += list(self.dma_cnt.items())
        waits = self._need(e, deps)
        sems = self.sems

        def emit(h):
            for (k, v) in waits:
                h.wait_ge(sems[k], v)
        self.q[e].append(emit)

    def barrier(self):
        for e in ENGS:
            self.wait_all(e)

    def emit_all(self, block):
        prog = self

        @block.tensor
        def _(h):
            for f in prog.q["pe"]:
                f(h)

        @block.scalar
        def _(h):
            for f in prog.q["act"]:
                f(h)

        @block.vector
        def _(h):
            for f in prog.q["dve"]:
                f(h)

        @block.gpsimd
        def _(h):
            for f in prog.q["pool"]:
                f(h)

        @block.sync
        def _(h):
            for f in prog.q["sp"]:
                f(h)


def build(NL, NSEQ, dbg=None):
    nc = bass.Bass("TRN2", target_bir_lowering=False)
    es = ExitStack()

    MINIO = os.environ.get("KMINIO", "")

    def din(name, shape, dt=F32):
        if MINIO and name not in ("x", "positions", "c_ident", "c_mask", "c_invf"):
            return None
        return nc.dram_tensor(name, list(shape), dt, kind="ExternalInput").ap()

    x_d = din("x", [NSEQ, S, D])
    pos_d = din("positions", [NSEQ, S], I32)
    w_in_d = din("w_in", [NL, D, 2048])
    attn_gain_d = din("attn_gain", [NL, 512])
    ssm_gain_d = din("ssm_gain", [NL, 512])
    a_re_d = din("ssm_a_re", [NL, 32, 64])
    a_im_d = din("ssm_a_im", [NL, 32, 64])
    log_dt_d = din("ssm_log_dt", [NL, 32])
    b_re_d = din("ssm_b_re", [NL, 32, 64, 16])
    b_im_d = din("ssm_b_im", [NL, 32, 64, 16])
    c_re_d = din("ssm_c_re", [NL, 32, 16, 64])
    c_im_d = din("ssm_c_im", [NL, 32, 16, 64])
    d_d = din("ssm_d", [NL, 32, 16])
    w_glu_d = din("w_glu", [NL, 512, 512])
    b_glu_d = din("b_glu", [NL, 512])
    w_out_d = din("w_out", [NL, D, D])
    b_out_d = din("b_out", [NL, D])
    ln1_g_d = din("ln1_g", [NL, D])
    ln1_b_d = din("ln1_b", [NL, D])
    w_ff1_d = din("w_ff1", [NL, D, 4096])
    b_ff1_d = din("b_ff1", [NL, 4096])
    w_ff2_d = din("w_ff2", [NL, 4096, D])
    b_ff2_d = din("b_ff2", [NL, D])
    ln2_g_d = din("ln2_g", [NL, D])
    ln2_b_d = din("ln2_b", [NL, D])
    ident_d = din("c_ident", [128, 128])
    mask_d = din("c_mask", [128, 256])
    invf_d = din("c_invf", [128, 32])
    out_d = nc.dram_tensor("out", [NSEQ, S, D], F32, kind="ExternalOutput").ap()
    vscr = None if MINIO else nc.dram_tensor("vscr", [S, 512], BF16, kind="ExternalOutput").ap()
    dbg_d = None
    if dbg:
        dbg_d = nc.dram_tensor("dbg", [128, 8, S], F32, kind="ExternalOutput").ap()

    def sb(name, shape, dt):
        return es.enter_context(nc.sbuf_tensor(name, list(shape), dt))

    P = Prog(nc, es)
    for i in range(int(os.environ.get("KNDMA", "8"))):
        k = "dma%d" % i
        P.sems[k] = es.enter_context(nc.semaphore("d_" + k))
        P.dma_keys.append(k)

    h32 = sb("h32", [128, 8, S], F32)
    hb = sb("hb", [128, 8, S], BF16)
    R1 = sb("R1", [128, 8192], F32)
    R1b = R1.bitcast(BF16)
    qT = R1b[:, 0:8192].rearrange("p (c n) -> p c n", c=4)
    kT = R1b[:, 8192:16384].rearrange("p (c n) -> p c n", c=4)
    X0 = R1[:, 0:4096].rearrange("p (c n) -> p c n", c=2)
    X1 = R1[:, 4096:8192].rearrange("p (c n) -> p c n", c=2)
    W1q = R1b[:, 0:8192].rearrange("p (c n) -> p c n", c=8)
    W2q = R1b[:, 8192:16384].rearrange("p (c n) -> p c n", c=8)
    uT = sb("uT", [128, 4, S], BF16)
    R2 = sb("R2", [128, 2048], F32)
    Oacc = R2
    R2b = R2.bitcast(BF16)
    XB = R2b[:, 0:4096].rearrange("p (c n) -> p c n", c=2)
    aT = R2b[:, 0:4096].rearrange("p (c n) -> p c n", c=8)
    vh = sb("vh", [128, 1, 16, 128], BF16)
    PT = [sb("PT%d" % i, [128, 256], BF16) for i in range(3)]
    tmpA = sb("tmpA", [128, 2048], F32)
    tmpB = sb("tmpB", [128, 512], F32)
    tmpC = sb("tmpC", [128, 512], F32)
    tbf2 = sb("tbf2", [128, 512], BF16)
    NWST = int(os.environ.get("KNWST", "2"))
    wsts = [sb("wst%d" % i, [128, 1024], F32) for i in range(NWST)]
    wst_ctr = [0]
    wb = sb("wb", [128, 1, 4096], BF16)
    xtm = tmpA[:, 0:1024]
    qk_tm = sb("qk_tm", [128, 512], BF16)
    v_tm = qk_tm
    identf = sb("identf", [128, 128], F32)
    identb = sb("identb", [128, 128], BF16)
    onesb = sb("onesb", [128, 128], BF16)
    maskb = sb("maskb", [128, 256], BF16)
    invf = sb("invf", [128, 32], F32)
    posf = sb("posf", [128, 16], F32)
    posi = sb("posi", [128, 16], I32)
    cosT = sb("cosT", [128, 16, 32], F32)
    sinT = sb("sinT", [128, 16, 32], F32)
    prm = sb("prm", [128, 512], F32)
    prm_stage = sb("prm_stage", [128, 128], F32)
    sA = sb("sA", [128, 16, 8], F32)
    LPr = sb("LPr", [128, 11, 16], F32)
    LPi = sb("LPi", [128, 11, 16], F32)
    LPn = sb("LPn", [128, 11, 16], F32)
    Bre = sb("Bre", [128, 16, 16], F32)
    Bim = sb("Bim", [128, 16, 16], F32)
    bbr = sb("bbr", [128, 16, 16], F32)
    bbi = sb("bbi", [128, 16, 16], F32)
    CT = sb("CT", [128, 2, 4, 128], BF16)
    Zf = sb("Zf", [128, 2, 128], F32)
    BbT = sb("BbT", [128, 2, 128], BF16)
    Cm = sb("Cm", [128, 2, 128], BF16)
    Dd = sb("Dd", [128, 4, 128], BF16)
    cstage = sb("cstage", [128, 64], F32)

    ps = [es.enter_context(nc.psum_tensor("ps%d" % i, [128, 512], F32)) for i in range(8)]
    psb = [p.bitcast(BF16) for p in ps]
    psB = [Buf("ps%d" % i, excl=True) for i in range(8)]
    bank_ctr = [0]
    bank_pool = [list(range(8))]

    def bank():
        pool = bank_pool[0]
        i = pool[bank_ctr[0] % len(pool)]
        bank_ctr[0] += 1
        return i

    B = {}

    def bf(name):
        if name not in B:
            B[name] = Buf(name)
        return B[name]

    hB = [[bf("h%d_%d" % (c, t)) for t in range(4)] for c in range(8)]
    hbB = [[bf("hb%d_%d" % (c, t)) for t in range(4)] for c in range(8)]
    R1B = bf("R1")
    uTB = [[bf("uT%d_%d" % (c, t)) for t in range(4)] for c in range(4)]
    R2B = bf("R2")

    def allhb():
        return [b for row in hbB for b in row]

    def allh():
        return [b for row in hB for b in row]

    def alluT():
        return [b for row in uTB for b in row]

    def dve(fn, reads, writes):
        return P.op("dve", fn, reads, writes)

    def act(fn, reads, writes):
        return P.op("act", fn, reads, writes)

    def pool(fn, reads, writes):
        return P.op("pool", fn, reads, writes)

    def mm(out, lhsT, rhs, start, stop, reads, writes, sig=None):
        if sig is None:
            sig = stop
        return P.op("pe", lambda h: h.matmul(out, lhsT=lhsT, rhs=rhs, start=start, stop=stop),
                    reads, writes, sig=sig)

    def tr(out, in_, ident, reads, writes):
        if os.environ.get("KTR", "") == "mm" and in_.dtype == F32:
            return P.op("pe", lambda h: h.matmul(out, lhsT=in_, rhs=ident, start=True, stop=True), reads, writes)
        return P.op("pe", lambda h: h.transpose(out=out, in_=in_, identity=ident), reads, writes)

    def load_w(dst, src, dstB, nparts=128):
        a, n = dst.shape[1], dst.shape[2]
        per = max(1, 1024 // n)
        for a0 in range(0, a, per):
            a1 = min(a, a0 + per)
            k = wst_ctr[0] % NWST
            wst_ctr[0] += 1
            wbuf = bf("wst%d" % k)
            st = wsts[k][:, 0:(a1 - a0) * n].rearrange("p (a n) -> p a n", a=a1 - a0)
            P.dma("sp", "w", lambda h, st=st, a0=a0, a1=a1: h.dma_start(out=st, in_=src[:, a0:a1, :]),
                  writes=[wbuf])
            pool(lambda h, st=st, a0=a0, a1=a1: h.tensor_copy(out=dst[:, a0:a1, :], in_=st),
                 [wbuf], [dstB])

    P.dma("sp", "misc", lambda h: h.dma_start(out=identf[:], in_=ident_d), writes=[bf("identf")])
    P.dma("sp", "misc", lambda h: h.dma_start(out=tmpB[:, 0:256], in_=mask_d), writes=[bf("tmpB")])
    P.dma("sp", "misc", lambda h: h.dma_start(out=invf[:], in_=invf_d), writes=[bf("invf")])
    dve(lambda h: h.tensor_copy(out=identb[:], in_=identf[:]), [bf("identf")], [bf("identb")])
    dve(lambda h: h.tensor_copy(out=maskb[:], in_=tmpB[:, 0:256]), [bf("tmpB")], [bf("maskb")])
    dve(lambda h: h.memset(onesb[:], 1.0), [], [bf("onesb")])
    if "vh" not in KSKIP:
        dve(lambda h: h.memset(vh[:, 0, :, :], 1.0), [], [bf("vh0")])

    prm_cols = {}
    col = [0]

    def load_T(name, src2d, R):
        if "prm" in KSKIP:
            prm_cols[name] = col[0]; col[0] += R
            return
        c0 = col[0]
        col[0] += R
        prm_cols[name] = c0
        P.dma("sp", "misc", lambda h: h.dma_start(out=prm_stage[0:R, :], in_=src2d), writes=[bf("prm_stage")])
        b = bank()
        tr(ps[b][:, 0:R], prm_stage[0:R, :], identf[0:R, 0:R], [bf("prm_stage"), bf("identf")], [psB[b]])
        dve(lambda h: h.tensor_copy(out=prm[:, c0:c0 + R], in_=ps[b][:, 0:R]), [psB[b]], [bf("prm")])
        return c0

    if not MINIO:
        load_T("b_ff1", b_ff1_d.rearrange("l (c p) -> (l c) p", p=128), NL * 32)
        for nm, ap_ in [("b_out", b_out_d), ("ln1_g", ln1_g_d), ("ln1_b", ln1_b_d), ("b_ff2", b_ff2_d),
                        ("ln2_g", ln2_g_d), ("ln2_b", ln2_b_d)]:
            load_T(nm, ap_.rearrange("l (c p) -> (l c) p", p=128), NL * 8)
        for nm, ap_ in [("attn_gain", attn_gain_d), ("ssm_gain", ssm_gain_d), ("b_glu", b_glu_d)]:
            load_T(nm, ap_.rearrange("l (c p) -> (l c) p", p=128), NL * 4)
        load_T("ssm_d", d_d.rearrange("l (c g) n -> (l c) (g n)", g=8), NL * 4)
        for nm, ap_ in [("a_re", a_re_d), ("a_im", a_im_d)]:
            load_T(nm, ap_.rearrange("l (gp two) p -> (l gp) (two p)", two=2), NL * 16)
    assert col[0] <= 512

    def pcol(name, idx):
        c = prm_cols[name] + idx
        return prm[:, c:c + 1]

    def sin_of(dst, src, add, n, rd, wr):
        ti = tmpB.bitcast(I32)[:, 0:n]
        tf = tmpC[:, 0:n]
        tg = tmpB[:, 0:n] if False else None
        if len(dst.shape) == 3:
            a_, b_ = dst.shape[1], dst.shape[2]
            ti = ti.rearrange("p (a b) -> p a b", a=a_)
            tf = tf.rearrange("p (a b) -> p a b", a=a_)
        T = [bf("tmpB"), bf("tmpC")]
        dve(lambda h: h.tensor_scalar(out=dst, in0=src, scalar1=float(add), scalar2=None, op0=ALU.add), rd, wr)
        dve(lambda h: h.tensor_scalar(out=tf, in0=dst, scalar1=1.0 / TWO_PI, scalar2=None, op0=ALU.mult), wr, T)
        dve(lambda h: h.tensor_copy(out=ti, in_=tf), T, T)
        dve(lambda h: h.tensor_copy(out=tf, in_=ti), T, T)
        dve(lambda h: h.scalar_tensor_tensor(out=dst, in0=tf, scalar=-TWO_PI, in1=dst, op0=ALU.mult, op1=ALU.add), T + wr, wr)
        dve(lambda h: h.tensor_scalar(out=tf, in0=dst, scalar1=-math.pi, scalar2=1e30, op0=ALU.add, op1=ALU.mult), wr, T)
        dve(lambda h: h.tensor_scalar(out=tf, in0=tf, scalar1=0.0, scalar2=1.0, op0=ALU.max, op1=ALU.min), T, T)
        dve(lambda h: h.scalar_tensor_tensor(out=dst, in0=tf, scalar=-TWO_PI, in1=dst, op0=ALU.mult, op1=ALU.add), T + wr, wr)
        act(lambda h: h.activation(out=dst, in_=dst, func=AF.Sin), wr, wr)

    def ssm_setup(l):
        T_ = bf("ssmtab")
        for gpar in range(2):
            src = bass.AP(log_dt_d.tensor, l * 32 + gpar, [[0, 64], [2, 16], [1, 1]])
            P.dma("sp", "misc", lambda h, src=src, gpar=gpar: h.dma_start(
                out=sA[gpar * 64:(gpar + 1) * 64, :, 0:1], in_=src), writes=[T_])
        are = prm[:, prm_cols["a_re"] + l * 16: prm_cols["a_re"] + l * 16 + 16]
        aim = prm[:, prm_cols["a_im"] + l * 16: prm_cols["a_im"] + l * 16 + 16]
        dt_ = sA[:, :, 0]
        mag = sA[:, :, 1]
        ang = sA[:, :, 2]
        t3 = sA[:, :, 3]
        t4 = sA[:, :, 4]
        t5 = sA[:, :, 5]
        t6 = sA[:, :, 6]
        t7 = sA[:, :, 7]
        rd = [T_, bf("prm")]
        act(lambda h: h.activation(out=dt_, in_=dt_, func=AF.Exp), rd, [T_])
        dve(lambda h: h.tensor_tensor(out=mag, in0=are, in1=dt_, op=ALU.mult), rd, [T_])
        act(lambda h: h.activation(out=mag, in_=mag, func=AF.Exp), rd, [T_])
        dve(lambda h: h.tensor_tensor(out=ang, in0=aim, in1=dt_, op=ALU.mult), rd, [T_])
        sin_of(t3, ang, 0.0, 16, rd, [T_])
        sin_of(t4, ang, 0.5 * math.pi, 16, rd, [T_])
        lr = LPr[:, 0, :]
        li = LPi[:, 0, :]
        dve(lambda h: h.tensor_tensor(out=lr, in0=mag, in1=t4, op=ALU.mult), rd, [T_])
        dve(lambda h: h.tensor_tensor(out=li, in0=mag, in1=t3, op=ALU.mult), rd, [T_])
        for k in range(1, 11):
            pr, pi_, nr_, ni_ = LPr[:, k - 1, :], LPi[:, k - 1, :], LPr[:, k, :], LPi[:, k, :]
            dve(lambda h, pr=pr, pi_=pi_: h.tensor_tensor(out=t5, in0=pr, in1=pr, op=ALU.mult), rd, [T_])
            dve(lambda h, pr=pr, pi_=pi_: h.tensor_tensor(out=t6, in0=pi_, in1=pi_, op=ALU.mult), rd, [T_])
            dve(lambda h, nr_=nr_: h.tensor_tensor(out=nr_, in0=t5, in1=t6, op=ALU.subtract), rd, [T_])
            dve(lambda h, pr=pr, pi_=pi_: h.tensor_tensor(out=t5, in0=pr, in1=pi_, op=ALU.mult), rd, [T_])
            dve(lambda h, ni_=ni_: h.tensor_scalar(out=ni_, in0=t5, scalar1=2.0, scalar2=None, op0=ALU.mult), rd, [T_])
        dve(lambda h: h.tensor_scalar(out=LPn[:], in0=LPi[:], scalar1=-1.0, scalar2=None, op0=ALU.mult), rd, [T_])
        dve(lambda h: h.tensor_scalar(out=t3, in0=lr, scalar1=-1.0, scalar2=None, op0=ALU.add), rd, [T_])
        dve(lambda h: h.tensor_tensor(out=t4, in0=are, in1=are, op=ALU.mult), rd, [T_])
        dve(lambda h: h.tensor_tensor(out=t5, in0=aim, in1=aim, op=ALU.mult), rd, [T_])
        dve(lambda h: h.tensor_tensor(out=t4, in0=t4, in1=t5, op=ALU.add), rd, [T_])
        dve(lambda h: h.reciprocal(out=t4, in_=t4), rd, [T_])
        dve(lambda h: h.tensor_tensor(out=t5, in0=t3, in1=are, op=ALU.mult), rd, [T_])
        dve(lambda h: h.tensor_tensor(out=t6, in0=li, in1=aim, op=ALU.mult), rd, [T_])
        dve(lambda h: h.tensor_tensor(out=t5, in0=t5, in1=t6, op=ALU.add), rd, [T_])
        dve(lambda h: h.tensor_tensor(out=t5, in0=t5, in1=t4, op=ALU.mult), rd, [T_])
        dve(lambda h: h.tensor_tensor(out=t6, in0=li, in1=are, op=ALU.mult), rd, [T_])
        dve(lambda h: h.tensor_tensor(out=t7, in0=t3, in1=aim, op=ALU.mult), rd, [T_])
        dve(lambda h: h.tensor_tensor(out=t6, in0=t6, in1=t7, op=ALU.subtract), rd, [T_])
        dve(lambda h: h.tensor_tensor(out=t6, in0=t6, in1=t4, op=ALU.mult), rd, [T_])
        for (dst, src_d) in [(Bre, b_re_d), (Bim, b_im_d)]:
            src = src_d[l].rearrange("(gp two) p n -> (two p) gp n", two=2)
            P.dma("sp", "misc", lambda h, dst=dst, src=src: h.dma_start(out=dst[:], in_=src), writes=[T_])
        crb = bass.AP(sA, 5, [[128, 128], [8, 16], [0, 16]])
        cib = bass.AP(sA, 6, [[128, 128], [8, 16], [0, 16]])
        dve(lambda h: h.tensor_tensor(out=bbr[:], in0=Bre[:], in1=crb, op=ALU.mult), rd, [T_])
        dve(lambda h: h.tensor_tensor(out=bbi[:], in0=Bim[:], in1=cib, op=ALU.mult), rd, [T_])
        dve(lambda h: h.tensor_tensor(out=bbr[:], in0=bbr[:], in1=bbi[:], op=ALU.subtract), rd, [T_])
        dve(lambda h: h.tensor_tensor(out=bbi[:], in0=Bim[:], in1=crb, op=ALU.mult), rd, [T_])
        dve(lambda h: h.tensor_tensor(out=Bre[:], in0=Bre[:], in1=cib, op=ALU.mult), rd, [T_])
        dve(lambda h: h.tensor_tensor(out=bbi[:], in0=bbi[:], in1=Bre[:], op=ALU.add), rd, [T_])
        for ri, src_d in enumerate([c_re_d, c_im_d]):
            for j in range(4):
                src = src_d[l, 8 * j:8 * j + 8].rearrange("g n p -> (g n) p")
                P.dma("sp", "misc", lambda h, src=src: h.dma_start(out=cstage[:], in_=src), writes=[bf("cstage")])
                b = bank()
                tr(ps[b][0:64, 0:128], cstage[:, :], identf[:], [bf("cstage"), bf("identf")], [psB[b]])
                if ri == 0:
                    dve(lambda h, b=b, j=j: h.tensor_copy(out=CT[0:64, 0, j, :], in_=ps[b][0:64, 0:128]), [psB[b]], [T_])
                else:
                    dve(lambda h, b=b, j=j: h.tensor_scalar(out=CT[0:64, 1, j, :], in0=ps[b][0:64, 0:128],
                                                           scalar1=-1.0, scalar2=None, op0=ALU.mult), [psB[b]], [T_])
        for j in range(4):
            dc = pcol("ssm_d", l * 4 + j)
            dve(lambda h, j=j, dc=dc: h.tensor_scalar(out=Dd[:, j, :], in0=identf[:], scalar1=dc, scalar2=None,
                                                    op0=ALU.mult), [bf("identf"), bf("prm")], [T_])

    def layer_norm(l, gname, bname):
        for t in range(4):
            sl = slice(t * 512, (t + 1) * 512)
            b1 = bank()
            b2 = bank()
            for c in range(8):
                act(lambda h, c=c, sl=sl: h.copy(out=tbf2[:], in_=h32[:, c, sl]), [hB[c][t]], [bf("tbf2")])
                mm(ps[b1][:], onesb[:], tbf2[:], c == 0, c == 7, [bf("tbf2"), bf("onesb")], [psB[b1]], sig=True)
                act(lambda h, c=c, sl=sl: h.activation(out=qk_tm[:], in_=h32[:, c, sl], func=AF.Square),
                    [hB[c][t]], [bf("qk_tm")])
                mm(ps[b2][:], onesb[:], qk_tm[:], c == 0, c == 7, [bf("qk_tm"), bf("onesb")], [psB[b2]], sig=True)
            act(lambda h, b1=b1: h.activation(out=tmpB[:], in_=ps[b1][:], func=AF.Identity, scale=1.0 / D), [psB[b1]], [bf("tmpB")])
            dve(lambda h: h.tensor_tensor(out=tmpC[:], in0=tmpB[:], in1=tmpB[:], op=ALU.mult), [bf("tmpB")], [bf("tmpC")])
            dve(lambda h, b2=b2: h.scalar_tensor_tensor(out=tmpC[:], in0=ps[b2][:], scalar=1.0 / D, in1=tmpC[:],
                                                 op0=ALU.mult, op1=ALU.subtract), [psB[b2], bf("tmpC")], [bf("tmpC")])
            act(lambda h: h.activation(out=tmpC[:], in_=tmpC[:], func=AF.Sqrt, bias=LN_EPS, scale=1.0), [bf("tmpC")], [bf("tmpC")])
            dve(lambda h: h.reciprocal(out=tmpC[:], in_=tmpC[:]), [bf("tmpC")], [bf("tmpC")])
            for c in range(8):
                g_ = pcol(gname, l * 8 + c)
                b_ = pcol(bname, l * 8 + c)
                dve(lambda h, c=c, sl=sl: h.tensor_tensor(out=h32[:, c, sl], in0=h32[:, c, sl], in1=tmpB[:], op=ALU.subtract),
                    [hB[c][t], bf("tmpB")], [hB[c][t]])
                dve(lambda h, c=c, sl=sl: h.tensor_tensor(out=h32[:, c, sl], in0=h32[:, c, sl], in1=tmpC[:], op=ALU.mult),
                    [hB[c][t], bf("tmpC")], [hB[c][t]])
                dve(lambda h, c=c, g_=g_, b_=b_, sl=sl: h.tensor_scalar(out=h32[:, c, sl], in0=h32[:, c, sl], scalar1=g_, scalar2=b_,
                                                                  op0=ALU.mult, op1=ALU.add), [hB[c][t], bf("prm")], [hB[c][t]])
                act(lambda h, c=c, sl=sl: h.copy(out=hb[:, c, sl], in_=h32[:, c, sl]), [hB[c][t]], [hbB[c][t]])

    def rms_half(l, c0, gname):
        bs = [bank() for _ in range(4)]
        for ci in range(4):
            c = c0 + ci
            for t in range(4):
                sl = slice(t * 512, (t + 1) * 512)
                dve(lambda h, c=c, sl=sl: h.tensor_tensor(out=tbf2[:], in0=hb[:, c, sl], in1=hb[:, c, sl], op=ALU.mult),
                    [hbB[c][t]], [bf("tbf2")])
                mm(ps[bs[t]][:], onesb[:], tbf2[:], ci == 0, ci == 3, [bf("tbf2"), bf("onesb")], [psB[bs[t]]], sig=True)
        for t in range(4):
            act(lambda h, t=t: h.activation(out=tmpA[:, t * 512:(t + 1) * 512], in_=ps[bs[t]][:], func=AF.Sqrt,
                                            bias=RMS_EPS, scale=1.0 / 512), [psB[bs[t]]], [bf("tmpA")])
        dve(lambda h: h.reciprocal(out=tmpA[:], in_=tmpA[:]), [bf("tmpA")], [bf("tmpA")])
        for ci in range(4):
            c = c0 + ci
            g_ = pcol(gname, l * 4 + ci)
            rd = [hbB[c][t] for t in range(4)]
            dve(lambda h, c=c, g_=g_: h.scalar_tensor_tensor(out=hb[:, c, :], in0=hb[:, c, :], scalar=g_, in1=tmpA[:],
                                                             op0=ALU.mult, op1=ALU.mult), rd + [bf("tmpA"), bf("prm")], rd)

    def block(s, l, first):
        if STAGE < 1:
            return
        for piece in range(2):
            slot = 0
            load_w(wb[:, slot, :].rearrange("p (a n) -> p a n", a=8),
                   w_in_d[l][:, piece * 512:(piece + 1) * 512].rearrange("(a p) n -> p a n", p=128), bf("wb%d" % slot))
            wv = wb[:, slot, :].rearrange("p (a n) -> p a n", a=8)
            dstT = qT if piece == 0 else kT
            qkbufs = [(qk_tm, bf("qk_tm")), (tbf2, bf("tbf2"))]

            def qk_tail(tt, dstT=dstT):
                qb, qB = qkbufs[tt % 2]
                b2 = bank()
                for c in range(4):
                    tr(psb[b2][:, c * 128:(c + 1) * 128], qb[:, c * 128:(c + 1) * 128], identb[:],
                       [qB, bf("identb")], [psB[b2]])
                act(lambda h, b2=b2, tt=tt, dstT=dstT: h.copy(
                    out=dstT[:, :, tt * 128:(tt + 1) * 128],
                    in_=psb[b2][:, 0:512].rearrange("p (c n) -> p c n", c=4)), [psB[b2]], [R1B])

            for tt in range(16):
                t4 = tt // 4
                b = bank()
                for dk in range(8):
                    mm(ps[b][:], hb[:, dk, tt * 128:(tt + 1) * 128], wv[:, dk, :], dk == 0, dk == 7,
                       [hbB[dk][t4], bf("wb%d" % slot)], [psB[b]])
                if tt > 0:
                    qk_tail(tt - 1)
                qb, qB = qkbufs[tt % 2]
                pv = ps[b][:].rearrange("p (h e) -> p h e", h=8)
                q1, q2 = pv[:, :, 0:32], pv[:, :, 32:64]
                cb = bass.AP(cosT, tt * 32, [[512, 128], [0, 8], [1, 32]])
                sbb = bass.AP(sinT, tt * 32, [[512, 128], [0, 8], [1, 32]])
                ta = tmpB[:, 0:256].rearrange("p (h e) -> p h e", h=8)
                tb_ = tmpB[:, 256:512].rearrange("p (h e) -> p h e", h=8)
                ov = qb[:].rearrange("p (h e) -> p h e", h=8)
                rdc = [psB[b], bf("rope"), bf("ropes")]
                dve(lambda h, q1=q1, cb=cb, ta=ta: h.tensor_tensor(out=ta, in0=q1, in1=cb, op=ALU.mult), rdc, [bf("tmpB")])
                dve(lambda h, q2=q2, sbb=sbb, tb_=tb_: h.tensor_tensor(out=tb_, in0=q2, in1=sbb, op=ALU.mult), rdc, [bf("tmpB")])
                dve(lambda h, ta=ta, tb_=tb_, ov=ov: h.tensor_tensor(out=ov[:, :, 0:32], in0=ta, in1=tb_, op=ALU.subtract),
                    [bf("tmpB")], [qB])
                dve(lambda h, q1=q1, sbb=sbb, ta=ta: h.tensor_tensor(out=ta, in0=q1, in1=sbb, op=ALU.mult), rdc, [bf("tmpB")])
                dve(lambda h, q2=q2, cb=cb, tb_=tb_: h.tensor_tensor(out=tb_, in0=q2, in1=cb, op=ALU.mult), rdc, [bf("tmpB")])
                dve(lambda h, ta=ta, tb_=tb_, ov=ov: h.tensor_tensor(out=ov[:, :, 32:64], in0=ta, in1=tb_, op=ALU.add),
                    [bf("tmpB")], [qB])
            qk_tail(15)
        load_w(wb[:, 0, :].rearrange("p (a n) -> p a n", a=8),
               w_in_d[l][:, 1024:1536].rearrange("(a p) n -> p a n", p=128), bf("wb0"))
        wv = wb[:, 0, :].rearrange("p (a n) -> p a n", a=8)
        for tt in range(16):
            t4 = tt // 4
            b = bank()
            for dk in range(8):
                mm(ps[b][:], hb[:, dk, tt * 128:(tt + 1) * 128], wv[:, dk, :], dk == 0, dk == 7,
                   [hbB[dk][t4], bf("wb0")], [psB[b]])
            vb, vB = (qk_tm, bf("qk_tm")) if tt % 2 == 0 else (tbf2, bf("tbf2"))
            act(lambda h, b=b, vb=vb: h.copy(out=vb[:], in_=ps[b][:]), [psB[b]], [vB])
            P.dma("sp", "v", lambda h, tt=tt, vb=vb: h.dma_start(out=vscr[tt * 128:(tt + 1) * 128, :], in_=vb[:]),
                  reads=[vB], writes=[bf("vscr")])
        load_w(wb[:, 0, :].rearrange("p (a n) -> p a n", a=8),
               w_in_d[l][:, 1536:2048].rearrange("(a p) n -> p a n", p=128), bf("wb0"))
        wv = wb[:, 0, :].rearrange("p (a n) -> p a n", a=8)
        for c in range(4):
            for t in range(4):
                b = bank()
                for dk in range(8):
                    mm(ps[b][:], wv[:, dk, c * 128:(c + 1) * 128], hb[:, dk, t * 512:(t + 1) * 512], dk == 0, dk == 7,
                       [hbB[dk][t], bf("wb0")], [psB[b]])
                act(lambda h, b=b, c=c, t=t: h.copy(out=uT[:, c, t * 512:(t + 1) * 512], in_=ps[b][:]), [psB[b]], [uTB[c][t]])

        if STAGE < 2:
            return
        def tokset(br, i):
            if br == 0:
                return slice(128 * i, 128 * i + 128)
            if br == 1:
                r, J = i // 4, i % 4
                return slice(512 * J + r, 512 * J + 512, 4)
            return slice(i, S, 16)

        def vslot(br, i):
            if br == 0:
                return i
            if br == 1:
                r, J = i // 4, i % 4
                return 4 * J + r
            return i

        subseqs = [
            (0, [list(range(16))]),
            (1, [[r * 4 + J for J in range(4)] for r in range(4)]),
            (2, [[r] for r in range(16)]),
        ]
        for hh in range(8):
            hs = slice((hh % 2) * 64, (hh % 2) * 64 + 64)
            cch = hh // 2
            vcol = vscr[:, 64 * hh:64 * hh + 64]
            srcs = [vcol.rearrange("(j p) e -> p j e", p=128),
                    vcol.rearrange("(J p r) e -> p J r e", J=4, r=4),
                    vcol.rearrange("(p r) e -> p r e", r=16)]
            items = []
            for (br, seqs) in subseqs:
                for seq in seqs:
                    for ii, i in enumerate(seq):
                        items.append((br, i, seq[ii + 1] if ii + 1 < len(seq) else None, ii > 0))
            st_pt = {}

            def stage_a(n):
                br, i, nxt, _ = items[n]
                ks = tokset(br, i)
                b = bank()
                mm(ps[b][:, 0:128], kT[hs, cch, ks], qT[hs, cch, ks], True, True, [R1B], [psB[b]], sig=nxt is None)
                if nxt is not None:
                    mm(ps[b][:, 128:256], kT[hs, cch, ks], qT[hs, cch, tokset(br, nxt)], True, True,
                       [R1B], [psB[b]], sig=True)
                w_ = 256 if nxt is not None else 128
                pt = PT[n % 3]
                ptB = bf("PT%d" % (n % 3))
                act(lambda h, b=b, pt=pt, w_=w_: h.activation(out=pt[:, 0:w_], in_=ps[b][:, 0:w_], func=AF.Exp, scale=0.125),
                    [psB[b]], [ptB])
                pool(lambda h, pt=pt, w_=w_: h.tensor_tensor(out=pt[:, 0:w_], in0=pt[:, 0:w_], in1=maskb[:, 0:w_], op=ALU.mult),
                     [ptB, bf("maskb")], [ptB])
                st_pt[n] = (pt, ptB)

            def stage_b(n):
                br, i, nxt, has_prev = items[n]
                if n == 0 or items[n - 1][0] != br:
                    if br == 1:
                        pairs = [(vh[:, 0, 4 * J:4 * J + 4, 0:64], srcs[1][:, J]) for J in range(4)]
                    else:
                        pairs = [(vh[:, 0, 8 * hf:8 * hf + 8, 0:64], srcs[br][:, 8 * hf:8 * hf + 8]) for hf in range(2)]
                    for (dst, src) in pairs:
                        P.dma("sp", "vh", lambda h, dst=dst, src=src: h.dma_start(out=dst, in_=src),
                              reads=[bf("vscr")], writes=[bf("vh0")])
                ks = tokset(br, i)
                pt, ptB = st_pt[n]
                bo = bank()
                if has_prev:
                    ppt, pptB = st_pt[n - 1]
                    mm(ps[bo][:, 0:128], vh[:, 0, vslot(br, items[n - 1][1]), :], ppt[:, 128:256], True, False,
                       [bf("vh0"), pptB], [psB[bo]], sig=False)
                mm(ps[bo][:, 0:128], vh[:, 0, vslot(br, i), :], pt[:, 0:128], not has_prev, True,
                   [bf("vh0"), ptB], [psB[bo]], sig=True)
                if br == 0:
                    dve(lambda h, bo=bo, ks=ks: h.tensor_copy(out=Oacc[:, ks], in_=ps[bo][:, 0:128]), [psB[bo]], [R2B])
                else:
                    dve(lambda h, bo=bo, ks=ks: h.tensor_tensor(out=Oacc[:, ks], in0=Oacc[:, ks], in1=ps[bo][:, 0:128], op=ALU.add),
                        [psB[bo], R2B], [R2B])

            stage_a(0)
            for n in range(len(items)):
                if n + 1 < len(items):
                    stage_a(n + 1)
                stage_b(n)
            dve(lambda h: h.tensor_copy(out=tmpA[0:64, :], in_=Oacc[64:128, :]), [R2B], [bf("tmpA")])
            dve(lambda h: h.reciprocal(out=tmpA[0:64, :], in_=tmpA[0:64, :]), [bf("tmpA")], [bf("tmpA")])
            wr = [hbB[cch][t] for t in range(4)]
            dve(lambda h, hs=hs, cch=cch: h.tensor_tensor(out=hb[hs, cch, :], in0=Oacc[0:64, :], in1=tmpA[0:64, :], op=ALU.mult),
                [R2B, bf("tmpA")], wr)
        rms_half(l, 0, "attn_gain")

        if STAGE < 3:
            return
        ssm_setup(l)
        T_ = bf("ssmtab")
        bank_pool[0] = [4, 5, 6, 7]
        for gp in range(16):
            j = gp // 4
            gl0 = (2 * gp) % 8
            for ri, srcB in enumerate([bbr, bbi]):
                dve(lambda h: h.memset(Zf[:, 0, :], 0.0), [], [bf("Zf")])
                dve(lambda h, srcB=srcB, gp=gp, gl0=gl0: h.tensor_copy(out=Zf[0:64, 0, 16 * gl0:16 * gl0 + 16], in_=srcB[0:64, gp, :]), [T_], [bf("Zf")])
                dve(lambda h, srcB=srcB, gp=gp, gl0=gl0: h.tensor_copy(out=Zf[64:128, 0, 16 * gl0 + 16:16 * gl0 + 32], in_=srcB[64:128, gp, :]), [T_], [bf("Zf")])
                b = bank()
                tr(ps[b][:, 0:128], Zf[:, 0, :], identf[:], [bf("Zf"), bf("identf")], [psB[b]])
                act(lambda h, b=b, ri=ri: h.copy(out=BbT[:, ri, :], in_=ps[b][:, 0:128]), [psB[b]], [bf("BbT")])
            dve(lambda h: h.memset(Zf[:], 0.0), [], [bf("Zf")])
            for ri in range(2):
                dve(lambda h, ri=ri, gl0=gl0, j=j: h.tensor_copy(out=Zf[0:64, ri, 16 * gl0:16 * gl0 + 16], in_=CT[0:64, ri, j, 16 * gl0:16 * gl0 + 16]), [T_], [bf("Zf")])
                dve(lambda h, ri=ri, gl0=gl0, j=j: h.tensor_copy(out=Zf[64:128, ri, 16 * gl0 + 16:16 * gl0 + 32], in_=CT[0:64, ri, j, 16 * gl0 + 16:16 * gl0 + 32]), [T_], [bf("Zf")])
            dve(lambda h: h.tensor_copy(out=Cm[:], in_=Zf[:]), [bf("Zf")], [bf("Cm")])
            for t in range(4):
                sl = slice(t * 512, (t + 1) * 512)
                for ri in range(2):
                    b = bank()
                    mm(ps[b][:], BbT[:, ri, :], uT[:, j, sl], True, True, [bf("BbT"), uTB[j][t]], [psB[b]])
                    if ri == 0:
                        act(lambda h, b=b, sl=sl: h.copy(out=X0[:, 0, sl], in_=ps[b][:]), [psB[b]], [R1B])
                    else:
                        act(lambda h, b=b, sl=sl: h.copy(out=X0[:, 1, sl], in_=ps[b][:]), [psB[b]], [R1B])
            src, dst = X0, X1
            for k in range(11):
                d = 1 << k
                ar = LPr[:, k, gp:gp + 1]
                ai = LPi[:, k, gp:gp + 1]
                an = LPn[:, k, gp:gp + 1]
                rdx = [R1B, T_]
                dve(lambda h, src=src, dst=dst, d=d: h.tensor_copy(out=dst[:, :, 0:d], in_=src[:, :, 0:d]), rdx, [R1B])
                dve(lambda h, src=src, dst=dst, d=d, ar=ar: h.scalar_tensor_tensor(
                    out=dst[:, 0, d:S], in0=src[:, 0, 0:S - d], scalar=ar, in1=src[:, 0, d:S], op0=ALU.mult, op1=ALU.add), rdx, [R1B])
                dve(lambda h, src=src, dst=dst, d=d, an=an: h.scalar_tensor_tensor(
                    out=dst[:, 0, d:S], in0=src[:, 1, 0:S - d], scalar=an, in1=dst[:, 0, d:S], op0=ALU.mult, op1=ALU.add), rdx, [R1B])
                dve(lambda h, src=src, dst=dst, d=d, ar=ar: h.scalar_tensor_tensor(
                    out=dst[:, 1, d:S], in0=src[:, 1, 0:S - d], scalar=ar, in1=src[:, 1, d:S], op0=ALU.mult, op1=ALU.add), rdx, [R1B])
                dve(lambda h, src=src, dst=dst, d=d, ai=ai: h.scalar_tensor_tensor(
                    out=dst[:, 1, d:S], in0=src[:, 0, 0:S - d], scalar=ai, in1=dst[:, 1, d:S], op0=ALU.mult, op1=ALU.add), rdx, [R1B])
                src, dst = dst, src
            act(lambda h, src=src: h.copy(out=XB[:], in_=src[:]), [R1B], [R2B])
            for t in range(4):
                sl = slice(t * 512, (t + 1) * 512)
                first_gp = (gp % 4 == 0)
                last_gp = (gp % 4 == 3)
                mm(ps[t][:], Cm[:, 0, :], XB[:, 0, sl], first_gp, False, [bf("Cm"), R2B], [psB[t]], sig=False)
                mm(ps[t][:], Cm[:, 1, :], XB[:, 1, sl], False, False, [bf("Cm"), R2B], [psB[t]], sig=not last_gp)
                if last_gp:
                    mm(ps[t][:], Dd[:, j, :], uT[:, j, sl], False, True, [T_, uTB[j][t]], [psB[t]], sig=True)
                    act(lambda h, t=t: h.activation(out=tmpB[:], in_=ps[t][:], func=AF.Square), [psB[t]], [bf("tmpB")])
                    dve(lambda h: h.tensor_scalar(out=tmpB[:], in0=tmpB[:], scalar1=0.044715, scalar2=1.0, op0=ALU.mult, op1=ALU.add),
                        [bf("tmpB")], [bf("tmpB")])
                    dve(lambda h, t=t: h.tensor_tensor(out=tmpB[:], in0=tmpB[:], in1=ps[t][:], op=ALU.mult), [bf("tmpB"), psB[t]], [bf("tmpB")])
                    act(lambda h: h.activation(out=tmpB[:], in_=tmpB[:], func=AF.Sigmoid, scale=1.5957691216057308), [bf("tmpB")], [bf("tmpB")])
                    dve(lambda h, t=t, sl=sl, j=j: h.tensor_tensor(out=uT[:, j, sl], in0=tmpB[:], in1=ps[t][:], op=ALU.mult),
                        [bf("tmpB"), psB[t]], [uTB[j][t]])
        bank_pool[0] = list(range(8))
        load_w(wb[:, 0, 0:2048].rearrange("p (a n) -> p a n", a=4),
               w_glu_d[l].rearrange("(a p) n -> p a n", p=128), bf("wb0"))
        wg = wb[:, 0, 0:2048].rearrange("p (a n) -> p a n", a=4)
        for co in range(4):
            bg = pcol("b_glu", l * 4 + co)
            for t in range(4):
                sl = slice(t * 512, (t + 1) * 512)
                b = bank()
                for ci in range(4):
                    mm(ps[b][:], wg[:, ci, co * 128:(co + 1) * 128], uT[:, ci, sl], ci == 0, ci == 3, [bf("wb0"), uTB[ci][t]], [psB[b]])
                act(lambda h, b=b, bg=bg: h.activation(out=tmpB[:], in_=ps[b][:], func=AF.Sigmoid, bias=bg, scale=1.0),
                    [psB[b], bf("prm")], [bf("tmpB")])
                dve(lambda h, co=co, sl=sl: h.tensor_tensor(out=hb[:, 4 + co, sl], in0=uT[:, co, sl], in1=tmpB[:], op=ALU.mult),
                    [bf("tmpB"), uTB[co][t]], [hbB[4 + co][t]])
        rms_half(l, 4, "ssm_gain")

        if dbg == "mixed" and s == 0 and l == 0:
            for c in range(8):
                dve(lambda h, c=c: h.tensor_copy(out=tmpA[:], in_=hb[:, c, :]), [hbB[c][t] for t in range(4)], [bf("tmpA")])
                P.dma("sp", "st", lambda h, c=c: h.dma_start(out=dbg_d[:, c, :], in_=tmpA[:]), reads=[bf("tmpA")])

        if STAGE < 4:
            return
        for co in range(8):
            if co % 4 == 0:
                half = co // 4
                load_w(wb[:, 0, :].rearrange("p (a n) -> p a n", a=8),
                       w_out_d[l][:, half * 512:(half + 1) * 512].rearrange("(a p) n -> p a n", p=128), bf("wb0"))
            wv = wb[:, 0, :].rearrange("p (a n) -> p a n", a=8)
            cl = co % 4
            bo_ = pcol("b_out", l * 8 + co)
            for t in range(4):
                sl = slice(t * 512, (t + 1) * 512)
                b = bank()
                for dk in range(8):
                    mm(ps[b][:], wv[:, dk, cl * 128:(cl + 1) * 128], hb[:, dk, sl], dk == 0, dk == 7,
                       [bf("wb0"), hbB[dk][t]], [psB[b]])
                act(lambda h, b=b, bo_=bo_: h.activation(out=tmpB[:], in_=ps[b][:], func=AF.Identity, bias=bo_, scale=1.0),
                    [psB[b], bf("prm")], [bf("tmpB")])
                dve(lambda h, co=co, sl=sl: h.scalar_tensor_tensor(out=h32[:, co, sl], in0=h32[:, co, sl], scalar=ALPHA, in1=tmpB[:],
                                                                 op0=ALU.mult, op1=ALU.add), [hB[co][t], bf("tmpB")], [hB[co][t]])
        P.barrier()
        layer_norm(l, "ln1_g", "ln1_b")
        if dbg == "ln1" and s == 0 and l == 0:
            for c in range(8):
                P.dma("sp", "st", lambda h, c=c: h.dma_start(out=dbg_d[:, c, :], in_=h32[:, c, :]), reads=[hB[c][t] for t in range(4)])

        if STAGE < 5:
            return
        P.barrier()
        W1e = [R1b[:, 0:4096].rearrange("p (c n) -> p c n", c=8), R1b[:, 4096:8192].rearrange("p (c n) -> p c n", c=8)]
        W2e = [R1b[:, 8192:12288].rearrange("p (c n) -> p c n", c=4), R1b[:, 12288:16384].rearrange("p (c n) -> p c n", c=4)]

        def load_pass(pp):
            bi = pp % 2
            load_w(W1e[bi], w_ff1_d[l][:, pp * 512:(pp + 1) * 512].rearrange("(a p) n -> p a n", p=128), bf("W1e%d" % bi))
            load_w(W2e[bi], w_ff2_d[l][pp * 512:(pp + 1) * 512, :].rearrange("(a p) n -> p a n", p=128), bf("W2e%d" % bi))

        load_pass(0)
        tile_ctr = 0
        for pp in range(8):
            if pp + 1 < 8:
                load_pass(pp + 1)
            bi = pp % 2
            for t in range(4):
                sl = slice(t * 512, (t + 1) * 512)
                ab = (tile_ctr % 2) * 4
                tile_ctr += 1
                for fc in range(4):
                    b = bank()
                    b1_ = pcol("b_ff1", l * 32 + pp * 4 + fc)
                    for dk in range(8):
                        mm(ps[b][:], W1e[bi][:, dk, fc * 128:(fc + 1) * 128], hb[:, dk, sl], dk == 0, dk == 7,
                           [bf("W1e%d" % bi), hbB[dk][t]], [psB[b]])
                    act(lambda h, b=b, b1_=b1_: h.activation(out=tbf2[:], in_=ps[b][:], func=AF.Relu, bias=b1_, scale=1.0),
                        [psB[b], bf("prm")], [bf("tbf2")])
                    dve(lambda h, fc=fc, ab=ab: h.tensor_tensor(out=aT[:, ab + fc, :], in0=tbf2[:], in1=tbf2[:], op=ALU.mult),
                        [bf("tbf2")], [bf("aT%d" % (ab + fc))])
                for co in range(8):
                    b = bank()
                    for fc in range(4):
                        mm(ps[b][:], W2e[bi][:, fc, co * 128:(co + 1) * 128], aT[:, ab + fc, :], fc == 0, fc == 3,
                           [bf("W2e%d" % bi), bf("aT%d" % (ab + fc))], [psB[b]])
                    if pp == 0:
                        b2_ = pcol("b_ff2", l * 8 + co)
                        act(lambda h, b=b, b2_=b2_: h.activation(out=tmpB[:], in_=ps[b][:], func=AF.Identity, bias=b2_, scale=1.0),
                            [psB[b], bf("prm")], [bf("tmpB")])
                        dve(lambda h, co=co, sl=sl: h.scalar_tensor_tensor(out=h32[:, co, sl], in0=h32[:, co, sl], scalar=ALPHA, in1=tmpB[:],
                                                                         op0=ALU.mult, op1=ALU.add), [hB[co][t], bf("tmpB")], [hB[co][t]])
                    else:
                        dve(lambda h, b=b, co=co, sl=sl: h.tensor_tensor(out=h32[:, co, sl], in0=h32[:, co, sl], in1=ps[b][:], op=ALU.add),
                            [hB[co][t], psB[b]], [hB[co][t]])
        P.barrier()
        layer_norm(l, "ln2_g", "ln2_b")
        P.barrier()
        P.new_epoch()

    for s in range(NSEQ):
        if "rope" not in KSKIP: P.dma("sp", "misc", lambda h, s=s: h.dma_start(out=posi[:], in_=pos_d[s].rearrange("(j p) -> p j", p=128)),
              writes=[bf("posi")])
        if "rope" not in KSKIP:
            dve(lambda h: h.tensor_copy(out=posf[:], in_=posi[:]), [bf("posi")], [bf("posf")])
        for tt in (range(16) if "rope" not in KSKIP else []):
            pc = posf[:, tt:tt + 1]
            dve(lambda h, tt=tt, pc=pc: h.tensor_scalar(out=cosT[:, tt, :], in0=invf[:], scalar1=pc, scalar2=None, op0=ALU.mult),
                [bf("posf"), bf("invf")], [bf("rope")])
        rr = [bf("rope")]
        if "rope" not in KSKIP:
            sin_of(sinT[:], cosT[:], 0.0, 512, rr, [bf("ropes")])
            sin_of(cosT[:], cosT[:], 0.5 * math.pi, 512, rr + [bf("ropes")], rr)
        for tt in range(int(os.environ.get("KXL", "16"))):
            t4 = tt // 4
            tq = 0 if os.environ.get("KXOFF") else tt
            P.dma("sp", "ld", lambda h, s=s, tt=tq: h.dma_start(out=xtm, in_=x_d[s, tt * 128:(tt + 1) * 128, :]),
                  writes=[bf("tmpA")])
            for g in range(2):
                b = bank()
                for c4 in range(4):
                    c = g * 4 + c4
                    tr(ps[b][:, c4 * 128:(c4 + 1) * 128], xtm[:, c * 128:(c + 1) * 128], identf[:], [bf("tmpA"), bf("identf")], [psB[b]])
                wr = [hB[g * 4 + c4][t4] for c4 in range(4)]
                wr2 = [hbB[g * 4 + c4][t4] for c4 in range(4)]
                pv = ps[b][:].rearrange("p (c n) -> p c n", c=4)
                if "h32copy" not in KSKIP:
                    dve(lambda h, pv=pv, g=g, tt=tt: h.tensor_copy(out=h32[:, g * 4:g * 4 + 4, tt * 128:(tt + 1) * 128], in_=pv), [psB[b]], wr)
                if "hbcopy" not in KSKIP:
                    act(lambda h, pv=pv, g=g, tt=tt: h.copy(out=hb[:, g * 4:g * 4 + 4, tt * 128:(tt + 1) * 128], in_=pv), [psB[b]], wr2)
        for l in range(NL):
            block(s, l, False)
        for tt in range(int(os.environ.get("KXS", "16"))):
            t4 = tt // 4
            for g in range(2):
                b = bank()
                for c4 in range(4):
                    c = g * 4 + c4
                    tr(ps[b][:, c4 * 128:(c4 + 1) * 128], h32[:, c, tt * 128:(tt + 1) * 128], identf[:], [hB[c][t4], bf("identf")], [psB[b]])
                dve(lambda h, b=b, g=g: h.tensor_copy(out=xtm[:, g * 512:(g + 1) * 512], in_=ps[b][:]), [psB[b]], [bf("tmpA")])
            P.dma("sp", "st", lambda h, s=s, tt=tt: h.dma_start(out=out_d[s, tt * 128:(tt + 1) * 128, :], in_=xtm),
                  reads=[bf("tmpA")])
        P.barrier()
    P.barrier()
    with nc.allow_non_contiguous_dma(reason="small strided param loads"), nc.Block() as blk:
        P.emit_all(blk)
    es.close()
    return nc, P


def consts():
    ident = np.eye(128, dtype=np.float32)
    k = np.arange(128)[:, None]
    q = np.arange(128)[None, :]
    mask = np.concatenate([(k <= q), (k >= q)], axis=1).astype(np.float32)
    invf = (10000.0 ** (-np.arange(32, dtype=np.float32) * 2.0 / 64.0)).astype(np.float32)
    invf = np.broadcast_to(invf[None, :], (128, 32)).copy()
    return {"c_ident": ident, "c_mask": mask, "c_invf": invf}


WNAMES = ["w_in", "attn_gain", "ssm_gain", "ssm_a_re", "ssm_a_im", "ssm_log_dt", "ssm_b_re", "ssm_b_im",
          "ssm_c_re", "ssm_c_im", "ssm_d", "w_glu", "b_glu", "w_out", "b_out", "ln1_g", "ln1_b",
          "w_ff1", "b_ff1", "w_ff2", "b_ff2", "ln2_g", "ln2_b"]

_CACHE = {}

LAUNCH_NL = 4
LAUNCH_NSEQ = 4


def kernel(**inputs):
    x = np.ascontiguousarray(inputs["x"], dtype=np.float32)
    pos = np.ascontiguousarray(inputs["positions"], dtype=np.int32)
    per_core = x.shape[0] // NCORES
    key = (LAUNCH_NL, LAUNCH_NSEQ)
    if key not in _CACHE:
        _CACHE[key] = build(LAUNCH_NL, LAUNCH_NSEQ)[0]
    nc = _CACHE[key]
    cs = consts()
    h = x
    for l0 in range(0, DEPTH, LAUNCH_NL):
        base = {n: np.ascontiguousarray(inputs[n][l0:l0 + LAUNCH_NL], dtype=np.float32) for n in WNAMES}
        base.update(cs)
        newh = np.empty_like(h)
        for s0 in range(0, per_core, LAUNCH_NSEQ):
            in_maps = []
            for c in range(NCORES):
                m = dict(base)
                lo = c * per_core + s0
                m["x"] = np.ascontiguousarray(h[lo:lo + LAUNCH_NSEQ])
                m["positions"] = np.ascontiguousarray(pos[lo:lo + LAUNCH_NSEQ])
                in_maps.append(m)
            res = run_bass_kernel_spmd(nc, in_maps, core_ids=list(range(NCORES)))
            for c in range(NCORES):
                lo = c * per_core + s0
                newh[lo:lo + LAUNCH_NSEQ] = res.results[c]["out"]
        h = newh
    return h
```

```python
import math
import os
STAGE = int(os.environ.get("KSTAGE", "9"))
KSKIP = set(os.environ.get("KSKIP", "").split(","))
from contextlib import ExitStack
import numpy as np
import concourse.bass as bass
import concourse.mybir as mybir
from concourse.bass_utils import run_bass_kernel_spmd

F32 = mybir.dt.float32
BF16 = mybir.dt.bfloat16
I32 = mybir.dt.int32
ALU = mybir.AluOpType
AF = mybir.ActivationFunctionType
ENGS = ("pe", "act", "dve", "pool", "sp")

D = 1024
S = 2048
DEPTH = 4
NCORES = 8
ALPHA = (2.0 * DEPTH) ** 0.25
LN_EPS = 1e-5
RMS_EPS = 1e-6
TWO_PI = 2.0 * math.pi


class Buf:
    __slots__ = ("name", "w", "r", "excl")

    def __init__(self, name, excl=False):
        self.name = name
        self.w = None
        self.r = {}
        self.excl = excl


class Prog:
    def __init__(self, nc, es):
        self.nc = nc
        self.es = es
        self.q = {e: [] for e in ENGS}
        self.sems = {}
        self.cnt = {}
        self.cur = {}
        self.dead = set()
        self.epoch = -1
        self.waited = {e: {} for e in ENGS}
        self.dma_cnt = {}
        self.dma_keys = []
        self.dma_rr = 0
        self.n_inst = 0
        self.new_epoch()

    def new_epoch(self):
        for k in self.cur.values():
            self.dead.add(k)
        self.epoch += 1
        for e in ENGS:
            k = "%s#%d" % (e, self.epoch)
            self.cur[e] = k
            self.cnt[k] = 0
            self.sems[k] = self.es.enter_context(self.nc.semaphore("s_%s_%d" % (e, self.epoch)))

    def _need(self, e, deps):
        out = []
        ce = self.cur[e]
        for (k, v) in deps:
            if k in self.dead:
                continue
            if k == ce and (e == "pe" or v > self.cnt[ce]):
                continue
            if self.waited[e].get(k, 0) >= v:
                continue
            self.waited[e][k] = v
            out.append((k, v))
        return out

    def _deps(self, reads, writes):
        deps = set()
        for b in reads:
            if b.w is not None:
                deps.add(b.w)
            if b.excl:
                for kv in b.r.items():
                    deps.add(kv)
        for b in writes:
            if b.w is not None:
                deps.add(b.w)
            for kv in b.r.items():
                deps.add(kv)
        return deps

    def op(self, e, fn, reads=(), writes=(), sig=True):
        waits = self._need(e, self._deps(reads, writes))
        ce = self.cur[e]
        if sig:
            self.cnt[ce] += 1
        tick = self.cnt[ce] if sig else self.cnt[ce] + 1
        sem = self.sems[ce]
        sems = self.sems

        def emit(h):
            for (k, v) in waits:
                h.wait_ge(sems[k], v)
            ins = fn(h)
            if sig:
                ins.then_inc(sem, 1)
        self.q[e].append(emit)
        self.n_inst += 1
        for b in reads:
            if b.excl:
                b.w = (ce, tick)
                b.r = {}
            elif b.r.get(ce, 0) < tick:
                b.r[ce] = tick
        for b in writes:
            b.w = (ce, tick)
            b.r = {}
        return tick

    def dma(self, qe, semkey, fn, reads=(), writes=()):
        semkey = self.dma_keys[self.dma_rr % len(self.dma_keys)]
        self.dma_rr += 1
        deps = self._deps(reads, writes)
        prev = self.dma_cnt.get(semkey, 0)
        if prev > 0:
            deps.add((semkey, prev))
        waits = self._need(qe, deps)
        self.dma_cnt[semkey] = prev + 16
        val = self.dma_cnt[semkey]
        sems = self.sems
        sem = sems[semkey]

        def emit(h):
            for (k, v) in waits:
                h.wait_ge(sems[k], v)
            fn(h).then_inc(sem, 16)
        self.q[qe].append(emit)
        self.n_inst += 1
        for b in reads:
            if b.r.get(semkey, 0) < val:
                b.r[semkey] = val
        for b in writes:
            b.w = (semkey, val)
            b.r = {}
        return val

    def wait_all(self, e):
        deps = [(self.cur[k], self.cnt[self.cur[k]]) for k in ENGS if self.cnt[self.cur[k]] > 0]
        deps += list(self.dma_cnt.items())
        waits = self._need(e, deps)
        sems = self.sems

        def emit(h):
            for (k, v) in waits:
                h.wait_ge(sems[k], v)
        self.q[e].append(emit)

    def barrier(self):
        for e in ENGS:
            self.wait_all(e)

    def emit_all(self, block):
        prog = self

        @block.tensor
        def _(h):
            for f in prog.q["pe"]:
                f(h)

        @block.scalar
        def _(h):
            for f in prog.q["act"]:
                f(h)

        @block.vector
        def _(h):
            for f in prog.q["dve"]:
                f(h)

        @block.gpsimd
        def _(h):
            for f in prog.q["pool"]:
                f(h)

        @block.sync
        def _(h):
            for f in prog.q["sp"]:
                f(h)


def build(NL, NSEQ, dbg=None):
    nc = bass.Bass("TRN2", target_bir_lowering=False)
    es = ExitStack()

    MINIO = os.environ.get("KMINIO", "")

    def din(name, shape, dt=F32):
        if MINIO and name not in ("x", "positions", "c_ident", "c_mask", "c_invf"):
            return None
        return nc.dram_tensor(name, list(shape), dt, kind="ExternalInput").ap()

    x_d = din("x", [NSEQ, S, D])
    pos_d = din("positions", [NSEQ, S], I32)
    w_in_d = din("w_in", [NL, D, 2048])
    attn_gain_d = din("attn_gain", [NL, 512])
    ssm_gain_d = din("ssm_gain", [NL, 512])
    a_re_d = din("ssm_a_re", [NL, 32, 64])
    a_im_d = din("ssm_a_im", [NL, 32, 64])
    log_dt_d = din("ssm_log_dt", [NL, 32])
    b_re_d = din("ssm_b_re", [NL, 32, 64, 16])
    b_im_d = din("ssm_b_im", [NL, 32, 64, 16])
    c_re_d = din("ssm_c_re", [NL, 32, 16, 64])
    c_im_d = din("ssm_c_im", [NL, 32, 16, 64])
    d_d = din("ssm_d", [NL, 32, 16])
    w_glu_d = din("w_glu", [NL, 512, 512])
    b_glu_d = din("b_glu", [NL, 512])
    w_out_d = din("w_out", [NL, D, D])
    b_out_d = din("b_out", [NL, D])
    ln1_g_d = din("ln1_g", [NL, D])
    ln1_b_d = din("ln1_b", [NL, D])
    w_ff1_d = din("w_ff1", [NL, D, 4096])
    b_ff1_d = din("b_ff1", [NL, 4096])
    w_ff2_d = din("w_ff2", [NL, 4096, D])
    b_ff2_d = din("b_ff2", [NL, D])
    ln2_g_d = din("ln2_g", [NL, D])
    ln2_b_d = din("ln2_b", [NL, D])
    ident_d = din("c_ident", [128, 128])
    mask_d = din("c_mask", [128, 256])
    invf_d = din("c_invf", [128, 32])
    out_d = nc.dram_tensor("out", [NSEQ, S, D], F32, kind="ExternalOutput").ap()
    vscr = None if MINIO else nc.dram_tensor("vscr", [S, 512], BF16, kind="ExternalOutput").ap()
    dbg_d = None
    if dbg:
        dbg_d = nc.dram_tensor("dbg", [128, 8, S], F32, kind="ExternalOutput").ap()

    def sb(name, shape, dt):
        return es.enter_context(nc.sbuf_tensor(name, list(shape), dt))

    P = Prog(nc, es)
    for i in range(int(os.environ.get("KNDMA", "8"))):
        k = "dma%d" % i
        P.sems[k] = es.enter_context(nc.semaphore("d_" + k))
        P.dma_keys.append(k)

    h32 = sb("h32", [128, 8, S], F32)
    hb = sb("hb", [128, 8, S], BF16)
    R1 = sb("R1", [128, 8192], F32)
    R1b = R1.bitcast(BF16)
    qT = R1b[:, 0:8192].rearrange("p (c n) -> p c n", c=4)
    kT = R1b[:, 8192:16384].rearrange("p (c n) -> p c n", c=4)
    X0 = R1[:, 0:4096].rearrange("p (c n) -> p c n", c=2)
    X1 = R1[:, 4096:8192].rearrange("p (c n) -> p c n", c=2)
    W1q = R1b[:, 0:8192].rearrange("p (c n) -> p c n", c=8)
    W2q = R1b[:, 8192:16384].rearrange("p (c n) -> p c n", c=8)
    uT = sb("uT", [128, 4, S], BF16)
    R2 = sb("R2", [128, 2048], F32)
    Oacc = R2
    R2b = R2.bitcast(BF16)
    XB = R2b[:, 0:4096].rearrange("p (c n) -> p c n", c=2)
    aT = R2b[:, 0:4096].rearrange("p (c n) -> p c n", c=8)
    vh = sb("vh", [128, 1, 16, 128], BF16)
    PT = [sb("PT%d" % i, [128, 256], BF16) for i in range(3)]
    tmpA = sb("tmpA", [128, 2048], F32)
    tmpB = sb("tmpB", [128, 512], F32)
    tmpC = sb("tmpC", [128, 512], F32)
    tbf2 = sb("tbf2", [128, 512], BF16)
    NWST = int(os.environ.get("KNWST", "2"))
    wsts = [sb("wst%d" % i, [128, 1024], F32) for i in range(NWST)]
    wst_ctr = [0]
    wb = sb("wb", [128, 1, 4096], BF16)
    xtm = tmpA[:, 0:1024]
    qk_tm = sb("qk_tm", [128, 512], BF16)
    v_tm = qk_tm
    identf = sb("identf", [128, 128], F32)
    identb = sb("identb", [128, 128], BF16)
    onesb = sb("onesb", [128, 128], BF16)
    maskb = sb("maskb", [128, 256], BF16)
    invf = sb("invf", [128, 32], F32)
    posf = sb("posf", [128, 16], F32)
    posi = sb("posi", [128, 16], I32)
    cosT = sb("cosT", [128, 16, 32], F32)
    sinT = sb("sinT", [128, 16, 32], F32)
    prm = sb("prm", [128, 512], F32)
    prm_stage = sb("prm_stage", [128, 128], F32)
    sA = sb("sA", [128, 16, 8], F32)
    LPr = sb("LPr", [128, 11, 16], F32)
    LPi = sb("LPi", [128, 11, 16], F32)
    LPn = sb("LPn", [128, 11, 16], F32)
    Bre = sb("Bre", [128, 16, 16], F32)
    Bim = sb("Bim", [128, 16, 16], F32)
    bbr = sb("bbr", [128, 16, 16], F32)
    bbi = sb("bbi", [128, 16, 16], F32)
    CT = sb("CT", [128, 2, 4, 128], BF16)
    Zf = sb("Zf", [128, 2, 128], F32)
    BbT = sb("BbT", [128, 2, 128], BF16)
    Cm = sb("Cm", [128, 2, 128], BF16)
    Dd = sb("Dd", [128, 4, 128], BF16)
    cstage = sb("cstage", [128, 64], F32)

    ps = [es.enter_context(nc.psum_tensor("ps%d" % i, [128, 512], F32)) for i in range(8)]
    psb = [p.bitcast(BF16) for p in ps]
    psB = [Buf("ps%d" % i, excl=True) for i in range(8)]
    bank_ctr = [0]
    bank_pool = [list(range(8))]

    def bank():
        pool = bank_pool[0]
        i = pool[bank_ctr[0] % len(pool)]
        bank_ctr[0] += 1
        return i

    B = {}

    def bf(name):
        if name not in B:
            B[name] = Buf(name)
        return B[name]

    hB = [[bf("h%d_%d" % (c, t)) for t in range(4)] for c in range(8)]
    hbB = [[bf("hb%d_%d" % (c, t)) for t in range(4)] for c in range(8)]
    R1B = bf("R1")
    uTB = [[bf("uT%d_%d" % (c, t)) for t in range(4)] for c in range(4)]
    R2B = bf("R2")

    def allhb():
        return [b for row in hbB for b in row]

    def allh():
        return [b for row in hB for b in row]

    def alluT():
        return [b for row in uTB for b in row]

    def dve(fn, reads, writes):
        return P.op("dve", fn, reads, writes)

    def act(fn, reads, writes):
        return P.op("act", fn, reads, writes)

    def pool(fn, reads, writes):
        return P.op("pool", fn, reads, writes)

    def mm(out, lhsT, rhs, start, stop, reads, writes, sig=None):
        if sig is None:
            sig = stop
        return P.op("pe", lambda h: h.matmul(out, lhsT=lhsT, rhs=rhs, start=start, stop=stop),
                    reads, writes, sig=sig)

    def tr(out, in_, ident, reads, writes):
        if os.environ.get("KTR", "") == "mm" and in_.dtype == F32:
            return P.op("pe", lambda h: h.matmul(out, lhsT=in_, rhs=ident, start=True, stop=True), reads, writes)
        return P.op("pe", lambda h: h.transpose(out=out, in_=in_, identity=ident), reads, writes)

    def load_w(dst, src, dstB, nparts=128):
        a, n = dst.shape[1], dst.shape[2]
        per = max(1, 1024 // n)
        for a0 in range(0, a, per):
            a1 = min(a, a0 + per)
            k = wst_ctr[0] % NWST
            wst_ctr[0] += 1
            wbuf = bf("wst%d" % k)
            st = wsts[k][:, 0:(a1 - a0) * n].rearrange("p (a n) -> p a n", a=a1 - a0)
            P.dma("sp", "w", lambda h, st=st, a0=a0, a1=a1: h.dma_start(out=st, in_=src[:, a0:a1, :]),
                  writes=[wbuf])
            pool(lambda h, st=st, a0=a0, a1=a1: h.tensor_copy(out=dst[:, a0:a1, :], in_=st),
                 [wbuf], [dstB])

    P.dma("sp", "misc", lambda h: h.dma_start(out=identf[:], in_=ident_d), writes=[bf("identf")])
    P.dma("sp", "misc", lambda h: h.dma_start(out=tmpB[:, 0:256], in_=mask_d), writes=[bf("tmpB")])
    P.dma("sp", "misc", lambda h: h.dma_start(out=invf[:], in_=invf_d), writes=[bf("invf")])
    dve(lambda h: h.tensor_copy(out=identb[:], in_=identf[:]), [bf("identf")], [bf("identb")])
    dve(lambda h: h.tensor_copy(out=maskb[:], in_=tmpB[:, 0:256]), [bf("tmpB")], [bf("maskb")])
    dve(lambda h: h.memset(onesb[:], 1.0), [], [bf("onesb")])
    if "vh" not in KSKIP:
        dve(lambda h: h.memset(vh[:, 0, :, :], 1.0), [], [bf("vh0")])

    prm_cols = {}
    col = [0]

    def load_T(name, src2d, R):
        if "prm" in KSKIP:
            prm_cols[name] = col[0]; col[0] += R
            return
        c0 = col[0]
        col[0] += R
        prm_cols[name] = c0
        P.dma("sp", "misc", lambda h: h.dma_start(out=prm_stage[0:R, :], in_=src2d), writes=[bf("prm_stage")])
        b = bank()
        tr(ps[b][:, 0:R], prm_stage[0:R, :], identf[0:R, 0:R], [bf("prm_stage"), bf("identf")], [psB[b]])
        dve(lambda h: h.tensor_copy(out=prm[:, c0:c0 + R], in_=ps[b][:, 0:R]), [psB[b]], [bf("prm")])
        return c0

    if not MINIO:
        load_T("b_ff1", b_ff1_d.rearrange("l (c p) -> (l c) p", p=128), NL * 32)
        for nm, ap_ in [("b_out", b_out_d), ("ln1_g", ln1_g_d), ("ln1_b", ln1_b_d), ("b_ff2", b_ff2_d),
                        ("ln2_g", ln2_g_d), ("ln2_b", ln2_b_d)]:
            load_T(nm, ap_.rearrange("l (c p) -> (l c) p", p=128), NL * 8)
        for nm, ap_ in [("attn_gain", attn_gain_d), ("ssm_gain", ssm_gain_d), ("b_glu", b_glu_d)]:
            load_T(nm, ap_.rearrange("l (c p) -> (l c) p", p=128), NL * 4)
        load_T("ssm_d", d_d.rearrange("l (c g) n -> (l c) (g n)", g=8), NL * 4)
        for nm, ap_ in [("a_re", a_re_d), ("a_im", a_im_d)]:
            load_T(nm, ap_.rearrange("l (gp two) p -> (l gp) (two p)", two=2), NL * 16)
    assert col[0] <= 512

    def pcol(name, idx):
        c = prm_cols[name] + idx
        return prm[:, c:c + 1]

    def sin_of(dst, src, add, n, rd, wr):
        ti = tmpB.bitcast(I32)[:, 0:n]
        tf = tmpC[:, 0:n]
        tg = tmpB[:, 0:n] if False else None
        if len(dst.shape) == 3:
            a_, b_ = dst.shape[1], dst.shape[2]
            ti = ti.rearrange("p (a b) -> p a b", a=a_)
            tf = tf.rearrange("p (a b) -> p a b", a=a_)
        T = [bf("tmpB"), bf("tmpC")]
        dve(lambda h: h.tensor_scalar(out=dst, in0=src, scalar1=float(add), scalar2=None, op0=ALU.add), rd, wr)
        dve(lambda h: h.tensor_scalar(out=tf, in0=dst, scalar1=1.0 / TWO_PI, scalar2=None, op0=ALU.mult), wr, T)
        dve(lambda h: h.tensor_copy(out=ti, in_=tf), T, T)
        dve(lambda h: h.tensor_copy(out=tf, in_=ti), T, T)
        dve(lambda h: h.scalar_tensor_tensor(out=dst, in0=tf, scalar=-TWO_PI, in1=dst, op0=ALU.mult, op1=ALU.add), T + wr, wr)
        dve(lambda h: h.tensor_scalar(out=tf, in0=dst, scalar1=-math.pi, scalar2=1e30, op0=ALU.add, op1=ALU.mult), wr, T)
        dve(lambda h: h.tensor_scalar(out=tf, in0=tf, scalar1=0.0, scalar2=1.0, op0=ALU.max, op1=ALU.min), T, T)
        dve(lambda h: h.scalar_tensor_tensor(out=dst, in0=tf, scalar=-TWO_PI, in1=dst, op0=ALU.mult, op1=ALU.add), T + wr, wr)
        act(lambda h: h.activation(out=dst, in_=dst, func=AF.Sin), wr, wr)

    def ssm_setup(l):
        T_ = bf("ssmtab")
        for gpar in range(2):
            src = bass.AP(log_dt_d.tensor, l * 32 + gpar, [[0, 64], [2, 16], [1, 1]])
            P.dma("sp", "misc", lambda h, src=src, gpar=gpar: h.dma_start(
                out=sA[gpar * 64:(gpar + 1) * 64, :, 0:1], in_=src), writes=[T_])
        are = prm[:, prm_cols["a_re"] + l * 16: prm_cols["a_re"] + l * 16 + 16]
        aim = prm[:, prm_cols["a_im"] + l * 16: prm_cols["a_im"] + l * 16 + 16]
        dt_ = sA[:, :, 0]
        mag = sA[:, :, 1]
        ang = sA[:, :, 2]
        t3 = sA[:, :, 3]
        t4 = sA[:, :, 4]
        t5 = sA[:, :, 5]
        t6 = sA[:, :, 6]
        t7 = sA[:, :, 7]
        rd = [T_, bf("prm")]
        act(lambda h: h.activation(out=dt_, in_=dt_, func=AF.Exp), rd, [T_])
        dve(lambda h: h.tensor_tensor(out=mag, in0=are, in1=dt_, op=ALU.mult), rd, [T_])
        act(lambda h: h.activation(out=mag, in_=mag, func=AF.Exp), rd, [T_])
        dve(lambda h: h.tensor_tensor(out=ang, in0=aim, in1=dt_, op=ALU.mult), rd, [T_])
        sin_of(t3, ang, 0.0, 16, rd, [T_])
        sin_of(t4, ang, 0.5 * math.pi, 16, rd, [T_])
        lr = LPr[:, 0, :]
        li = LPi[:, 0, :]
        dve(lambda h: h.tensor_tensor(out=lr, in0=mag, in1=t4, op=ALU.mult), rd, [T_])
        dve(lambda h: h.tensor_tensor(out=li, in0=mag, in1=t3, op=ALU.mult), rd, [T_])
        for k in range(1, 11):
            pr, pi_, nr_, ni_ = LPr[:, k - 1, :], LPi[:, k - 1, :], LPr[:, k, :], LPi[:, k, :]
            dve(lambda h, pr=pr, pi_=pi_: h.tensor_tensor(out=t5, in0=pr, in1=pr, op=ALU.mult), rd, [T_])
            dve(lambda h, pr=pr, pi_=pi_: h.tensor_tensor(out=t6, in0=pi_, in1=pi_, op=ALU.mult), rd, [T_])
            dve(lambda h, nr_=nr_: h.tensor_tensor(out=nr_, in0=t5, in1=t6, op=ALU.subtract), rd, [T_])
            dve(lambda h, pr=pr, pi_=pi_: h.tensor_tensor(out=t5, in0=pr, in1=pi_, op=ALU.mult), rd, [T_])
            dve(lambda h, ni_=ni_: h.tensor_scalar(out=ni_, in0=t5, scalar1=2.0, scalar2=None, op0=ALU.mult), rd, [T_])
        dve(lambda h: h.tensor_scalar(out=LPn[:], in0=LPi[:], scalar1=-1.0, scalar2=None, op0=ALU.mult), rd, [T_])
        dve(lambda h: h.tensor_scalar(out=t3, in0=lr, scalar1=-1.0, scalar2=None, op0=ALU.add), rd, [T_])
        dve(lambda h: h.tensor_tensor(out=t4, in0=are, in1=are, op=ALU.mult), rd, [T_])
        dve(lambda h: h.tensor_tensor(out=t5, in0=aim, in1=aim, op=ALU.mult), rd, [T_])
        dve(lambda h: h.tensor_tensor(out=t4, in0=t4, in1=t5, op=ALU.add), rd, [T_])
        dve(lambda h: h.reciprocal(out=t4, in_=t4), rd, [T_])
        dve(lambda h: h.tensor_tensor(out=t5, in0=t3, in1=are, op=ALU.mult), rd, [T_])
        dve(lambda h: h.tensor_tensor(out=t6, in0=li, in1=aim, op=ALU.mult), rd, [T_])
        dve(lambda h: h.tensor_tensor(out=t5, in0=t5, in1=t6, op=ALU.add), rd, [T_])
        dve(lambda h: h.tensor_tensor(out=t5, in0=t5, in1=t4, op=ALU.mult), rd, [T_])
        dve(lambda h: h.tensor_tensor(out=t6, in0=li, in1=are, op=ALU.mult), rd, [T_])
        dve(lambda h: h.tensor_tensor(out=t7, in0=t3, in1=aim, op=ALU.mult), rd, [T_])
        dve(lambda h: h.tensor_tensor(out=t6, in0=t6, in1=t7, op=ALU.subtract), rd, [T_])
        dve(lambda h: h.tensor_tensor(out=t6, in0=t6, in1=t4, op=ALU.mult), rd, [T_])
        for (dst, src_d) in [(Bre, b_re_d), (Bim, b_im_d)]:
            src = src_d[l].rearrange("(gp two) p n -> (two p) gp n", two=2)
            P.dma("sp", "misc", lambda h, dst=dst, src=src: h.dma_start(out=dst[:], in_=src), writes=[T_])
        crb = bass.AP(sA, 5, [[128, 128], [8, 16], [0, 16]])
        cib = bass.AP(sA, 6, [[128, 128], [8, 16], [0, 16]])
        dve(lambda h: h.tensor_tensor(out=bbr[:], in0=Bre[:], in1=crb, op=ALU.mult), rd, [T_])
        dve(lambda h: h.tensor_tensor(out=bbi[:], in0=Bim[:], in1=cib, op=ALU.mult), rd, [T_])
        dve(lambda h: h.tensor_tensor(out=bbr[:], in0=bbr[:], in1=bbi[:], op=ALU.subtract), rd, [T_])
        dve(lambda h: h.tensor_tensor(out=bbi[:], in0=Bim[:], in1=crb, op=ALU.mult), rd, [T_])
        dve(lambda h: h.tensor_tensor(out=Bre[:], in0=Bre[:], in1=cib, op=ALU.mult), rd, [T_])
        dve(lambda h: h.tensor_tensor(out=bbi[:], in0=bbi[:], in1=Bre[:], op=ALU.add), rd, [T_])
        for ri, src_d in enumerate([c_re_d, c_im_d]):
            for j in range(4):
                src = src_d[l, 8 * j:8 * j + 8].rearrange("g n p -> (g n) p")
                P.dma("sp", "misc", lambda h, src=src: h.dma_start(out=cstage[:], in_=src), writes=[bf("cstage")])
                b = bank()
                tr(ps[b][0:64, 0:128], cstage[:, :], identf[:], [bf("cstage"), bf("identf")], [psB[b]])
                if ri == 0:
                    dve(lambda h, b=b, j=j: h.tensor_copy(out=CT[0:64, 0, j, :], in_=ps[b][0:64, 0:128]), [psB[b]], [T_])
                else:
                    dve(lambda h, b=b, j=j: h.tensor_scalar(out=CT[0:64, 1, j, :], in0=ps[b][0:64, 0:128],
                                                           scalar1=-1.0, scalar2=None, op0=ALU.mult), [psB[b]], [T_])
        for j in range(4):
            dc = pcol("ssm_d", l * 4 + j)
            dve(lambda h, j=j, dc=dc: h.tensor_scalar(out=Dd[:, j, :], in0=identf[:], scalar1=dc, scalar2=None,
                                                    op0=ALU.mult), [bf("identf"), bf("prm")], [T_])

    def layer_norm(l, gname, bname):
        for t in range(4):
            sl = slice(t * 512, (t + 1) * 512)
            b1 = bank()
            b2 = bank()
            for c in range(8):
                act(lambda h, c=c, sl=sl: h.copy(out=tbf2[:], in_=h32[:, c, sl]), [hB[c][t]], [bf("tbf2")])
                mm(ps[b1][:], onesb[:], tbf2[:], c == 0, c == 7, [bf("tbf2"), bf("onesb")], [psB[b1]], sig=True)
                act(lambda h, c=c, sl=sl: h.activation(out=qk_tm[:], in_=h32[:, c, sl], func=AF.Square),
                    [hB[c][t]], [bf("qk_tm")])
                mm(ps[b2][:], onesb[:], qk_tm[:], c == 0, c == 7, [bf("qk_tm"), bf("onesb")], [psB[b2]], sig=True)
            act(lambda h, b1=b1: h.activation(out=tmpB[:], in_=ps[b1][:], func=AF.Identity, scale=1.0 / D), [psB[b1]], [bf("tmpB")])
            dve(lambda h: h.tensor_tensor(out=tmpC[:], in0=tmpB[:], in1=tmpB[:], op=ALU.mult), [bf("tmpB")], [bf("tmpC")])
            dve(lambda h, b2=b2: h.scalar_tensor_tensor(out=tmpC[:], in0=ps[b2][:], scalar=1.0 / D, in1=tmpC[:],
                                                 op0=ALU.mult, op1=ALU.subtract), [psB[b2], bf("tmpC")], [bf("tmpC")])
            act(lambda h: h.activation(out=tmpC[:], in_=tmpC[:], func=AF.Sqrt, bias=LN_EPS, scale=1.0), [bf("tmpC")], [bf("tmpC")])
            dve(lambda h: h.reciprocal(out=tmpC[:], in_=tmpC[:]), [bf("tmpC")], [bf("tmpC")])
            for c in range(8):
                g_ = pcol(gname, l * 8 + c)
                b_ = pcol(bname, l * 8 + c)
                dve(lambda h, c=c, sl=sl: h.tensor_tensor(out=h32[:, c, sl], in0=h32[:, c, sl], in1=tmpB[:], op=ALU.subtract),
                    [hB[c][t], bf("tmpB")], [hB[c][t]])
                dve(lambda h, c=c, sl=sl: h.tensor_tensor(out=h32[:, c, sl], in0=h32[:, c, sl], in1=tmpC[:], op=ALU.mult),
                    [hB[c][t], bf("tmpC")], [hB[c][t]])
                dve(lambda h, c=c, g_=g_, b_=b_, sl=sl: h.tensor_scalar(out=h32[:, c, sl], in0=h32[:, c, sl], scalar1=g_, scalar2=b_,
                                                                  op0=ALU.mult, op1=ALU.add), [hB[c][t], bf("prm")], [hB[c][t]])
                act(lambda h, c=c, sl=sl: h.copy(out=hb[:, c, sl], in_=h32[:, c, sl]), [hB[c][t]], [hbB[c][t]])

    def rms_half(l, c0, gname):
        bs = [bank() for _ in range(4)]
        for ci in range(4):
            c = c0 + ci
            for t in range(4):
                sl = slice(t * 512, (t + 1) * 512)
                dve(lambda h, c=c, sl=sl: h.tensor_tensor(out=tbf2[:], in0=hb[:, c, sl], in1=hb[:, c, sl], op=ALU.mult),
                    [hbB[c][t]], [bf("tbf2")])
                mm(ps[bs[t]][:], onesb[:], tbf2[:], ci == 0, ci == 3, [bf("tbf2"), bf("onesb")], [psB[bs[t]]], sig=True)
        for t in range(4):
            act(lambda h, t=t: h.activation(out=tmpA[:, t * 512:(t + 1) * 512], in_=ps[bs[t]][:], func=AF.Sqrt,
                                            bias=RMS_EPS, scale=1.0 / 512), [psB[bs[t]]], [bf("tmpA")])
        dve(lambda h: h.reciprocal(out=tmpA[:], in_=tmpA[:]), [bf("tmpA")], [bf("tmpA")])
        for ci in range(4):
            c = c0 + ci
            g_ = pcol(gname, l * 4 + ci)
            rd = [hbB[c][t] for t in range(4)]
            dve(lambda h, c=c, g_=g_: h.scalar_tensor_tensor(out=hb[:, c, :], in0=hb[:, c, :], scalar=g_, in1=tmpA[:],
                                                             op0=ALU.mult, op1=ALU.mult), rd + [bf("tmpA"), bf("prm")], rd)

    def block(s, l, first):
        if STAGE < 1:
            return
        for piece in range(2):
            slot = 0
            load_w(wb[:, slot, :].rearrange("p (a n) -> p a n", a=8),
                   w_in_d[l][:, piece * 512:(piece + 1) * 512].rearrange("(a p) n -> p a n", p=128), bf("wb%d" % slot))
            wv = wb[:, slot, :].rearrange("p (a n) -> p a n", a=8)
            dstT = qT if piece == 0 else kT
            qkbufs = [(qk_tm, bf("qk_tm")), (tbf2, bf("tbf2"))]

            def qk_tail(tt, dstT=dstT):
                qb, qB = qkbufs[tt % 2]
                b2 = bank()
                for c in range(4):
                    tr(psb[b2][:, c * 128:(c + 1) * 128], qb[:, c * 128:(c + 1) * 128], identb[:],
                       [qB, bf("identb")], [psB[b2]])
                act(lambda h, b2=b2, tt=tt, dstT=dstT: h.copy(
                    out=dstT[:, :, tt * 128:(tt + 1) * 128],
                    in_=psb[b2][:, 0:512].rearrange("p (c n) -> p c n", c=4)), [psB[b2]], [R1B])

            for tt in range(16):
                t4 = tt // 4
                b = bank()
                for dk in range(8):
                    mm(ps[b][:], hb[:, dk, tt * 128:(tt + 1) * 128], wv[:, dk, :], dk == 0, dk == 7,
                       [hbB[dk][t4], bf("wb%d" % slot)], [psB[b]])
                if tt > 0:
                    qk_tail(tt - 1)
                qb, qB = qkbufs[tt % 2]
                pv = ps[b][:].rearrange("p (h e) -> p h e", h=8)
                q1, q2 = pv[:, :, 0:32], pv[:, :, 32:64]
                cb = bass.AP(cosT, tt * 32, [[512, 128], [0, 8], [1, 32]])
                sbb = bass.AP(sinT, tt * 32, [[512, 128], [0, 8], [1, 32]])
                ta = tmpB[:, 0:256].rearrange("p (h e) -> p h e", h=8)
                tb_ = tmpB[:, 256:512].rearrange("p (h e) -> p h e", h=8)
                ov = qb[:].rearrange("p (h e) -> p h e", h=8)
                rdc = [psB[b], bf("rope"), bf("ropes")]
                dve(lambda h, q1=q1, cb=cb, ta=ta: h.tensor_tensor(out=ta, in0=q1, in1=cb, op=ALU.mult), rdc, [bf("tmpB")])
                dve(lambda h, q2=q2, sbb=sbb, tb_=tb_: h.tensor_tensor(out=tb_, in0=q2, in1=sbb, op=ALU.mult), rdc, [bf("tmpB")])
                dve(lambda h, ta=ta, tb_=tb_, ov=ov: h.tensor_tensor(out=ov[:, :, 0:32], in0=ta, in1=tb_, op=ALU.subtract),
                    [bf("tmpB")], [qB])
                dve(lambda h, q1=q1, sbb=sbb, ta=ta: h.tensor_tensor(out=ta, in0=q1, in1=sbb, op=ALU.mult), rdc, [bf("tmpB")])
                dve(lambda h, q2=q2, cb=cb, tb_=tb_: h.tensor_tensor(out=tb_, in0=q2, in1=cb, op=ALU.mult), rdc, [bf("tmpB")])
                dve(lambda h, ta=ta, tb_=tb_, ov=ov: h.tensor_tensor(out=ov[:, :, 32:64], in0=ta, in1=tb_, op=ALU.add),
                    [bf("tmpB")], [qB])
            qk_tail(15)
        load_w(wb[:, 0, :].rearrange("p (a n) -> p a n", a=8),
               w_in_d[l][:, 1024:1536].rearrange("(a p) n -> p a n", p=128), bf("wb0"))
        wv = wb[:, 0, :].rearrange("p (a n) -> p a n", a=8)
        for tt in range(16):
            t4 = tt // 4
            b = bank()
            for dk in range(8):
                mm(ps[b][:], hb[:, dk, tt * 128:(tt + 1) * 128], wv[:, dk, :], dk == 0, dk == 7,
                   [hbB[dk][t4], bf("wb0")], [psB[b]])
            vb, vB = (qk_tm, bf("qk_tm")) if tt % 2 == 0 else (tbf2, bf("tbf2"))
            act(lambda h, b=b, vb=vb: h.copy(out=vb[:], in_=ps[b][:]), [psB[b]], [vB])
            P.dma("sp", "v", lambda h, tt=tt, vb=vb: h.dma_start(out=vscr[tt * 128:(tt + 1) * 128, :], in_=vb[:]),
                  reads=[vB], writes=[bf("vscr")])
        load_w(wb[:, 0, :].rearrange("p (a n) -> p a n", a=8),
               w_in_d[l][:, 1536:2048].rearrange("(a p) n -> p a n", p=128), bf("wb0"))
        wv = wb[:, 0, :].rearrange("p (a n) -> p a n", a=8)
        for c in range(4):
            for t in range(4):
                b = bank()
                for dk in range(8):
                    mm(ps[b][:], wv[:, dk, c * 128:(c + 1) * 128], hb[:, dk, t * 512:(t + 1) * 512], dk == 0, dk == 7,
                       [hbB[dk][t], bf("wb0")], [psB[b]])
                act(lambda h, b=b, c=c, t=t: h.copy(out=uT[:, c, t * 512:(t + 1) * 512], in_=ps[b][:]), [psB[b]], [uTB[c][t]])

        if STAGE < 2:
            return
        def tokset(br, i):
            if br == 0:
                return slice(128 * i, 128 * i + 128)
            if br == 1:
                r, J = i // 4, i % 4
                return slice(512 * J + r, 512 * J + 512, 4)
            return slice(i, S, 16)

        def vslot(br, i):
            if br == 0:
                return i
            if br == 1:
                r, J = i // 4, i % 4
                return 4 * J + r
            return i

        subseqs = [
            (0, [list(range(16))]),
            (1, [[r * 4 + J for J in range(4)] for r in range(4)]),
            (2, [[r] for r in range(16)]),
        ]
        for hh in range(8):
            hs = slice((hh % 2) * 64, (hh % 2) * 64 + 64)
            cch = hh // 2
            vcol = vscr[:, 64 * hh:64 * hh + 64]
            srcs = [vcol.rearrange("(j p) e -> p j e", p=128),
                    vcol.rearrange("(J p r) e -> p J r e", J=4, r=4),
                    vcol.rearrange("(p r) e -> p r e", r=16)]
            items = []
            for (br, seqs) in subseqs:
                for seq in seqs:
                    for ii, i in enumerate(seq):
                        items.append((br, i, seq[ii + 1] if ii + 1 < len(seq) else None, ii > 0))
            st_pt = {}

            def stage_a(n):
                br, i, nxt, _ = items[n]
                ks = tokset(br, i)
                b = bank()
                mm(ps[b][:, 0:128], kT[hs, cch, ks], qT[hs, cch, ks], True, True, [R1B], [psB[b]], sig=nxt is None)
                if nxt is not None:
                    mm(ps[b][:, 128:256], kT[hs, cch, ks], qT[hs, cch, tokset(br, nxt)], True, True,
                       [R1B], [psB[b]], sig=True)
                w_ = 256 if nxt is not None else 128
                pt = PT[n % 3]
                ptB = bf("PT%d" % (n % 3))
                act(lambda h, b=b, pt=pt, w_=w_: h.activation(out=pt[:, 0:w_], in_=ps[b][:, 0:w_], func=AF.Exp, scale=0.125),
                    [psB[b]], [ptB])
                pool(lambda h, pt=pt, w_=w_: h.tensor_tensor(out=pt[:, 0:w_], in0=pt[:, 0:w_], in1=maskb[:, 0:w_], op=ALU.mult),
                     [ptB, bf("maskb")], [ptB])
                st_pt[n] = (pt, ptB)

            def stage_b(n):
                br, i, nxt, has_prev = items[n]
                if n == 0 or items[n - 1][0] != br:
                    if br == 1:
                        pairs = [(vh[:, 0, 4 * J:4 * J + 4, 0:64], srcs[1][:, J]) for J in range(4)]
                    else:
                        pairs = [(vh[:, 0, 8 * hf:8 * hf + 8, 0:64], srcs[br][:, 8 * hf:8 * hf + 8]) for hf in range(2)]
                    for (dst, src) in pairs:
                        P.dma("sp", "vh", lambda h, dst=dst, src=src: h.dma_start(out=dst, in_=src),
                              reads=[bf("vscr")], writes=[bf("vh0")])
                ks = tokset(br, i)
                pt, ptB = st_pt[n]
                bo = bank()
                if has_prev:
                    ppt, pptB = st_pt[n - 1]
                    mm(ps[bo][:, 0:128], vh[:, 0, vslot(br, items[n - 1][1]), :], ppt[:, 128:256], True, False,
                       [bf("vh0"), pptB], [psB[bo]], sig=False)
                mm(ps[bo][:, 0:128], vh[:, 0, vslot(br, i), :], pt[:, 0:128], not has_prev, True,
                   [bf("vh0"), ptB], [psB[bo]], sig=True)
                if br == 0:
                    dve(lambda h, bo=bo, ks=ks: h.tensor_copy(out=Oacc[:, ks], in_=ps[bo][:, 0:128]), [psB[bo]], [R2B])
                else:
                    dve(lambda h, bo=bo, ks=ks: h.tensor_tensor(out=Oacc[:, ks], in0=Oacc[:, ks], in1=ps[bo][:, 0:128], op=ALU.add),
                        [psB[bo], R2B], [R2B])

            stage_a(0)
            for n in range(len(items)):
                if n + 1 < len(items):
                    stage_a(n + 1)
                stage_b(n)
            dve(lambda h: h.tensor_copy(out=tmpA[0:64, :], in_=Oacc[64:128, :]), [R2B], [bf("tmpA")])
            dve(lambda h: h.reciprocal(out=tmpA[0:64, :], in_=tmpA[0:64, :]), [bf("tmpA")], [bf("tmpA")])
            wr = [hbB[cch][t] for t in range(4)]
            dve(lambda h, hs=hs, cch=cch: h.tensor_tensor(out=hb[hs, cch, :], in0=Oacc[0:64, :], in1=tmpA[0:64, :], op=ALU.mult),
                [R2B, bf("tmpA")], wr)
        rms_half(l, 0, "attn_gain")

        if STAGE < 3:
            return
        ssm_setup(l)
        T_ = bf("ssmtab")
        bank_pool[0] = [4, 5, 6, 7]
        for gp in range(16):
            j = gp // 4
            gl0 = (2 * gp) % 8
            for ri, srcB in enumerate([bbr, bbi]):
                dve(lambda h: h.memset(Zf[:, 0, :], 0.0), [], [bf("Zf")])
                dve(lambda h, srcB=srcB, gp=gp, gl0=gl0: h.tensor_copy(out=Zf[0:64, 0, 16 * gl0:16 * gl0 + 16], in_=srcB[0:64, gp, :]), [T_], [bf("Zf")])
                dve(lambda h, srcB=srcB, gp=gp, gl0=gl0: h.tensor_copy(out=Zf[64:128, 0, 16 * gl0 + 16:16 * gl0 + 32], in_=srcB[64:128, gp, :]), [T_], [bf("Zf")])
                b = bank()
                tr(ps[b][:, 0:128], Zf[:, 0, :], identf[:], [bf("Zf"), bf("identf")], [psB[b]])
                act(lambda h, b=b, ri=ri: h.copy(out=BbT[:, ri, :], in_=ps[b][:, 0:128]), [psB[b]], [bf("BbT")])
            dve(lambda h: h.memset(Zf[:], 0.0), [], [bf("Zf")])
            for ri in range(2):
                dve(lambda h, ri=ri, gl0=gl0, j=j: h.tensor_copy(out=Zf[0:64, ri, 16 * gl0:16 * gl0 + 16], in_=CT[0:64, ri, j, 16 * gl0:16 * gl0 + 16]), [T_], [bf("Zf")])
                dve(lambda h, ri=ri, gl0=gl0, j=j: h.tensor_copy(out=Zf[64:128, ri, 16 * gl0 + 16:16 * gl0 + 32], in_=CT[0:64, ri, j, 16 * gl0 + 16:16 * gl0 + 32]), [T_], [bf("Zf")])
            dve(lambda h: h.tensor_copy(out=Cm[:], in_=Zf[:]), [bf("Zf")], [bf("Cm")])
            for t in range(4):
                sl = slice(t * 512, (t + 1) * 512)
                for ri in range(2):
                    b = bank()
                    mm(ps[b][:], BbT[:, ri, :], uT[:, j, sl], True, True, [bf("BbT"), uTB[j][t]], [psB[b]])
                    if ri == 0:
                        act(lambda h, b=b, sl=sl: h.copy(out=X0[:, 0, sl], in_=ps[b][:]), [psB[b]], [R1B])
                    else:
                        act(lambda h, b=b, sl=sl: h.copy(out=X0[:, 1, sl], in_=ps[b][:]), [psB[b]], [R1B])
            def combine(T, Sx, k):
                ar = LPr[:, k, gp:gp + 1]
                ai = LPi[:, k, gp:gp + 1]
                an = LPn[:, k, gp:gp + 1]
                rdx = [R1B, T_]
                dve(lambda h, T=T, Sx=Sx, ar=ar: h.scalar_tensor_tensor(
                    out=X0[:, :, T], in0=X0[:, :, Sx], scalar=ar, in1=X0[:, :, T], op0=ALU.mult, op1=ALU.add), rdx, [R1B])
                dve(lambda h, T=T, Sx=Sx, an=an: h.scalar_tensor_tensor(
                    out=X0[:, 0, T], in0=X0[:, 1, Sx], scalar=an, in1=X0[:, 0, T], op0=ALU.mult, op1=ALU.add), rdx, [R1B])
                dve(lambda h, T=T, Sx=Sx, ai=ai: h.scalar_tensor_tensor(
                    out=X0[:, 1, T], in0=X0[:, 0, Sx], scalar=ai, in1=X0[:, 1, T], op0=ALU.mult, op1=ALU.add), rdx, [R1B])

            for k in range(11):
                st_, d = 1 << (k + 1), 1 << k
                combine(slice(st_ - 1, S, st_), slice(d - 1, S, st_), k)
            for k in range(9, -1, -1):
                st_, d = 1 << (k + 1), 1 << k
                combine(slice(st_ + d - 1, S, st_), slice(st_ - 1, S - d, st_), k)
            src = X0
            act(lambda h, src=src: h.copy(out=XB[:], in_=src[:]), [R1B], [R2B])
            for t in range(4):
                sl = slice(t * 512, (t + 1) * 512)
                first_gp = (gp % 4 == 0)
                last_gp = (gp % 4 == 3)
                mm(ps[t][:], Cm[:, 0, :], XB[:, 0, sl], first_gp, False, [bf("Cm"), R2B], [psB[t]], sig=False)
                mm(ps[t][:], Cm[:, 1, :], XB[:, 1, sl], False, False, [bf("Cm"), R2B], [psB[t]], sig=not last_gp)
                if last_gp:
                    mm(ps[t][:], Dd[:, j, :], uT[:, j, sl], False, True, [T_, uTB[j][t]], [psB[t]], sig=True)
                    act(lambda h, t=t: h.activation(out=tmpB[:], in_=ps[t][:], func=AF.Square), [psB[t]], [bf("tmpB")])
                    dve(lambda h: h.tensor_scalar(out=tmpB[:], in0=tmpB[:], scalar1=0.044715, scalar2=1.0, op0=ALU.mult, op1=ALU.add),
                        [bf("tmpB")], [bf("tmpB")])
                    dve(lambda h, t=t: h.tensor_tensor(out=tmpB[:], in0=tmpB[:], in1=ps[t][:], op=ALU.mult), [bf("tmpB"), psB[t]], [bf("tmpB")])
                    act(lambda h: h.activation(out=tmpB[:], in_=tmpB[:], func=AF.Sigmoid, scale=1.5957691216057308), [bf("tmpB")], [bf("tmpB")])
                    dve(lambda h, t=t, sl=sl, j=j: h.tensor_tensor(out=uT[:, j, sl], in0=tmpB[:], in1=ps[t][:], op=ALU.mult),
                        [bf("tmpB"), psB[t]], [uTB[j][t]])
        bank_pool[0] = list(range(8))
        load_w(wb[:, 0, 0:2048].rearrange("p (a n) -> p a n", a=4),
               w_glu_d[l].rearrange("(a p) n -> p a n", p=128), bf("wb0"))
        wg = wb[:, 0, 0:2048].rearrange("p (a n) -> p a n", a=4)
        for co in range(4):
            bg = pcol("b_glu", l * 4 + co)
            for t in range(4):
                sl = slice(t * 512, (t + 1) * 512)
                b = bank()
                for ci in range(4):
                    mm(ps[b][:], wg[:, ci, co * 128:(co + 1) * 128], uT[:, ci, sl], ci == 0, ci == 3, [bf("wb0"), uTB[ci][t]], [psB[b]])
                act(lambda h, b=b, bg=bg: h.activation(out=tmpB[:], in_=ps[b][:], func=AF.Sigmoid, bias=bg, scale=1.0),
                    [psB[b], bf("prm")], [bf("tmpB")])
                dve(lambda h, co=co, sl=sl: h.tensor_tensor(out=hb[:, 4 + co, sl], in0=uT[:, co, sl], in1=tmpB[:], op=ALU.mult),
                    [bf("tmpB"), uTB[co][t]], [hbB[4 + co][t]])
        rms_half(l, 4, "ssm_gain")

        if dbg == "mixed" and s == 0 and l == 0:
            for c in range(8):
                dve(lambda h, c=c: h.tensor_copy(out=tmpA[:], in_=hb[:, c, :]), [hbB[c][t] for t in range(4)], [bf("tmpA")])
                P.dma("sp", "st", lambda h, c=c: h.dma_start(out=dbg_d[:, c, :], in_=tmpA[:]), reads=[bf("tmpA")])

        if STAGE < 4:
            return
        for co in range(8):
            if co % 4 == 0:
                half = co // 4
                load_w(wb[:, 0, :].rearrange("p (a n) -> p a n", a=8),
                       w_out_d[l][:, half * 512:(half + 1) * 512].rearrange("(a p) n -> p a n", p=128), bf("wb0"))
            wv = wb[:, 0, :].rearrange("p (a n) -> p a n", a=8)
            cl = co % 4
            bo_ = pcol("b_out", l * 8 + co)
            for t in range(4):
                sl = slice(t * 512, (t + 1) * 512)
                b = bank()
                for dk in range(8):
                    mm(ps[b][:], wv[:, dk, cl * 128:(cl + 1) * 128], hb[:, dk, sl], dk == 0, dk == 7,
                       [bf("wb0"), hbB[dk][t]], [psB[b]])
                act(lambda h, b=b, bo_=bo_: h.activation(out=tmpB[:], in_=ps[b][:], func=AF.Identity, bias=bo_, scale=1.0),
                    [psB[b], bf("prm")], [bf("tmpB")])
                dve(lambda h, co=co, sl=sl: h.scalar_tensor_tensor(out=h32[:, co, sl], in0=h32[:, co, sl], scalar=ALPHA, in1=tmpB[:],
                                                                 op0=ALU.mult, op1=ALU.add), [hB[co][t], bf("tmpB")], [hB[co][t]])
        P.barrier()
        layer_norm(l, "ln1_g", "ln1_b")
        if dbg == "ln1" and s == 0 and l == 0:
            for c in range(8):
                P.dma("sp", "st", lambda h, c=c: h.dma_start(out=dbg_d[:, c, :], in_=h32[:, c, :]), reads=[hB[c][t] for t in range(4)])

        if STAGE < 5:
            return
        P.barrier()
        W1e = [R1b[:, 0:4096].rearrange("p (c n) -> p c n", c=8), R1b[:, 4096:8192].rearrange("p (c n) -> p c n", c=8)]
        W2e = [R1b[:, 8192:12288].rearrange("p (c n) -> p c n", c=4), R1b[:, 12288:16384].rearrange("p (c n) -> p c n", c=4)]

        def load_pass(pp):
            bi = pp % 2
            load_w(W1e[bi], w_ff1_d[l][:, pp * 512:(pp + 1) * 512].rearrange("(a p) n -> p a n", p=128), bf("W1e%d" % bi))
            load_w(W2e[bi], w_ff2_d[l][pp * 512:(pp + 1) * 512, :].rearrange("(a p) n -> p a n", p=128), bf("W2e%d" % bi))

        load_pass(0)
        tile_ctr = 0
        for pp in range(8):
            if pp + 1 < 8:
                load_pass(pp + 1)
            bi = pp % 2
            for t in range(4):
                sl = slice(t * 512, (t + 1) * 512)
                ab = (tile_ctr % 2) * 4
                tile_ctr += 1
                for fc in range(4):
                    b = bank()
                    b1_ = pcol("b_ff1", l * 32 + pp * 4 + fc)
                    for dk in range(8):
                        mm(ps[b][:], W1e[bi][:, dk, fc * 128:(fc + 1) * 128], hb[:, dk, sl], dk == 0, dk == 7,
                           [bf("W1e%d" % bi), hbB[dk][t]], [psB[b]])
                    act(lambda h, b=b, b1_=b1_: h.activation(out=tbf2[:], in_=ps[b][:], func=AF.Relu, bias=b1_, scale=1.0),
                        [psB[b], bf("prm")], [bf("tbf2")])
                    dve(lambda h, fc=fc, ab=ab: h.tensor_tensor(out=aT[:, ab + fc, :], in0=tbf2[:], in1=tbf2[:], op=ALU.mult),
                        [bf("tbf2")], [bf("aT%d" % (ab + fc))])
                for co in range(8):
                    b = bank()
                    for fc in range(4):
                        mm(ps[b][:], W2e[bi][:, fc, co * 128:(co + 1) * 128], aT[:, ab + fc, :], fc == 0, fc == 3,
                           [bf("W2e%d" % bi), bf("aT%d" % (ab + fc))], [psB[b]])
                    if pp == 0:
                        b2_ = pcol("b_ff2", l * 8 + co)
                        act(lambda h, b=b, b2_=b2_: h.activation(out=tmpB[:], in_=ps[b][:], func=AF.Identity, bias=b2_, scale=1.0),
                            [psB[b], bf("prm")], [bf("tmpB")])
                        dve(lambda h, co=co, sl=sl: h.scalar_tensor_tensor(out=h32[:, co, sl], in0=h32[:, co, sl], scalar=ALPHA, in1=tmpB[:],
                                                                         op0=ALU.mult, op1=ALU.add), [hB[co][t], bf("tmpB")], [hB[co][t]])
                    else:
                        dve(lambda h, b=b, co=co, sl=sl: h.tensor_tensor(out=h32[:, co, sl], in0=h32[:, co, sl], in1=ps[b][:], op=ALU.add),
                            [hB[co][t], psB[b]], [hB[co][t]])
        P.barrier()
        layer_norm(l, "ln2_g", "ln2_b")
        P.barrier()
        P.new_epoch()

    for s in range(NSEQ):
        if "rope" not in KSKIP: P.dma("sp", "misc", lambda h, s=s: h.dma_start(out=posi[:], in_=pos_d[s].rearrange("(j p) -> p j", p=128)),
              writes=[bf("posi")])
        if "rope" not in KSKIP:
            dve(lambda h: h.tensor_copy(out=posf[:], in_=posi[:]), [bf("posi")], [bf("posf")])
        for tt in (range(16) if "rope" not in KSKIP else []):
            pc = posf[:, tt:tt + 1]
            dve(lambda h, tt=tt, pc=pc: h.tensor_scalar(out=cosT[:, tt, :], in0=invf[:], scalar1=pc, scalar2=None, op0=ALU.mult),
                [bf("posf"), bf("invf")], [bf("rope")])
        rr = [bf("rope")]
        if "rope" not in KSKIP:
            sin_of(sinT[:], cosT[:], 0.0, 512, rr, [bf("ropes")])
            sin_of(cosT[:], cosT[:], 0.5 * math.pi, 512, rr + [bf("ropes")], rr)
        for tt in range(int(os.environ.get("KXL", "16"))):
            t4 = tt // 4
            tq = 0 if os.environ.get("KXOFF") else tt
            P.dma("sp", "ld", lambda h, s=s, tt=tq: h.dma_start(out=xtm, in_=x_d[s, tt * 128:(tt + 1) * 128, :]),
                  writes=[bf("tmpA")])
            for g in range(2):
                b = bank()
                for c4 in range(4):
                    c = g * 4 + c4
                    tr(ps[b][:, c4 * 128:(c4 + 1) * 128], xtm[:, c * 128:(c + 1) * 128], identf[:], [bf("tmpA"), bf("identf")], [psB[b]])
                wr = [hB[g * 4 + c4][t4] for c4 in range(4)]
                wr2 = [hbB[g * 4 + c4][t4] for c4 in range(4)]
                pv = ps[b][:].rearrange("p (c n) -> p c n", c=4)
                if "h32copy" not in KSKIP:
                    dve(lambda h, pv=pv, g=g, tt=tt: h.tensor_copy(out=h32[:, g * 4:g * 4 + 4, tt * 128:(tt + 1) * 128], in_=pv), [psB[b]], wr)
                if "hbcopy" not in KSKIP:
                    act(lambda h, pv=pv, g=g, tt=tt: h.copy(out=hb[:, g * 4:g * 4 + 4, tt * 128:(tt + 1) * 128], in_=pv), [psB[b]], wr2)
        for l in range(NL):
            block(s, l, False)
        for tt in range(int(os.environ.get("KXS", "16"))):
            t4 = tt // 4
            for g in range(2):
                b = bank()
                for c4 in range(4):
                    c = g * 4 + c4
                    tr(ps[b][:, c4 * 128:(c4 + 1) * 128], h32[:, c, tt * 128:(tt + 1) * 128], identf[:], [hB[c][t4], bf("identf")], [psB[b]])
                dve(lambda h, b=b, g=g: h.tensor_copy(out=xtm[:, g * 512:(g + 1) * 512], in_=ps[b][:]), [psB[b]], [bf("tmpA")])
            P.dma("sp", "st", lambda h, s=s, tt=tt: h.dma_start(out=out_d[s, tt * 128:(tt + 1) * 128, :], in_=xtm),
                  reads=[bf("tmpA")])
        P.barrier()
    P.barrier()
    with nc.allow_non_contiguous_dma(reason="small strided param loads"), nc.Block() as blk:
        P.emit_all(blk)
    es.close()
    return nc, P


def consts():
    ident = np.eye(128, dtype=np.float32)
    k = np.arange(128)[:, None]
    q = np.arange(128)[None, :]
    mask = np.concatenate([(k <= q), (k >= q)], axis=1).astype(np.float32)
    invf = (10000.0 ** (-np.arange(32, dtype=np.float32) * 2.0 / 64.0)).astype(np.float32)
    invf = np.broadcast_to(invf[None, :], (128, 32)).copy()
    return {"c_ident": ident, "c_mask": mask, "c_invf": invf}


WNAMES = ["w_in", "attn_gain", "ssm_gain", "ssm_a_re", "ssm_a_im", "ssm_log_dt", "ssm_b_re", "ssm_b_im",
          "ssm_c_re", "ssm_c_im", "ssm_d", "w_glu", "b_glu", "w_out", "b_out", "ln1_g", "ln1_b",
          "w_ff1", "b_ff1", "w_ff2", "b_ff2", "ln2_g", "ln2_b"]

_CACHE = {}

LAUNCH_NL = 4
LAUNCH_NSEQ = 4


def kernel(**inputs):
    x = np.ascontiguousarray(inputs["x"], dtype=np.float32)
    pos = np.ascontiguousarray(inputs["positions"], dtype=np.int32)
    per_core = x.shape[0] // NCORES
    key = (LAUNCH_NL, LAUNCH_NSEQ)
    if key not in _CACHE:
        _CACHE[key] = build(LAUNCH_NL, LAUNCH_NSEQ)[0]
    nc = _CACHE[key]
    cs = consts()
    h = x
    for l0 in range(0, DEPTH, LAUNCH_NL):
        base = {n: np.ascontiguousarray(inputs[n][l0:l0 + LAUNCH_NL], dtype=np.float32) for n in WNAMES}
        base.update(cs)
        newh = np.empty_like(h)
        for s0 in range(0, per_core, LAUNCH_NSEQ):
            in_maps = []
            for c in range(NCORES):
                m = dict(base)
                lo = c * per_core + s0
                m["x"] = np.ascontiguousarray(h[lo:lo + LAUNCH_NSEQ])
                m["positions"] = np.ascontiguousarray(pos[lo:lo + LAUNCH_NSEQ])
                in_maps.append(m)
            res = run_bass_kernel_spmd(nc, in_maps, core_ids=list(range(NCORES)))
            for c in range(NCORES):
                lo = c * per_core + s0
                newh[lo:lo + LAUNCH_NSEQ] = res.results[c]["out"]
        h = newh
    return h
```

```python
import math
import os
STAGE = int(os.environ.get("KSTAGE", "9"))
KSKIP = set(os.environ.get("KSKIP", "").split(","))
from contextlib import ExitStack
import numpy as np
import concourse.bass as bass
import concourse.mybir as mybir
from concourse.bass_utils import run_bass_kernel_spmd

F32 = mybir.dt.float32
BF16 = mybir.dt.bfloat16
I32 = mybir.dt.int32
ALU = mybir.AluOpType
AF = mybir.ActivationFunctionType
ENGS = ("pe", "act", "dve", "pool", "sp")

D = 1024
S = 2048
DEPTH = 4
NCORES = 8
ALPHA = (2.0 * DEPTH) ** 0.25
LN_EPS = 1e-5
RMS_EPS = 1e-6
TWO_PI = 2.0 * math.pi


class Buf:
    __slots__ = ("name", "w", "r", "excl")

    def __init__(self, name, excl=False):
        self.name = name
        self.w = None
        self.r = {}
        self.excl = excl


class Prog:
    def __init__(self, nc, es):
        self.nc = nc
        self.es = es
        self.q = {e: [] for e in ENGS}
        self.sems = {}
        self.cnt = {}
        self.cur = {}
        self.dead = set()
        self.epoch = -1
        self.waited = {e: {} for e in ENGS}
        self.dma_cnt = {}
        self.dma_keys = []
        self.dma_rr = 0
        self.n_inst = 0
        self.new_epoch()

    def new_epoch(self):
        for k in self.cur.values():
            self.dead.add(k)
        self.epoch += 1
        for e in ENGS:
            k = "%s#%d" % (e, self.epoch)
            self.cur[e] = k
            self.cnt[k] = 0
            self.sems[k] = self.es.enter_context(self.nc.semaphore("s_%s_%d" % (e, self.epoch)))

    def _need(self, e, deps):
        out = []
        ce = self.cur[e]
        for (k, v) in deps:
            if k in self.dead:
                continue
            if k == ce and (e == "pe" or v > self.cnt[ce]):
                continue
            if self.waited[e].get(k, 0) >= v:
                continue
            self.waited[e][k] = v
            out.append((k, v))
        return out

    def _deps(self, reads, writes):
        deps = set()
        for b in reads:
            if b.w is not None:
                deps.add(b.w)
            if b.excl:
                for kv in b.r.items():
                    deps.add(kv)
        for b in writes:
            if b.w is not None:
                deps.add(b.w)
            for kv in b.r.items():
                deps.add(kv)
        return deps

    def op(self, e, fn, reads=(), writes=(), sig=True):
        waits = self._need(e, self._deps(reads, writes))
        ce = self.cur[e]
        if sig:
            self.cnt[ce] += 1
        tick = self.cnt[ce] if sig else self.cnt[ce] + 1
        sem = self.sems[ce]
        sems = self.sems

        def emit(h):
            for (k, v) in waits:
                h.wait_ge(sems[k], v)
            ins = fn(h)
            if sig:
                ins.then_inc(sem, 1)
        self.q[e].append(emit)
        self.n_inst += 1
        for b in reads:
            if b.excl:
                b.w = (ce, tick)
                b.r = {}
            elif b.r.get(ce, 0) < tick:
                b.r[ce] = tick
        for b in writes:
            b.w = (ce, tick)
            b.r = {}
        return tick

    def dma(self, qe, semkey, fn, reads=(), writes=()):
        semkey = self.dma_keys[self.dma_rr % len(self.dma_keys)]
        self.dma_rr += 1
        deps = self._deps(reads, writes)
        prev = self.dma_cnt.get(semkey, 0)
        if prev > 0:
            deps.add((semkey, prev))
        waits = self._need(qe, deps)
        self.dma_cnt[semkey] = prev + 16
        val = self.dma_cnt[semkey]
        sems = self.sems
        sem = sems[semkey]

        def emit(h):
            for (k, v) in waits:
                h.wait_ge(sems[k], v)
            fn(h).then_inc(sem, 16)
        self.q[qe].append(emit)
        self.n_inst += 1
        for b in reads:
            if b.r.get(semkey, 0) < val:
                b.r[semkey] = val
        for b in writes:
            b.w = (semkey, val)
            b.r = {}
        return val

    def wait_all(self, e):
        deps = [(self.cur[k], self.cnt[self.cur[k]]) for k in ENGS if self.cnt[self.cur[k]] > 0]
        deps += list(self.dma_cnt.items())
        waits = self._need(e, deps)
        sems = self.sems

        def emit(h):
            for (k, v) in waits:
                h.wait_ge(sems[k], v)
        self.q[e].append(emit)

    def barrier(self):
        for e in ENGS:
            self.wait_all(e)

    def emit_all(self, block):
        prog = self

        @block.tensor
        def _(h):
            for f in prog.q["pe"]:
                f(h)

        @block.scalar
        def _(h):
            for f in prog.q["act"]:
                f(h)

        @block.vector
        def _(h):
            for f in prog.q["dve"]:
                f(h)

        @block.gpsimd
        def _(h):
            for f in prog.q["pool"]:
                f(h)

        @block.sync
        def _(h):
            for f in prog.q["sp"]:
                f(h)


def build(NL, NSEQ, dbg=None):
    nc = bass.Bass("TRN2", target_bir_lowering=False)
    es = ExitStack()

    MINIO = os.environ.get("KMINIO", "")

    def din(name, shape, dt=F32):
        if MINIO and name not in ("x", "positions", "c_ident", "c_mask", "c_invf"):
            return None
        return nc.dram_tensor(name, list(shape), dt, kind="ExternalInput").ap()

    x_d = din("x", [NSEQ, S, D])
    pos_d = din("positions", [NSEQ, S], I32)
    w_in_d = din("w_in", [NL, D, 2048])
    attn_gain_d = din("attn_gain", [NL, 512])
    ssm_gain_d = din("ssm_gain", [NL, 512])
    a_re_d = din("ssm_a_re", [NL, 32, 64])
    a_im_d = din("ssm_a_im", [NL, 32, 64])
    log_dt_d = din("ssm_log_dt", [NL, 32])
    b_re_d = din("ssm_b_re", [NL, 32, 64, 16])
    b_im_d = din("ssm_b_im", [NL, 32, 64, 16])
    c_re_d = din("ssm_c_re", [NL, 32, 16, 64])
    c_im_d = din("ssm_c_im", [NL, 32, 16, 64])
    d_d = din("ssm_d", [NL, 32, 16])
    w_glu_d = din("w_glu", [NL, 512, 512])
    b_glu_d = din("b_glu", [NL, 512])
    w_out_d = din("w_out", [NL, D, D])
    b_out_d = din("b_out", [NL, D])
    ln1_g_d = din("ln1_g", [NL, D])
    ln1_b_d = din("ln1_b", [NL, D])
    w_ff1_d = din("w_ff1", [NL, D, 4096])
    b_ff1_d = din("b_ff1", [NL, 4096])
    w_ff2_d = din("w_ff2", [NL, 4096, D])
    b_ff2_d = din("b_ff2", [NL, D])
    ln2_g_d = din("ln2_g", [NL, D])
    ln2_b_d = din("ln2_b", [NL, D])
    ident_d = din("c_ident", [128, 128])
    mask_d = din("c_mask", [128, 256])
    invf_d = din("c_invf", [128, 32])
    out_d = nc.dram_tensor("out", [NSEQ, S, D], F32, kind="ExternalOutput").ap()
    vscr = None if MINIO else nc.dram_tensor("vscr", [S, 512], BF16, kind="ExternalOutput").ap()
    dbg_d = None
    if dbg:
        dbg_d = nc.dram_tensor("dbg", [128, 8, S], F32, kind="ExternalOutput").ap()

    def sb(name, shape, dt):
        return es.enter_context(nc.sbuf_tensor(name, list(shape), dt))

    P = Prog(nc, es)
    for i in range(int(os.environ.get("KNDMA", "8"))):
        k = "dma%d" % i
        P.sems[k] = es.enter_context(nc.semaphore("d_" + k))
        P.dma_keys.append(k)

    h32 = sb("h32", [128, 8, S], F32)
    hb = sb("hb", [128, 8, S], BF16)
    R1 = sb("R1", [128, 8192], F32)
    R1b = R1.bitcast(BF16)
    qT = R1b[:, 0:8192].rearrange("p (c n) -> p c n", c=4)
    kT = R1b[:, 8192:16384].rearrange("p (c n) -> p c n", c=4)
    X0 = R1[:, 0:4096].rearrange("p (c n) -> p c n", c=2)
    X1 = R1[:, 4096:8192].rearrange("p (c n) -> p c n", c=2)
    W1q = R1b[:, 0:8192].rearrange("p (c n) -> p c n", c=8)
    W2q = R1b[:, 8192:16384].rearrange("p (c n) -> p c n", c=8)
    uT = sb("uT", [128, 4, S], BF16)
    R2 = sb("R2", [128, 2048], F32)
    Oacc = R2
    R2b = R2.bitcast(BF16)
    XB = R2b[:, 0:4096].rearrange("p (c n) -> p c n", c=2)
    aT = R2b[:, 0:4096].rearrange("p (c n) -> p c n", c=8)
    vh = sb("vh", [128, 1, 16, 128], BF16)
    PT = [sb("PT%d" % i, [128, 256], BF16) for i in range(3)]
    tmpA = sb("tmpA", [128, 2048], F32)
    tmpB = sb("tmpB", [128, 512], F32)
    tmpC = sb("tmpC", [128, 512], F32)
    tbf2 = sb("tbf2", [128, 512], BF16)
    NWST = int(os.environ.get("KNWST", "2"))
    wsts = [sb("wst%d" % i, [128, 1024], F32) for i in range(NWST)]
    wst_ctr = [0]
    wb = sb("wb", [128, 1, 4096], BF16)
    xtm = tmpA[:, 0:1024]
    qk_tm = sb("qk_tm", [128, 512], BF16)
    v_tm = qk_tm
    identf = sb("identf", [128, 128], F32)
    identb = sb("identb", [128, 128], BF16)
    onesb = sb("onesb", [128, 128], BF16)
    maskb = sb("maskb", [128, 256], BF16)
    invf = sb("invf", [128, 32], F32)
    posf = sb("posf", [128, 16], F32)
    posi = sb("posi", [128, 16], I32)
    cosT = sb("cosT", [128, 16, 32], F32)
    sinT = sb("sinT", [128, 16, 32], F32)
    prm = sb("prm", [128, 512], F32)
    prm_stage = sb("prm_stage", [128, 128], F32)
    sA = sb("sA", [128, 16, 8], F32)
    LPr = sb("LPr", [128, 11, 16], F32)
    LPi = sb("LPi", [128, 11, 16], F32)
    LPn = sb("LPn", [128, 11, 16], F32)
    Bre = sb("Bre", [128, 16, 16], F32)
    Bim = sb("Bim", [128, 16, 16], F32)
    bbr = sb("bbr", [128, 16, 16], F32)
    bbi = sb("bbi", [128, 16, 16], F32)
    CT = sb("CT", [128, 2, 4, 128], BF16)
    Zf = sb("Zf", [128, 2, 128], F32)
    BbT = sb("BbT", [128, 2, 128], BF16)
    Cm = sb("Cm", [128, 2, 128], BF16)
    Dd = sb("Dd", [128, 4, 128], BF16)
    cstage = sb("cstage", [128, 64], F32)

    ps = [es.enter_context(nc.psum_tensor("ps%d" % i, [128, 512], F32)) for i in range(8)]
    psb = [p.bitcast(BF16) for p in ps]
    psB = [Buf("ps%d" % i, excl=True) for i in range(8)]
    bank_ctr = [0]
    bank_pool = [list(range(8))]

    def bank():
        pool = bank_pool[0]
        i = pool[bank_ctr[0] % len(pool)]
        bank_ctr[0] += 1
        return i

    B = {}

    def bf(name):
        if name not in B:
            B[name] = Buf(name)
        return B[name]

    hB = [[bf("h%d_%d" % (c, t)) for t in range(4)] for c in range(8)]
    hbB = [[bf("hb%d_%d" % (c, t)) for t in range(4)] for c in range(8)]
    R1B = bf("R1")
    uTB = [[bf("uT%d_%d" % (c, t)) for t in range(4)] for c in range(4)]
    R2B = bf("R2")

    def allhb():
        return [b for row in hbB for b in row]

    def allh():
        return [b for row in hB for b in row]

    def alluT():
        return [b for row in uTB for b in row]

    def dve(fn, reads, writes):
        return P.op("dve", fn, reads, writes)

    def act(fn, reads, writes):
        return P.op("act", fn, reads, writes)

    def pool(fn, reads, writes):
        return P.op("pool", fn, reads, writes)

    def mm(out, lhsT, rhs, start, stop, reads, writes, sig=None):
        if sig is None:
            sig = stop
        return P.op("pe", lambda h: h.matmul(out, lhsT=lhsT, rhs=rhs, start=start, stop=stop),
                    reads, writes, sig=sig)

    def tr(out, in_, ident, reads, writes):
        if os.environ.get("KTR", "") == "mm" and in_.dtype == F32:
            return P.op("pe", lambda h: h.matmul(out, lhsT=in_, rhs=ident, start=True, stop=True), reads, writes)
        return P.op("pe", lambda h: h.transpose(out=out, in_=in_, identity=ident), reads, writes)

    def load_w(dst, src, dstB, nparts=128):
        a, n = dst.shape[1], dst.shape[2]
        per = max(1, 1024 // n)
        for a0 in range(0, a, per):
            a1 = min(a, a0 + per)
            k = wst_ctr[0] % NWST
            wst_ctr[0] += 1
            wbuf = bf("wst%d" % k)
            st = wsts[k][:, 0:(a1 - a0) * n].rearrange("p (a n) -> p a n", a=a1 - a0)
            P.dma("sp", "w", lambda h, st=st, a0=a0, a1=a1: h.dma_start(out=st, in_=src[:, a0:a1, :]),
                  writes=[wbuf])
            pool(lambda h, st=st, a0=a0, a1=a1: h.tensor_copy(out=dst[:, a0:a1, :], in_=st),
                 [wbuf], [dstB])

    P.dma("sp", "misc", lambda h: h.dma_start(out=identf[:], in_=ident_d), writes=[bf("identf")])
    P.dma("sp", "misc", lambda h: h.dma_start(out=tmpB[:, 0:256], in_=mask_d), writes=[bf("tmpB")])
    P.dma("sp", "misc", lambda h: h.dma_start(out=invf[:], in_=invf_d), writes=[bf("invf")])
    dve(lambda h: h.tensor_copy(out=identb[:], in_=identf[:]), [bf("identf")], [bf("identb")])
    dve(lambda h: h.tensor_copy(out=maskb[:], in_=tmpB[:, 0:256]), [bf("tmpB")], [bf("maskb")])
    dve(lambda h: h.memset(onesb[:], 1.0), [], [bf("onesb")])
    if "vh" not in KSKIP:
        dve(lambda h: h.memset(vh[:, 0, :, :], 1.0), [], [bf("vh0")])

    prm_cols = {}
    col = [0]

    def load_T(name, src2d, R):
        if "prm" in KSKIP:
            prm_cols[name] = col[0]; col[0] += R
            return
        c0 = col[0]
        col[0] += R
        prm_cols[name] = c0
        P.dma("sp", "misc", lambda h: h.dma_start(out=prm_stage[0:R, :], in_=src2d), writes=[bf("prm_stage")])
        b = bank()
        tr(ps[b][:, 0:R], prm_stage[0:R, :], identf[0:R, 0:R], [bf("prm_stage"), bf("identf")], [psB[b]])
        dve(lambda h: h.tensor_copy(out=prm[:, c0:c0 + R], in_=ps[b][:, 0:R]), [psB[b]], [bf("prm")])
        return c0

    if not MINIO:
        load_T("b_ff1", b_ff1_d.rearrange("l (c p) -> (l c) p", p=128), NL * 32)
        for nm, ap_ in [("b_out", b_out_d), ("ln1_g", ln1_g_d), ("ln1_b", ln1_b_d), ("b_ff2", b_ff2_d),
                        ("ln2_g", ln2_g_d), ("ln2_b", ln2_b_d)]:
            load_T(nm, ap_.rearrange("l (c p) -> (l c) p", p=128), NL * 8)
        for nm, ap_ in [("attn_gain", attn_gain_d), ("ssm_gain", ssm_gain_d), ("b_glu", b_glu_d)]:
            load_T(nm, ap_.rearrange("l (c p) -> (l c) p", p=128), NL * 4)
        load_T("ssm_d", d_d.rearrange("l (c g) n -> (l c) (g n)", g=8), NL * 4)
        for nm, ap_ in [("a_re", a_re_d), ("a_im", a_im_d)]:
            load_T(nm, ap_.rearrange("l (gp two) p -> (l gp) (two p)", two=2), NL * 16)
    assert col[0] <= 512

    def pcol(name, idx):
        c = prm_cols[name] + idx
        return prm[:, c:c + 1]

    def sin_of(dst, src, add, n, rd, wr):
        ti = tmpB.bitcast(I32)[:, 0:n]
        tf = tmpC[:, 0:n]
        tg = tmpB[:, 0:n] if False else None
        if len(dst.shape) == 3:
            a_, b_ = dst.shape[1], dst.shape[2]
            ti = ti.rearrange("p (a b) -> p a b", a=a_)
            tf = tf.rearrange("p (a b) -> p a b", a=a_)
        T = [bf("tmpB"), bf("tmpC")]
        dve(lambda h: h.tensor_scalar(out=dst, in0=src, scalar1=float(add), scalar2=None, op0=ALU.add), rd, wr)
        dve(lambda h: h.tensor_scalar(out=tf, in0=dst, scalar1=1.0 / TWO_PI, scalar2=None, op0=ALU.mult), wr, T)
        dve(lambda h: h.tensor_copy(out=ti, in_=tf), T, T)
        dve(lambda h: h.tensor_copy(out=tf, in_=ti), T, T)
        dve(lambda h: h.scalar_tensor_tensor(out=dst, in0=tf, scalar=-TWO_PI, in1=dst, op0=ALU.mult, op1=ALU.add), T + wr, wr)
        dve(lambda h: h.tensor_scalar(out=tf, in0=dst, scalar1=-math.pi, scalar2=1e30, op0=ALU.add, op1=ALU.mult), wr, T)
        dve(lambda h: h.tensor_scalar(out=tf, in0=tf, scalar1=0.0, scalar2=1.0, op0=ALU.max, op1=ALU.min), T, T)
        dve(lambda h: h.scalar_tensor_tensor(out=dst, in0=tf, scalar=-TWO_PI, in1=dst, op0=ALU.mult, op1=ALU.add), T + wr, wr)
        act(lambda h: h.activation(out=dst, in_=dst, func=AF.Sin), wr, wr)

    def ssm_setup(l):
        T_ = bf("ssmtab")
        for gpar in range(2):
            src = bass.AP(log_dt_d.tensor, l * 32 + gpar, [[0, 64], [2, 16], [1, 1]])
            P.dma("sp", "misc", lambda h, src=src, gpar=gpar: h.dma_start(
                out=sA[gpar * 64:(gpar + 1) * 64, :, 0:1], in_=src), writes=[T_])
        are = prm[:, prm_cols["a_re"] + l * 16: prm_cols["a_re"] + l * 16 + 16]
        aim = prm[:, prm_cols["a_im"] + l * 16: prm_cols["a_im"] + l * 16 + 16]
        dt_ = sA[:, :, 0]
        mag = sA[:, :, 1]
        ang = sA[:, :, 2]
        t3 = sA[:, :, 3]
        t4 = sA[:, :, 4]
        t5 = sA[:, :, 5]
        t6 = sA[:, :, 6]
        t7 = sA[:, :, 7]
        rd = [T_, bf("prm")]
        act(lambda h: h.activation(out=dt_, in_=dt_, func=AF.Exp), rd, [T_])
        dve(lambda h: h.tensor_tensor(out=mag, in0=are, in1=dt_, op=ALU.mult), rd, [T_])
        act(lambda h: h.activation(out=mag, in_=mag, func=AF.Exp), rd, [T_])
        dve(lambda h: h.tensor_tensor(out=ang, in0=aim, in1=dt_, op=ALU.mult), rd, [T_])
        sin_of(t3, ang, 0.0, 16, rd, [T_])
        sin_of(t4, ang, 0.5 * math.pi, 16, rd, [T_])
        lr = LPr[:, 0, :]
        li = LPi[:, 0, :]
        dve(lambda h: h.tensor_tensor(out=lr, in0=mag, in1=t4, op=ALU.mult), rd, [T_])
        dve(lambda h: h.tensor_tensor(out=li, in0=mag, in1=t3, op=ALU.mult), rd, [T_])
        for k in range(1, 11):
            pr, pi_, nr_, ni_ = LPr[:, k - 1, :], LPi[:, k - 1, :], LPr[:, k, :], LPi[:, k, :]
            dve(lambda h, pr=pr, pi_=pi_: h.tensor_tensor(out=t5, in0=pr, in1=pr, op=ALU.mult), rd, [T_])
            dve(lambda h, pr=pr, pi_=pi_: h.tensor_tensor(out=t6, in0=pi_, in1=pi_, op=ALU.mult), rd, [T_])
            dve(lambda h, nr_=nr_: h.tensor_tensor(out=nr_, in0=t5, in1=t6, op=ALU.subtract), rd, [T_])
            dve(lambda h, pr=pr, pi_=pi_: h.tensor_tensor(out=t5, in0=pr, in1=pi_, op=ALU.mult), rd, [T_])
            dve(lambda h, ni_=ni_: h.tensor_scalar(out=ni_, in0=t5, scalar1=2.0, scalar2=None, op0=ALU.mult), rd, [T_])
        dve(lambda h: h.tensor_scalar(out=LPn[:], in0=LPi[:], scalar1=-1.0, scalar2=None, op0=ALU.mult), rd, [T_])
        dve(lambda h: h.tensor_scalar(out=t3, in0=lr, scalar1=-1.0, scalar2=None, op0=ALU.add), rd, [T_])
        dve(lambda h: h.tensor_tensor(out=t4, in0=are, in1=are, op=ALU.mult), rd, [T_])
        dve(lambda h: h.tensor_tensor(out=t5, in0=aim, in1=aim, op=ALU.mult), rd, [T_])
        dve(lambda h: h.tensor_tensor(out=t4, in0=t4, in1=t5, op=ALU.add), rd, [T_])
        dve(lambda h: h.reciprocal(out=t4, in_=t4), rd, [T_])
        dve(lambda h: h.tensor_tensor(out=t5, in0=t3, in1=are, op=ALU.mult), rd, [T_])
        dve(lambda h: h.tensor_tensor(out=t6, in0=li, in1=aim, op=ALU.mult), rd, [T_])
        dve(lambda h: h.tensor_tensor(out=t5, in0=t5, in1=t6, op=ALU.add), rd, [T_])
        dve(lambda h: h.tensor_tensor(out=t5, in0=t5, in1=t4, op=ALU.mult), rd, [T_])
        dve(lambda h: h.tensor_tensor(out=t6, in0=li, in1=are, op=ALU.mult), rd, [T_])
        dve(lambda h: h.tensor_tensor(out=t7, in0=t3, in1=aim, op=ALU.mult), rd, [T_])
        dve(lambda h: h.tensor_tensor(out=t6, in0=t6, in1=t7, op=ALU.subtract), rd, [T_])
        dve(lambda h: h.tensor_tensor(out=t6, in0=t6, in1=t4, op=ALU.mult), rd, [T_])
        for (dst, src_d) in [(Bre, b_re_d), (Bim, b_im_d)]:
            src = src_d[l].rearrange("(gp two) p n -> (two p) gp n", two=2)
            P.dma("sp", "misc", lambda h, dst=dst, src=src: h.dma_start(out=dst[:], in_=src), writes=[T_])
        crb = bass.AP(sA, 5, [[128, 128], [8, 16], [0, 16]])
        cib = bass.AP(sA, 6, [[128, 128], [8, 16], [0, 16]])
        dve(lambda h: h.tensor_tensor(out=bbr[:], in0=Bre[:], in1=crb, op=ALU.mult), rd, [T_])
        dve(lambda h: h.tensor_tensor(out=bbi[:], in0=Bim[:], in1=cib, op=ALU.mult), rd, [T_])
        dve(lambda h: h.tensor_tensor(out=bbr[:], in0=bbr[:], in1=bbi[:], op=ALU.subtract), rd, [T_])
        dve(lambda h: h.tensor_tensor(out=bbi[:], in0=Bim[:], in1=crb, op=ALU.mult), rd, [T_])
        dve(lambda h: h.tensor_tensor(out=Bre[:], in0=Bre[:], in1=cib, op=ALU.mult), rd, [T_])
        dve(lambda h: h.tensor_tensor(out=bbi[:], in0=bbi[:], in1=Bre[:], op=ALU.add), rd, [T_])
        for ri, src_d in enumerate([c_re_d, c_im_d]):
            for j in range(4):
                src = src_d[l, 8 * j:8 * j + 8].rearrange("g n p -> (g n) p")
                P.dma("sp", "misc", lambda h, src=src: h.dma_start(out=cstage[:], in_=src), writes=[bf("cstage")])
                b = bank()
                tr(ps[b][0:64, 0:128], cstage[:, :], identf[:], [bf("cstage"), bf("identf")], [psB[b]])
                if ri == 0:
                    dve(lambda h, b=b, j=j: h.tensor_copy(out=CT[0:64, 0, j, :], in_=ps[b][0:64, 0:128]), [psB[b]], [T_])
                else:
                    dve(lambda h, b=b, j=j: h.tensor_scalar(out=CT[0:64, 1, j, :], in0=ps[b][0:64, 0:128],
                                                           scalar1=-1.0, scalar2=None, op0=ALU.mult), [psB[b]], [T_])
        for j in range(4):
            dc = pcol("ssm_d", l * 4 + j)
            dve(lambda h, j=j, dc=dc: h.tensor_scalar(out=Dd[:, j, :], in0=identf[:], scalar1=dc, scalar2=None,
                                                    op0=ALU.mult), [bf("identf"), bf("prm")], [T_])

    def layer_norm(l, gname, bname):
        for t in range(4):
            sl = slice(t * 512, (t + 1) * 512)
            b1 = bank()
            b2 = bank()
            for c in range(8):
                act(lambda h, c=c, sl=sl: h.copy(out=tbf2[:], in_=h32[:, c, sl]), [hB[c][t]], [bf("tbf2")])
                mm(ps[b1][:], onesb[:], tbf2[:], c == 0, c == 7, [bf("tbf2"), bf("onesb")], [psB[b1]], sig=True)
                act(lambda h, c=c, sl=sl: h.activation(out=qk_tm[:], in_=h32[:, c, sl], func=AF.Square),
                    [hB[c][t]], [bf("qk_tm")])
                mm(ps[b2][:], onesb[:], qk_tm[:], c == 0, c == 7, [bf("qk_tm"), bf("onesb")], [psB[b2]], sig=True)
            act(lambda h, b1=b1: h.activation(out=tmpB[:], in_=ps[b1][:], func=AF.Identity, scale=1.0 / D), [psB[b1]], [bf("tmpB")])
            dve(lambda h: h.tensor_tensor(out=tmpC[:], in0=tmpB[:], in1=tmpB[:], op=ALU.mult), [bf("tmpB")], [bf("tmpC")])
            dve(lambda h, b2=b2: h.scalar_tensor_tensor(out=tmpC[:], in0=ps[b2][:], scalar=1.0 / D, in1=tmpC[:],
                                                 op0=ALU.mult, op1=ALU.subtract), [psB[b2], bf("tmpC")], [bf("tmpC")])
            act(lambda h: h.activation(out=tmpC[:], in_=tmpC[:], func=AF.Sqrt, bias=LN_EPS, scale=1.0), [bf("tmpC")], [bf("tmpC")])
            dve(lambda h: h.reciprocal(out=tmpC[:], in_=tmpC[:]), [bf("tmpC")], [bf("tmpC")])
            for c in range(8):
                g_ = pcol(gname, l * 8 + c)
                b_ = pcol(bname, l * 8 + c)
                dve(lambda h, c=c, sl=sl: h.tensor_tensor(out=h32[:, c, sl], in0=h32[:, c, sl], in1=tmpB[:], op=ALU.subtract),
                    [hB[c][t], bf("tmpB")], [hB[c][t]])
                dve(lambda h, c=c, sl=sl: h.tensor_tensor(out=h32[:, c, sl], in0=h32[:, c, sl], in1=tmpC[:], op=ALU.mult),
                    [hB[c][t], bf("tmpC")], [hB[c][t]])
                dve(lambda h, c=c, g_=g_, b_=b_, sl=sl: h.tensor_scalar(out=h32[:, c, sl], in0=h32[:, c, sl], scalar1=g_, scalar2=b_,
                                                                  op0=ALU.mult, op1=ALU.add), [hB[c][t], bf("prm")], [hB[c][t]])
                act(lambda h, c=c, sl=sl: h.copy(out=hb[:, c, sl], in_=h32[:, c, sl]), [hB[c][t]], [hbB[c][t]])

    def rms_half(l, c0, gname):
        bs = [bank() for _ in range(4)]
        for ci in range(4):
            c = c0 + ci
            for t in range(4):
                sl = slice(t * 512, (t + 1) * 512)
                dve(lambda h, c=c, sl=sl: h.tensor_tensor(out=tbf2[:], in0=hb[:, c, sl], in1=hb[:, c, sl], op=ALU.mult),
                    [hbB[c][t]], [bf("tbf2")])
                mm(ps[bs[t]][:], onesb[:], tbf2[:], ci == 0, ci == 3, [bf("tbf2"), bf("onesb")], [psB[bs[t]]], sig=True)
        for t in range(4):
            act(lambda h, t=t: h.activation(out=tmpA[:, t * 512:(t + 1) * 512], in_=ps[bs[t]][:], func=AF.Sqrt,
                                            bias=RMS_EPS, scale=1.0 / 512), [psB[bs[t]]], [bf("tmpA")])
        dve(lambda h: h.reciprocal(out=tmpA[:], in_=tmpA[:]), [bf("tmpA")], [bf("tmpA")])
        for ci in range(4):
            c = c0 + ci
            g_ = pcol(gname, l * 4 + ci)
            rd = [hbB[c][t] for t in range(4)]
            dve(lambda h, c=c, g_=g_: h.scalar_tensor_tensor(out=hb[:, c, :], in0=hb[:, c, :], scalar=g_, in1=tmpA[:],
                                                             op0=ALU.mult, op1=ALU.mult), rd + [bf("tmpA"), bf("prm")], rd)

    def block(s, l, first):
        if STAGE < 1:
            return
        for piece in range(2):
            slot = 0
            load_w(wb[:, slot, :].rearrange("p (a n) -> p a n", a=8),
                   w_in_d[l][:, piece * 512:(piece + 1) * 512].rearrange("(a p) n -> p a n", p=128), bf("wb%d" % slot))
            wv = wb[:, slot, :].rearrange("p (a n) -> p a n", a=8)
            dstT = qT if piece == 0 else kT
            qkbufs = [(qk_tm, bf("qk_tm")), (tbf2, bf("tbf2"))]

            def qk_tail(tt, dstT=dstT):
                qb, qB = qkbufs[tt % 2]
                b2 = bank()
                for c in range(4):
                    tr(psb[b2][:, c * 128:(c + 1) * 128], qb[:, c * 128:(c + 1) * 128], identb[:],
                       [qB, bf("identb")], [psB[b2]])
                act(lambda h, b2=b2, tt=tt, dstT=dstT: h.copy(
                    out=dstT[:, :, tt * 128:(tt + 1) * 128],
                    in_=psb[b2][:, 0:512].rearrange("p (c n) -> p c n", c=4)), [psB[b2]], [R1B])

            for tt in range(16):
                t4 = tt // 4
                b = bank()
                for dk in range(8):
                    mm(ps[b][:], hb[:, dk, tt * 128:(tt + 1) * 128], wv[:, dk, :], dk == 0, dk == 7,
                       [hbB[dk][t4], bf("wb%d" % slot)], [psB[b]])
                if tt > 0:
                    qk_tail(tt - 1)
                qb, qB = qkbufs[tt % 2]
                pv = ps[b][:].rearrange("p (h e) -> p h e", h=8)
                q1, q2 = pv[:, :, 0:32], pv[:, :, 32:64]
                cb = bass.AP(cosT, tt * 32, [[512, 128], [0, 8], [1, 32]])
                sbb = bass.AP(sinT, tt * 32, [[512, 128], [0, 8], [1, 32]])
                ta = tmpB[:, 0:256].rearrange("p (h e) -> p h e", h=8)
                tb_ = tmpB[:, 256:512].rearrange("p (h e) -> p h e", h=8)
                ov = qb[:].rearrange("p (h e) -> p h e", h=8)
                rdc = [psB[b], bf("rope"), bf("ropes")]
                dve(lambda h, q1=q1, cb=cb, ta=ta: h.tensor_tensor(out=ta, in0=q1, in1=cb, op=ALU.mult), rdc, [bf("tmpB")])
                dve(lambda h, q2=q2, sbb=sbb, tb_=tb_: h.tensor_tensor(out=tb_, in0=q2, in1=sbb, op=ALU.mult), rdc, [bf("tmpB")])
                dve(lambda h, ta=ta, tb_=tb_, ov=ov: h.tensor_tensor(out=ov[:, :, 0:32], in0=ta, in1=tb_, op=ALU.subtract),
                    [bf("tmpB")], [qB])
                dve(lambda h, q1=q1, sbb=sbb, ta=ta: h.tensor_tensor(out=ta, in0=q1, in1=sbb, op=ALU.mult), rdc, [bf("tmpB")])
                dve(lambda h, q2=q2, cb=cb, tb_=tb_: h.tensor_tensor(out=tb_, in0=q2, in1=cb, op=ALU.mult), rdc, [bf("tmpB")])
                dve(lambda h, ta=ta, tb_=tb_, ov=ov: h.tensor_tensor(out=ov[:, :, 32:64], in0=ta, in1=tb_, op=ALU.add),
                    [bf("tmpB")], [qB])
            qk_tail(15)
        load_w(wb[:, 0, :].rearrange("p (a n) -> p a n", a=8),
               w_in_d[l][:, 1024:1536].rearrange("(a p) n -> p a n", p=128), bf("wb0"))
        wv = wb[:, 0, :].rearrange("p (a n) -> p a n", a=8)
        for tt in range(16):
            t4 = tt // 4
            b = bank()
            for dk in range(8):
                mm(ps[b][:], hb[:, dk, tt * 128:(tt + 1) * 128], wv[:, dk, :], dk == 0, dk == 7,
                   [hbB[dk][t4], bf("wb0")], [psB[b]])
            vb, vB = (qk_tm, bf("qk_tm")) if tt % 2 == 0 else (tbf2, bf("tbf2"))
            act(lambda h, b=b, vb=vb: h.copy(out=vb[:], in_=ps[b][:]), [psB[b]], [vB])
            P.dma("sp", "v", lambda h, tt=tt, vb=vb: h.dma_start(out=vscr[tt * 128:(tt + 1) * 128, :], in_=vb[:]),
                  reads=[vB], writes=[bf("vscr")])
        load_w(wb[:, 0, :].rearrange("p (a n) -> p a n", a=8),
               w_in_d[l][:, 1536:2048].rearrange("(a p) n -> p a n", p=128), bf("wb0"))
        wv = wb[:, 0, :].rearrange("p (a n) -> p a n", a=8)
        for c in range(4):
            for t in range(4):
                b = bank()
                for dk in range(8):
                    mm(ps[b][:], wv[:, dk, c * 128:(c + 1) * 128], hb[:, dk, t * 512:(t + 1) * 512], dk == 0, dk == 7,
                       [hbB[dk][t], bf("wb0")], [psB[b]])
                act(lambda h, b=b, c=c, t=t: h.copy(out=uT[:, c, t * 512:(t + 1) * 512], in_=ps[b][:]), [psB[b]], [uTB[c][t]])

        if STAGE < 2:
            return
        def tokset(br, i):
            if br == 0:
                return slice(128 * i, 128 * i + 128)
            if br == 1:
                r, J = i // 4, i % 4
                return slice(512 * J + r, 512 * J + 512, 4)
            return slice(i, S, 16)

        def vslot(br, i):
            if br == 0:
                return i
            if br == 1:
                r, J = i // 4, i % 4
                return 4 * J + r
            return i

        subseqs = [
            (0, [list(range(16))]),
            (1, [[r * 4 + J for J in range(4)] for r in range(4)]),
            (2, [[r] for r in range(16)]),
        ]
        for hh in range(8):
            hs = slice((hh % 2) * 64, (hh % 2) * 64 + 64)
            cch = hh // 2
            vcol = vscr[:, 64 * hh:64 * hh + 64]
            srcs = [vcol.rearrange("(j p) e -> p j e", p=128),
                    vcol.rearrange("(J p r) e -> p J r e", J=4, r=4),
                    vcol.rearrange("(p r) e -> p r e", r=16)]
            items = []
            for (br, seqs) in subseqs:
                for seq in seqs:
                    for ii, i in enumerate(seq):
                        items.append((br, i, seq[ii + 1] if ii + 1 < len(seq) else None, ii > 0))
            st_pt = {}

            def stage_a(n):
                br, i, nxt, _ = items[n]
                ks = tokset(br, i)
                b = bank()
                mm(ps[b][:, 0:128], kT[hs, cch, ks], qT[hs, cch, ks], True, True, [R1B], [psB[b]], sig=nxt is None)
                if nxt is not None:
                    mm(ps[b][:, 128:256], kT[hs, cch, ks], qT[hs, cch, tokset(br, nxt)], True, True,
                       [R1B], [psB[b]], sig=True)
                w_ = 256 if nxt is not None else 128
                pt = PT[n % 3]
                ptB = bf("PT%d" % (n % 3))
                act(lambda h, b=b, pt=pt, w_=w_: h.activation(out=pt[:, 0:w_], in_=ps[b][:, 0:w_], func=AF.Exp, scale=0.125),
                    [psB[b]], [ptB])
                (dve if os.environ.get("KMASK", "dve") == "dve" else pool)(lambda h, pt=pt, w_=w_: h.tensor_tensor(out=pt[:, 0:w_], in0=pt[:, 0:w_], in1=maskb[:, 0:w_], op=ALU.mult),
                     [ptB, bf("maskb")], [ptB])
                st_pt[n] = (pt, ptB)

            def stage_b(n):
                br, i, nxt, has_prev = items[n]
                if n == 0 or items[n - 1][0] != br:
                    if br == 1:
                        pairs = [(vh[:, 0, 4 * J:4 * J + 4, 0:64], srcs[1][:, J]) for J in range(4)]
                    else:
                        pairs = [(vh[:, 0, 8 * hf:8 * hf + 8, 0:64], srcs[br][:, 8 * hf:8 * hf + 8]) for hf in range(2)]
                    for (dst, src) in pairs:
                        P.dma("sp", "vh", lambda h, dst=dst, src=src: h.dma_start(out=dst, in_=src),
                              reads=[bf("vscr")], writes=[bf("vh0")])
                ks = tokset(br, i)
                pt, ptB = st_pt[n]
                bo = bank()
                if has_prev:
                    ppt, pptB = st_pt[n - 1]
                    mm(ps[bo][:, 0:128], vh[:, 0, vslot(br, items[n - 1][1]), :], ppt[:, 128:256], True, False,
                       [bf("vh0"), pptB], [psB[bo]], sig=False)
                mm(ps[bo][:, 0:128], vh[:, 0, vslot(br, i), :], pt[:, 0:128], not has_prev, True,
                   [bf("vh0"), ptB], [psB[bo]], sig=True)
                if br == 0:
                    dve(lambda h, bo=bo, ks=ks: h.tensor_copy(out=Oacc[:, ks], in_=ps[bo][:, 0:128]), [psB[bo]], [R2B])
                else:
                    dve(lambda h, bo=bo, ks=ks: h.tensor_tensor(out=Oacc[:, ks], in0=Oacc[:, ks], in1=ps[bo][:, 0:128], op=ALU.add),
                        [psB[bo], R2B], [R2B])

            stage_a(0)
            for n in range(len(items)):
                if n + 1 < len(items):
                    stage_a(n + 1)
                stage_b(n)
            dve(lambda h: h.tensor_copy(out=tmpA[0:64, :], in_=Oacc[64:128, :]), [R2B], [bf("tmpA")])
            dve(lambda h: h.reciprocal(out=tmpA[0:64, :], in_=tmpA[0:64, :]), [bf("tmpA")], [bf("tmpA")])
            wr = [hbB[cch][t] for t in range(4)]
            dve(lambda h, hs=hs, cch=cch: h.tensor_tensor(out=hb[hs, cch, :], in0=Oacc[0:64, :], in1=tmpA[0:64, :], op=ALU.mult),
                [R2B, bf("tmpA")], wr)
        rms_half(l, 0, "attn_gain")

        if STAGE < 3:
            return
        ssm_setup(l)
        T_ = bf("ssmtab")
        bank_pool[0] = [4, 5, 6, 7]
        Xs = [X0, X1]
        XqB = [R1B, bf("X1B")]

        def ssm_bbt_bu(gp, q):
            j = gp // 4
            gl0 = (2 * gp) % 8
            Xq = Xs[q]
            wrs = [XqB[q]] if q == 0 else [XqB[1], R1B]
            for ri, srcB in enumerate([bbr, bbi]):
                dve(lambda h: h.memset(Zf[:, 0, :], 0.0), [], [bf("Zf")])
                dve(lambda h, srcB=srcB: h.tensor_copy(out=Zf[0:64, 0, 16 * gl0:16 * gl0 + 16], in_=srcB[0:64, gp, :]), [T_], [bf("Zf")])
                dve(lambda h, srcB=srcB: h.tensor_copy(out=Zf[64:128, 0, 16 * gl0 + 16:16 * gl0 + 32], in_=srcB[64:128, gp, :]), [T_], [bf("Zf")])
                b = bank()
                tr(ps[b][:, 0:128], Zf[:, 0, :], identf[:], [bf("Zf"), bf("identf")], [psB[b]])
                act(lambda h, b=b, ri=ri: h.copy(out=BbT[:, ri, :], in_=ps[b][:, 0:128]), [psB[b]], [bf("BbT")])
            for t in range(4):
                sl = slice(t * 512, (t + 1) * 512)
                for ri in range(2):
                    b = bank()
                    mm(ps[b][:], BbT[:, ri, :], uT[:, j, sl], True, True, [bf("BbT"), uTB[j][t]], [psB[b]])
                    act(lambda h, b=b, sl=sl, ri=ri: h.copy(out=Xq[:, ri, sl], in_=ps[b][:]), [psB[b]], wrs)

        def ssm_combine(gp, q, T, Sx, k):
            Xq = Xs[q]
            ar = LPr[:, k, gp:gp + 1]
            ai = LPi[:, k, gp:gp + 1]
            an = LPn[:, k, gp:gp + 1]
            rdx = [XqB[q], T_]
            wr = [XqB[q]]
            dve(lambda h: h.scalar_tensor_tensor(
                out=Xq[:, :, T], in0=Xq[:, :, Sx], scalar=ar, in1=Xq[:, :, T], op0=ALU.mult, op1=ALU.add), rdx, wr)
            dve(lambda h: h.scalar_tensor_tensor(
                out=Xq[:, 0, T], in0=Xq[:, 1, Sx], scalar=an, in1=Xq[:, 0, T], op0=ALU.mult, op1=ALU.add), rdx, wr)
            dve(lambda h: h.scalar_tensor_tensor(
                out=Xq[:, 1, T], in0=Xq[:, 0, Sx], scalar=ai, in1=Xq[:, 1, T], op0=ALU.mult, op1=ALU.add), rdx, wr)

        def ssm_cm_y(gp, q):
            j = gp // 4
            gl0 = (2 * gp) % 8
            Xq = Xs[q]
            dve(lambda h: h.memset(Zf[:], 0.0), [], [bf("Zf")])
            for ri in range(2):
                dve(lambda h, ri=ri: h.tensor_copy(out=Zf[0:64, ri, 16 * gl0:16 * gl0 + 16], in_=CT[0:64, ri, j, 16 * gl0:16 * gl0 + 16]), [T_], [bf("Zf")])
                dve(lambda h, ri=ri: h.tensor_copy(out=Zf[64:128, ri, 16 * gl0 + 16:16 * gl0 + 32], in_=CT[0:64, ri, j, 16 * gl0 + 16:16 * gl0 + 32]), [T_], [bf("Zf")])
            dve(lambda h: h.tensor_copy(out=Cm[:], in_=Zf[:]), [bf("Zf")], [bf("Cm")])
            act(lambda h: h.copy(out=XB[:], in_=Xq[:]), [XqB[q]], [R2B])
            for t in range(4):
                sl = slice(t * 512, (t + 1) * 512)
                first_gp = (gp % 4 == 0)
                last_gp = (gp % 4 == 3)
                mm(ps[t][:], Cm[:, 0, :], XB[:, 0, sl], first_gp, False, [bf("Cm"), R2B], [psB[t]], sig=False)
                mm(ps[t][:], Cm[:, 1, :], XB[:, 1, sl], False, False, [bf("Cm"), R2B], [psB[t]], sig=not last_gp)
                if last_gp:
                    mm(ps[t][:], Dd[:, j, :], uT[:, j, sl], False, True, [T_, uTB[j][t]], [psB[t]], sig=True)
                    act(lambda h, t=t: h.activation(out=tmpB[:], in_=ps[t][:], func=AF.Square), [psB[t]], [bf("tmpB")])
                    dve(lambda h: h.tensor_scalar(out=tmpB[:], in0=tmpB[:], scalar1=0.044715, scalar2=1.0, op0=ALU.mult, op1=ALU.add),
                        [bf("tmpB")], [bf("tmpB")])
                    dve(lambda h, t=t: h.tensor_tensor(out=tmpB[:], in0=tmpB[:], in1=ps[t][:], op=ALU.mult), [bf("tmpB"), psB[t]], [bf("tmpB")])
                    act(lambda h: h.activation(out=tmpB[:], in_=tmpB[:], func=AF.Sigmoid, scale=1.5957691216057308), [bf("tmpB")], [bf("tmpB")])
                    dve(lambda h, t=t, sl=sl: h.tensor_tensor(out=uT[:, j, sl], in0=tmpB[:], in1=ps[t][:], op=ALU.mult),
                        [bf("tmpB"), psB[t]], [uTB[j][t]])

        levels = []
        for k in range(11):
            st_, d = 1 << (k + 1), 1 << k
            levels.append((slice(st_ - 1, S, st_), slice(d - 1, S, st_), k))
        for k in range(9, -1, -1):
            st_, d = 1 << (k + 1), 1 << k
            levels.append((slice(st_ + d - 1, S, st_), slice(st_ - 1, S - d, st_), k))
        for gp0 in range(0, 16, 2):
            for q in range(2):
                ssm_bbt_bu(gp0 + q, q)
            for (T, Sx, k) in levels:
                for q in range(2):
                    ssm_combine(gp0 + q, q, T, Sx, k)
            for q in range(2):
                ssm_cm_y(gp0 + q, q)
        bank_pool[0] = list(range(8))
        load_w(wb[:, 0, 0:2048].rearrange("p (a n) -> p a n", a=4),
               w_glu_d[l].rearrange("(a p) n -> p a n", p=128), bf("wb0"))
        wg = wb[:, 0, 0:2048].rearrange("p (a n) -> p a n", a=4)
        for co in range(4):
            bg = pcol("b_glu", l * 4 + co)
            for t in range(4):
                sl = slice(t * 512, (t + 1) * 512)
                b = bank()
                for ci in range(4):
                    mm(ps[b][:], wg[:, ci, co * 128:(co + 1) * 128], uT[:, ci, sl], ci == 0, ci == 3, [bf("wb0"), uTB[ci][t]], [psB[b]])
                act(lambda h, b=b, bg=bg: h.activation(out=tmpB[:], in_=ps[b][:], func=AF.Sigmoid, bias=bg, scale=1.0),
                    [psB[b], bf("prm")], [bf("tmpB")])
                dve(lambda h, co=co, sl=sl: h.tensor_tensor(out=hb[:, 4 + co, sl], in0=uT[:, co, sl], in1=tmpB[:], op=ALU.mult),
                    [bf("tmpB"), uTB[co][t]], [hbB[4 + co][t]])
        rms_half(l, 4, "ssm_gain")

        if dbg == "mixed" and s == 0 and l == 0:
            for c in range(8):
                dve(lambda h, c=c: h.tensor_copy(out=tmpA[:], in_=hb[:, c, :]), [hbB[c][t] for t in range(4)], [bf("tmpA")])
                P.dma("sp", "st", lambda h, c=c: h.dma_start(out=dbg_d[:, c, :], in_=tmpA[:]), reads=[bf("tmpA")])

        if STAGE < 4:
            return
        for co in range(8):
            if co % 4 == 0:
                half = co // 4
                load_w(wb[:, 0, :].rearrange("p (a n) -> p a n", a=8),
                       w_out_d[l][:, half * 512:(half + 1) * 512].rearrange("(a p) n -> p a n", p=128), bf("wb0"))
            wv = wb[:, 0, :].rearrange("p (a n) -> p a n", a=8)
            cl = co % 4
            bo_ = pcol("b_out", l * 8 + co)
            for t in range(4):
                sl = slice(t * 512, (t + 1) * 512)
                b = bank()
                for dk in range(8):
                    mm(ps[b][:], wv[:, dk, cl * 128:(cl + 1) * 128], hb[:, dk, sl], dk == 0, dk == 7,
                       [bf("wb0"), hbB[dk][t]], [psB[b]])
                act(lambda h, b=b, bo_=bo_: h.activation(out=tmpB[:], in_=ps[b][:], func=AF.Identity, bias=bo_, scale=1.0),
                    [psB[b], bf("prm")], [bf("tmpB")])
                dve(lambda h, co=co, sl=sl: h.scalar_tensor_tensor(out=h32[:, co, sl], in0=h32[:, co, sl], scalar=ALPHA, in1=tmpB[:],
                                                                 op0=ALU.mult, op1=ALU.add), [hB[co][t], bf("tmpB")], [hB[co][t]])
        P.barrier()
        layer_norm(l, "ln1_g", "ln1_b")
        if dbg == "ln1" and s == 0 and l == 0:
            for c in range(8):
                P.dma("sp", "st", lambda h, c=c: h.dma_start(out=dbg_d[:, c, :], in_=h32[:, c, :]), reads=[hB[c][t] for t in range(4)])

        if STAGE < 5:
            return
        P.barrier()
        W1e = [R1b[:, 0:4096].rearrange("p (c n) -> p c n", c=8), R1b[:, 4096:8192].rearrange("p (c n) -> p c n", c=8)]
        W2e = [R1b[:, 8192:12288].rearrange("p (c n) -> p c n", c=4), R1b[:, 12288:16384].rearrange("p (c n) -> p c n", c=4)]

        def load_pass(pp):
            bi = pp % 2
            load_w(W1e[bi], w_ff1_d[l][:, pp * 512:(pp + 1) * 512].rearrange("(a p) n -> p a n", p=128), bf("W1e%d" % bi))
            load_w(W2e[bi], w_ff2_d[l][pp * 512:(pp + 1) * 512, :].rearrange("(a p) n -> p a n", p=128), bf("W2e%d" % bi))

        load_pass(0)
        tile_ctr = 0
        for pp in range(8):
            if pp + 1 < 8:
                load_pass(pp + 1)
            bi = pp % 2
            for t in range(4):
                sl = slice(t * 512, (t + 1) * 512)
                ab = (tile_ctr % 2) * 4
                tile_ctr += 1
                for fc in range(4):
                    b = bank()
                    b1_ = pcol("b_ff1", l * 32 + pp * 4 + fc)
                    for dk in range(8):
                        mm(ps[b][:], W1e[bi][:, dk, fc * 128:(fc + 1) * 128], hb[:, dk, sl], dk == 0, dk == 7,
                           [bf("W1e%d" % bi), hbB[dk][t]], [psB[b]])
                    act(lambda h, b=b, b1_=b1_: h.activation(out=tbf2[:], in_=ps[b][:], func=AF.Relu, bias=b1_, scale=1.0),
                        [psB[b], bf("prm")], [bf("tbf2")])
                    dve(lambda h, fc=fc, ab=ab: h.tensor_tensor(out=aT[:, ab + fc, :], in0=tbf2[:], in1=tbf2[:], op=ALU.mult),
                        [bf("tbf2")], [bf("aT%d" % (ab + fc))])
                for co in range(8):
                    b = bank()
                    for fc in range(4):
                        mm(ps[b][:], W2e[bi][:, fc, co * 128:(co + 1) * 128], aT[:, ab + fc, :], fc == 0, fc == 3,
                           [bf("W2e%d" % bi), bf("aT%d" % (ab + fc))], [psB[b]])
                    if pp == 0:
                        b2_ = pcol("b_ff2", l * 8 + co)
                        act(lambda h, b=b, b2_=b2_: h.activation(out=tmpB[:], in_=ps[b][:], func=AF.Identity, bias=b2_, scale=1.0),
                            [psB[b], bf("prm")], [bf("tmpB")])
                        dve(lambda h, co=co, sl=sl: h.scalar_tensor_tensor(out=h32[:, co, sl], in0=h32[:, co, sl], scalar=ALPHA, in1=tmpB[:],
                                                                         op0=ALU.mult, op1=ALU.add), [hB[co][t], bf("tmpB")], [hB[co][t]])
                    else:
                        dve(lambda h, b=b, co=co, sl=sl: h.tensor_tensor(out=h32[:, co, sl], in0=h32[:, co, sl], in1=ps[b][:], op=ALU.add),
                            [hB[co][t], psB[b]], [hB[co][t]])
        P.barrier()
        layer_norm(l, "ln2_g", "ln2_b")
        P.barrier()
        P.new_epoch()

    for s in range(NSEQ):
        if "rope" not in KSKIP: P.dma("sp", "misc", lambda h, s=s: h.dma_start(out=posi[:], in_=pos_d[s].rearrange("(j p) -> p j", p=128)),
              writes=[bf("posi")])
        if "rope" not in KSKIP:
            dve(lambda h: h.tensor_copy(out=posf[:], in_=posi[:]), [bf("posi")], [bf("posf")])
        for tt in (range(16) if "rope" not in KSKIP else []):
            pc = posf[:, tt:tt + 1]
            dve(lambda h, tt=tt, pc=pc: h.tensor_scalar(out=cosT[:, tt, :], in0=invf[:], scalar1=pc, scalar2=None, op0=ALU.mult),
                [bf("posf"), bf("invf")], [bf("rope")])
        rr = [bf("rope")]
        if "rope" not in KSKIP:
            sin_of(sinT[:], cosT[:], 0.0, 512, rr, [bf("ropes")])
            sin_of(cosT[:], cosT[:], 0.5 * math.pi, 512, rr + [bf("ropes")], rr)
        for tt in range(int(os.environ.get("KXL", "16"))):
            t4 = tt // 4
            tq = 0 if os.environ.get("KXOFF") else tt
            P.dma("sp", "ld", lambda h, s=s, tt=tq: h.dma_start(out=xtm, in_=x_d[s, tt * 128:(tt + 1) * 128, :]),
                  writes=[bf("tmpA")])
            for g in range(2):
                b = bank()
                for c4 in range(4):
                    c = g * 4 + c4
                    tr(ps[b][:, c4 * 128:(c4 + 1) * 128], xtm[:, c * 128:(c + 1) * 128], identf[:], [bf("tmpA"), bf("identf")], [psB[b]])
                wr = [hB[g * 4 + c4][t4] for c4 in range(4)]
                wr2 = [hbB[g * 4 + c4][t4] for c4 in range(4)]
                pv = ps[b][:].rearrange("p (c n) -> p c n", c=4)
                if "h32copy" not in KSKIP:
                    dve(lambda h, pv=pv, g=g, tt=tt: h.tensor_copy(out=h32[:, g * 4:g * 4 + 4, tt * 128:(tt + 1) * 128], in_=pv), [psB[b]], wr)
                if "hbcopy" not in KSKIP:
                    act(lambda h, pv=pv, g=g, tt=tt: h.copy(out=hb[:, g * 4:g * 4 + 4, tt * 128:(tt + 1) * 128], in_=pv), [psB[b]], wr2)
        for l in range(NL):
            block(s, l, False)
        for tt in range(int(os.environ.get("KXS", "16"))):
            t4 = tt // 4
            for g in range(2):
                b = bank()
                for c4 in range(4):
                    c = g * 4 + c4
                    tr(ps[b][:, c4 * 128:(c4 + 1) * 128], h32[:, c, tt * 128:(tt + 1) * 128], identf[:], [hB[c][t4], bf("identf")], [psB[b]])
                dve(lambda h, b=b, g=g: h.tensor_copy(out=xtm[:, g * 512:(g + 1) * 512], in_=ps[b][:]), [psB[b]], [bf("tmpA")])
            P.dma("sp", "st", lambda h, s=s, tt=tt: h.dma_start(out=out_d[s, tt * 128:(tt + 1) * 128, :], in_=xtm),
                  reads=[bf("tmpA")])
        P.barrier()
    P.barrier()
    with nc.allow_non_contiguous_dma(reason="small strided param loads"), nc.Block() as blk:
        P.emit_all(blk)
    es.close()
    return nc, P


def consts():
    ident = np.eye(128, dtype=np.float32)
    k = np.arange(128)[:, None]
    q = np.arange(128)[None, :]
    mask = np.concatenate([(k <= q), (k >= q)], axis=1).astype(np.float32)
    invf = (10000.0 ** (-np.arange(32, dtype=np.float32) * 2.0 / 64.0)).astype(np.float32)
    invf = np.broadcast_to(invf[None, :], (128, 32)).copy()
    return {"c_ident": ident, "c_mask": mask, "c_invf": invf}


WNAMES = ["w_in", "attn_gain", "ssm_gain", "ssm_a_re", "ssm_a_im", "ssm_log_dt", "ssm_b_re", "ssm_b_im",
          "ssm_c_re", "ssm_c_im", "ssm_d", "w_glu", "b_glu", "w_out", "b_out", "ln1_g", "ln1_b",
          "w_ff1", "b_ff1", "w_ff2", "b_ff2", "ln2_g", "ln2_b"]

_CACHE = {}

LAUNCH_NL = 4
LAUNCH_NSEQ = 4


def kernel(**inputs):
    x = np.ascontiguousarray(inputs["x"], dtype=np.float32)
    pos = np.ascontiguousarray(inputs["positions"], dtype=np.int32)
    per_core = x.shape[0] // NCORES
    key = (LAUNCH_NL, LAUNCH_NSEQ)
    if key not in _CACHE:
        _CACHE[key] = build(LAUNCH_NL, LAUNCH_NSEQ)[0]
    nc = _CACHE[key]
    cs = consts()
    h = x
    for l0 in range(0, DEPTH, LAUNCH_NL):
        base = {n: np.ascontiguousarray(inputs[n][l0:l0 + LAUNCH_NL], dtype=np.float32) for n in WNAMES}
        base.update(cs)
        newh = np.empty_like(h)
        for s0 in range(0, per_core, LAUNCH_NSEQ):
            in_maps = []
            for c in range(NCORES):
                m = dict(base)
                lo = c * per_core + s0
                m["x"] = np.ascontiguousarray(h[lo:lo + LAUNCH_NSEQ])
                m["positions"] = np.ascontiguousarray(pos[lo:lo + LAUNCH_NSEQ])
                in_maps.append(m)
            res = run_bass_kernel_spmd(nc, in_maps, core_ids=list(range(NCORES)))
            for c in range(NCORES):
                lo = c * per_core + s0
                newh[lo:lo + LAUNCH_NSEQ] = res.results[c]["out"]
        h = newh
    return h
```

```python
import math
import os
STAGE = int(os.environ.get("KSTAGE", "9"))
KSKIP = set(os.environ.get("KSKIP", "").split(","))
from contextlib import ExitStack
import numpy as np
import concourse.bass as bass
import concourse.mybir as mybir
from concourse.bass_utils import run_bass_kernel_spmd

F32 = mybir.dt.float32
BF16 = mybir.dt.bfloat16
I32 = mybir.dt.int32
ALU = mybir.AluOpType
AF = mybir.ActivationFunctionType
ENGS = ("pe", "act", "dve", "pool", "sp")

D = 1024
S = 2048
DEPTH = 4
NCORES = 8
ALPHA = (2.0 * DEPTH) ** 0.25
LN_EPS = 1e-5
RMS_EPS = 1e-6
TWO_PI = 2.0 * math.pi


class Buf:
    __slots__ = ("name", "w", "r", "excl")

    def __init__(self, name, excl=False):
        self.name = name
        self.w = None
        self.r = {}
        self.excl = excl


class Prog:
    def __init__(self, nc, es):
        self.nc = nc
        self.es = es
        self.q = {e: [] for e in ENGS}
        self.sems = {}
        self.cnt = {}
        self.cur = {}
        self.dead = set()
        self.epoch = -1
        self.waited = {e: {} for e in ENGS}
        self.dma_cnt = {}
        self.dma_keys = []
        self.dma_rr = 0
        self.n_inst = 0
        self.new_epoch()

    def new_epoch(self):
        for k in self.cur.values():
            self.dead.add(k)
        self.epoch += 1
        for e in ENGS:
            k = "%s#%d" % (e, self.epoch)
            self.cur[e] = k
            self.cnt[k] = 0
            self.sems[k] = self.es.enter_context(self.nc.semaphore("s_%s_%d" % (e, self.epoch)))

    def _need(self, e, deps):
        out = []
        ce = self.cur[e]
        for (k, v) in deps:
            if k in self.dead:
                continue
            if k == ce and (e == "pe" or v > self.cnt[ce]):
                continue
            if self.waited[e].get(k, 0) >= v:
                continue
            self.waited[e][k] = v
            out.append((k, v))
        return out

    def _deps(self, reads, writes):
        deps = set()
        for b in reads:
            if b.w is not None:
                deps.add(b.w)
            if b.excl:
                for kv in b.r.items():
                    deps.add(kv)
        for b in writes:
            if b.w is not None:
                deps.add(b.w)
            for kv in b.r.items():
                deps.add(kv)
        return deps

    def op(self, e, fn, reads=(), writes=(), sig=True):
        waits = self._need(e, self._deps(reads, writes))
        ce = self.cur[e]
        if sig:
            self.cnt[ce] += 1
        tick = self.cnt[ce] if sig else self.cnt[ce] + 1
        sem = self.sems[ce]
        sems = self.sems

        def emit(h):
            for (k, v) in waits:
                h.wait_ge(sems[k], v)
            ins = fn(h)
            if sig:
                ins.then_inc(sem, 1)
        self.q[e].append(emit)
        self.n_inst += 1
        for b in reads:
            if b.excl:
                b.w = (ce, tick)
                b.r = {}
            elif b.r.get(ce, 0) < tick:
                b.r[ce] = tick
        for b in writes:
            b.w = (ce, tick)
            b.r = {}
        return tick

    def dma(self, qe, semkey, fn, reads=(), writes=()):
        semkey = self.dma_keys[self.dma_rr % len(self.dma_keys)]
        self.dma_rr += 1
        deps = self._deps(reads, writes)
        prev = self.dma_cnt.get(semkey, 0)
        if prev > 0:
            deps.add((semkey, prev))
        waits = self._need(qe, deps)
        self.dma_cnt[semkey] = prev + 16
        val = self.dma_cnt[semkey]
        sems = self.sems
        sem = sems[semkey]

        def emit(h):
            for (k, v) in waits:
                h.wait_ge(sems[k], v)
            fn(h).then_inc(sem, 16)
        self.q[qe].append(emit)
        self.n_inst += 1
        for b in reads:
            if b.r.get(semkey, 0) < val:
                b.r[semkey] = val
        for b in writes:
            b.w = (semkey, val)
            b.r = {}
        return val

    def wait_all(self, e):
        deps = [(self.cur[k], self.cnt[self.cur[k]]) for k in ENGS if self.cnt[self.cur[k]] > 0]
        deps += list(self.dma_cnt.items())
        waits = self._need(e, deps)
        sems = self.sems

        def emit(h):
            for (k, v) in waits:
                h.wait_ge(sems[k], v)
        self.q[e].append(emit)

    def barrier(self):
        for e in ENGS:
            self.wait_all(e)

    def emit_all(self, block):
        prog = self

        @block.tensor
        def _(h):
            for f in prog.q["pe"]:
                f(h)

        @block.scalar
        def _(h):
            for f in prog.q["act"]:
                f(h)

        @block.vector
        def _(h):
            for f in prog.q["dve"]:
                f(h)

        @block.gpsimd
        def _(h):
            for f in prog.q["pool"]:
                f(h)

        @block.sync
        def _(h):
            for f in prog.q["sp"]:
                f(h)


def build(NL, NSEQ, dbg=None):
    nc = bass.Bass("TRN2", target_bir_lowering=False)
    es = ExitStack()

    MINIO = os.environ.get("KMINIO", "")

    def din(name, shape, dt=F32):
        if MINIO and name not in ("x", "positions", "c_ident", "c_mask", "c_invf"):
            return None
        return nc.dram_tensor(name, list(shape), dt, kind="ExternalInput").ap()

    x_d = din("x", [NSEQ, S, D])
    pos_d = din("positions", [NSEQ, S], I32)
    w_in_d = din("w_in", [NL, D, 2048])
    attn_gain_d = din("attn_gain", [NL, 512])
    ssm_gain_d = din("ssm_gain", [NL, 512])
    a_re_d = din("ssm_a_re", [NL, 32, 64])
    a_im_d = din("ssm_a_im", [NL, 32, 64])
    log_dt_d = din("ssm_log_dt", [NL, 32])
    b_re_d = din("ssm_b_re", [NL, 32, 64, 16])
    b_im_d = din("ssm_b_im", [NL, 32, 64, 16])
    c_re_d = din("ssm_c_re", [NL, 32, 16, 64])
    c_im_d = din("ssm_c_im", [NL, 32, 16, 64])
    d_d = din("ssm_d", [NL, 32, 16])
    w_glu_d = din("w_glu", [NL, 512, 512])
    b_glu_d = din("b_glu", [NL, 512])
    w_out_d = din("w_out", [NL, D, D])
    b_out_d = din("b_out", [NL, D])
    ln1_g_d = din("ln1_g", [NL, D])
    ln1_b_d = din("ln1_b", [NL, D])
    w_ff1_d = din("w_ff1", [NL, D, 4096])
    b_ff1_d = din("b_ff1", [NL, 4096])
    w_ff2_d = din("w_ff2", [NL, 4096, D])
    b_ff2_d = din("b_ff2", [NL, D])
    ln2_g_d = din("ln2_g", [NL, D])
    ln2_b_d = din("ln2_b", [NL, D])
    ident_d = din("c_ident", [128, 128])
    mask_d = din("c_mask", [128, 256])
    invf_d = din("c_invf", [128, 32])
    out_d = nc.dram_tensor("out", [NSEQ, S, D], F32, kind="ExternalOutput").ap()
    vscr = None if MINIO else nc.dram_tensor("vscr", [S, 512], BF16, kind="ExternalOutput").ap()
    dbg_d = None
    if dbg:
        dbg_d = nc.dram_tensor("dbg", [128, 8, S], F32, kind="ExternalOutput").ap()

    def sb(name, shape, dt):
        return es.enter_context(nc.sbuf_tensor(name, list(shape), dt))

    P = Prog(nc, es)
    for i in range(int(os.environ.get("KNDMA", "8"))):
        k = "dma%d" % i
        P.sems[k] = es.enter_context(nc.semaphore("d_" + k))
        P.dma_keys.append(k)

    h32 = sb("h32", [128, 8, S], F32)
    hb = sb("hb", [128, 8, S], BF16)
    R1 = sb("R1", [128, 8192], F32)
    R1b = R1.bitcast(BF16)
    qT = R1b[:, 0:8192].rearrange("p (c n) -> p c n", c=4)
    kT = R1b[:, 8192:16384].rearrange("p (c n) -> p c n", c=4)
    X0 = R1[:, 0:4096].rearrange("p (c n) -> p c n", c=2)
    X1 = R1[:, 4096:8192].rearrange("p (c n) -> p c n", c=2)
    W1q = R1b[:, 0:8192].rearrange("p (c n) -> p c n", c=8)
    W2q = R1b[:, 8192:16384].rearrange("p (c n) -> p c n", c=8)
    uT = sb("uT", [128, 4, S], BF16)
    R2 = sb("R2", [128, 2048], F32)
    Oacc = R2
    R2b = R2.bitcast(BF16)
    XB = R2b[:, 0:4096].rearrange("p (c n) -> p c n", c=2)
    aT = R2b[:, 0:4096].rearrange("p (c n) -> p c n", c=8)
    vh = sb("vh", [128, 1, 16, 128], BF16)
    PT = [sb("PT%d" % i, [128, 256], BF16) for i in range(3)]
    tmpA = sb("tmpA", [128, 2048], F32)
    tmpB = sb("tmpB", [128, 512], F32)
    tmpC = sb("tmpC", [128, 512], F32)
    tbf2 = sb("tbf2", [128, 512], BF16)
    NWST = int(os.environ.get("KNWST", "2"))
    wsts = [sb("wst%d" % i, [128, 1024], F32) for i in range(NWST)]
    wst_ctr = [0]
    wb = sb("wb", [128, 1, 4096], BF16)
    xtm = tmpA[:, 0:1024]
    qk_tm = sb("qk_tm", [128, 512], BF16)
    v_tm = qk_tm
    identf = sb("identf", [128, 128], F32)
    identb = sb("identb", [128, 128], BF16)
    onesb = sb("onesb", [128, 128], BF16)
    maskb = sb("maskb", [128, 256], BF16)
    invf = sb("invf", [128, 32], F32)
    posf = sb("posf", [128, 16], F32)
    posi = sb("posi", [128, 16], I32)
    cosT = sb("cosT", [128, 16, 32], F32)
    sinT = sb("sinT", [128, 16, 32], F32)
    prm = sb("prm", [128, 512], F32)
    prm_stage = sb("prm_stage", [128, 128], F32)
    sA = sb("sA", [128, 16, 8], F32)
    LPr = sb("LPr", [128, 11, 16], F32)
    LPi = sb("LPi", [128, 11, 16], F32)
    LPn = sb("LPn", [128, 11, 16], F32)
    Bre = sb("Bre", [128, 16, 16], F32)
    Bim = sb("Bim", [128, 16, 16], F32)
    bbr = sb("bbr", [128, 16, 16], F32)
    bbi = sb("bbi", [128, 16, 16], F32)
    CT = sb("CT", [128, 2, 4, 128], BF16)
    Zf = sb("Zf", [128, 2, 128], F32)
    BbT = sb("BbT", [128, 2, 128], BF16)
    Cm = sb("Cm", [128, 2, 128], BF16)
    Dd = sb("Dd", [128, 4, 128], BF16)
    cstage = sb("cstage", [128, 64], F32)

    ps = [es.enter_context(nc.psum_tensor("ps%d" % i, [128, 512], F32)) for i in range(8)]
    psb = [p.bitcast(BF16) for p in ps]
    psB = [Buf("ps%d" % i, excl=True) for i in range(8)]
    bank_ctr = [0]
    bank_pool = [list(range(8))]

    def bank():
        pool = bank_pool[0]
        i = pool[bank_ctr[0] % len(pool)]
        bank_ctr[0] += 1
        return i

    B = {}

    def bf(name):
        if name not in B:
            B[name] = Buf(name)
        return B[name]

    hB = [[bf("h%d_%d" % (c, t)) for t in range(4)] for c in range(8)]
    hbB = [[bf("hb%d_%d" % (c, t)) for t in range(4)] for c in range(8)]
    R1B = bf("R1")
    uTB = [[bf("uT%d_%d" % (c, t)) for t in range(4)] for c in range(4)]
    R2B = bf("R2")

    def allhb():
        return [b for row in hbB for b in row]

    def allh():
        return [b for row in hB for b in row]

    def alluT():
        return [b for row in uTB for b in row]

    def dve(fn, reads, writes):
        return P.op("dve", fn, reads, writes)

    def act(fn, reads, writes):
        return P.op("act", fn, reads, writes)

    def pool(fn, reads, writes):
        return P.op("pool", fn, reads, writes)

    def mm(out, lhsT, rhs, start, stop, reads, writes, sig=None):
        if sig is None:
            sig = stop
        return P.op("pe", lambda h: h.matmul(out, lhsT=lhsT, rhs=rhs, start=start, stop=stop),
                    reads, writes, sig=sig)

    def tr(out, in_, ident, reads, writes):
        if os.environ.get("KTR", "") == "mm" and in_.dtype == F32:
            return P.op("pe", lambda h: h.matmul(out, lhsT=in_, rhs=ident, start=True, stop=True), reads, writes)
        return P.op("pe", lambda h: h.transpose(out=out, in_=in_, identity=ident), reads, writes)

    def load_w(dst, src, dstB, nparts=128):
        a, n = dst.shape[1], dst.shape[2]
        per = max(1, 1024 // n)
        for a0 in range(0, a, per):
            a1 = min(a, a0 + per)
            k = wst_ctr[0] % NWST
            wst_ctr[0] += 1
            wbuf = bf("wst%d" % k)
            st = wsts[k][:, 0:(a1 - a0) * n].rearrange("p (a n) -> p a n", a=a1 - a0)
            P.dma("sp", "w", lambda h, st=st, a0=a0, a1=a1: h.dma_start(out=st, in_=src[:, a0:a1, :]),
                  writes=[wbuf])
            pool(lambda h, st=st, a0=a0, a1=a1: h.tensor_copy(out=dst[:, a0:a1, :], in_=st),
                 [wbuf], [dstB])

    P.dma("sp", "misc", lambda h: h.dma_start(out=identf[:], in_=ident_d), writes=[bf("identf")])
    P.dma("sp", "misc", lambda h: h.dma_start(out=tmpB[:, 0:256], in_=mask_d), writes=[bf("tmpB")])
    P.dma("sp", "misc", lambda h: h.dma_start(out=invf[:], in_=invf_d), writes=[bf("invf")])
    dve(lambda h: h.tensor_copy(out=identb[:], in_=identf[:]), [bf("identf")], [bf("identb")])
    dve(lambda h: h.tensor_copy(out=maskb[:], in_=tmpB[:, 0:256]), [bf("tmpB")], [bf("maskb")])
    dve(lambda h: h.memset(onesb[:], 1.0), [], [bf("onesb")])
    if "vh" not in KSKIP:
        dve(lambda h: h.memset(vh[:, 0, :, :], 1.0), [], [bf("vh0")])

    prm_cols = {}
    col = [0]

    def load_T(name, src2d, R):
        if "prm" in KSKIP:
            prm_cols[name] = col[0]; col[0] += R
            return
        c0 = col[0]
        col[0] += R
        prm_cols[name] = c0
        P.dma("sp", "misc", lambda h: h.dma_start(out=prm_stage[0:R, :], in_=src2d), writes=[bf("prm_stage")])
        b = bank()
        tr(ps[b][:, 0:R], prm_stage[0:R, :], identf[0:R, 0:R], [bf("prm_stage"), bf("identf")], [psB[b]])
        dve(lambda h: h.tensor_copy(out=prm[:, c0:c0 + R], in_=ps[b][:, 0:R]), [psB[b]], [bf("prm")])
        return c0

    if not MINIO:
        load_T("b_ff1", b_ff1_d.rearrange("l (c p) -> (l c) p", p=128), NL * 32)
        for nm, ap_ in [("b_out", b_out_d), ("ln1_g", ln1_g_d), ("ln1_b", ln1_b_d), ("b_ff2", b_ff2_d),
                        ("ln2_g", ln2_g_d), ("ln2_b", ln2_b_d)]:
            load_T(nm, ap_.rearrange("l (c p) -> (l c) p", p=128), NL * 8)
        for nm, ap_ in [("attn_gain", attn_gain_d), ("ssm_gain", ssm_gain_d), ("b_glu", b_glu_d)]:
            load_T(nm, ap_.rearrange("l (c p) -> (l c) p", p=128), NL * 4)
        load_T("ssm_d", d_d.rearrange("l (c g) n -> (l c) (g n)", g=8), NL * 4)
        for nm, ap_ in [("a_re", a_re_d), ("a_im", a_im_d)]:
            load_T(nm, ap_.rearrange("l (gp two) p -> (l gp) (two p)", two=2), NL * 16)
    assert col[0] <= 512

    def pcol(name, idx):
        c = prm_cols[name] + idx
        return prm[:, c:c + 1]

    def sin_of(dst, src, add, n, rd, wr):
        ti = tmpB.bitcast(I32)[:, 0:n]
        tf = tmpC[:, 0:n]
        tg = tmpB[:, 0:n] if False else None
        if len(dst.shape) == 3:
            a_, b_ = dst.shape[1], dst.shape[2]
            ti = ti.rearrange("p (a b) -> p a b", a=a_)
            tf = tf.rearrange("p (a b) -> p a b", a=a_)
        T = [bf("tmpB"), bf("tmpC")]
        dve(lambda h: h.tensor_scalar(out=dst, in0=src, scalar1=float(add), scalar2=None, op0=ALU.add), rd, wr)
        dve(lambda h: h.tensor_scalar(out=tf, in0=dst, scalar1=1.0 / TWO_PI, scalar2=None, op0=ALU.mult), wr, T)
        dve(lambda h: h.tensor_copy(out=ti, in_=tf), T, T)
        dve(lambda h: h.tensor_copy(out=tf, in_=ti), T, T)
        dve(lambda h: h.scalar_tensor_tensor(out=dst, in0=tf, scalar=-TWO_PI, in1=dst, op0=ALU.mult, op1=ALU.add), T + wr, wr)
        dve(lambda h: h.tensor_scalar(out=tf, in0=dst, scalar1=-math.pi, scalar2=1e30, op0=ALU.add, op1=ALU.mult), wr, T)
        dve(lambda h: h.tensor_scalar(out=tf, in0=tf, scalar1=0.0, scalar2=1.0, op0=ALU.max, op1=ALU.min), T, T)
        dve(lambda h: h.scalar_tensor_tensor(out=dst, in0=tf, scalar=-TWO_PI, in1=dst, op0=ALU.mult, op1=ALU.add), T + wr, wr)
        act(lambda h: h.activation(out=dst, in_=dst, func=AF.Sin), wr, wr)

    def ssm_setup(l):
        T_ = bf("ssmtab")
        for gpar in range(2):
            src = bass.AP(log_dt_d.tensor, l * 32 + gpar, [[0, 64], [2, 16], [1, 1]])
            P.dma("sp", "misc", lambda h, src=src, gpar=gpar: h.dma_start(
                out=sA[gpar * 64:(gpar + 1) * 64, :, 0:1], in_=src), writes=[T_])
        are = prm[:, prm_cols["a_re"] + l * 16: prm_cols["a_re"] + l * 16 + 16]
        aim = prm[:, prm_cols["a_im"] + l * 16: prm_cols["a_im"] + l * 16 + 16]
        dt_ = sA[:, :, 0]
        mag = sA[:, :, 1]
        ang = sA[:, :, 2]
        t3 = sA[:, :, 3]
        t4 = sA[:, :, 4]
        t5 = sA[:, :, 5]
        t6 = sA[:, :, 6]
        t7 = sA[:, :, 7]
        rd = [T_, bf("prm")]
        act(lambda h: h.activation(out=dt_, in_=dt_, func=AF.Exp), rd, [T_])
        dve(lambda h: h.tensor_tensor(out=mag, in0=are, in1=dt_, op=ALU.mult), rd, [T_])
        act(lambda h: h.activation(out=mag, in_=mag, func=AF.Exp), rd, [T_])
        dve(lambda h: h.tensor_tensor(out=ang, in0=aim, in1=dt_, op=ALU.mult), rd, [T_])
        sin_of(t3, ang, 0.0, 16, rd, [T_])
        sin_of(t4, ang, 0.5 * math.pi, 16, rd, [T_])
        lr = LPr[:, 0, :]
        li = LPi[:, 0, :]
        dve(lambda h: h.tensor_tensor(out=lr, in0=mag, in1=t4, op=ALU.mult), rd, [T_])
        dve(lambda h: h.tensor_tensor(out=li, in0=mag, in1=t3, op=ALU.mult), rd, [T_])
        for k in range(1, 11):
            pr, pi_, nr_, ni_ = LPr[:, k - 1, :], LPi[:, k - 1, :], LPr[:, k, :], LPi[:, k, :]
            dve(lambda h, pr=pr, pi_=pi_: h.tensor_tensor(out=t5, in0=pr, in1=pr, op=ALU.mult), rd, [T_])
            dve(lambda h, pr=pr, pi_=pi_: h.tensor_tensor(out=t6, in0=pi_, in1=pi_, op=ALU.mult), rd, [T_])
            dve(lambda h, nr_=nr_: h.tensor_tensor(out=nr_, in0=t5, in1=t6, op=ALU.subtract), rd, [T_])
            dve(lambda h, pr=pr, pi_=pi_: h.tensor_tensor(out=t5, in0=pr, in1=pi_, op=ALU.mult), rd, [T_])
            dve(lambda h, ni_=ni_: h.tensor_scalar(out=ni_, in0=t5, scalar1=2.0, scalar2=None, op0=ALU.mult), rd, [T_])
        dve(lambda h: h.tensor_scalar(out=LPn[:], in0=LPi[:], scalar1=-1.0, scalar2=None, op0=ALU.mult), rd, [T_])
        dve(lambda h: h.tensor_scalar(out=t3, in0=lr, scalar1=-1.0, scalar2=None, op0=ALU.add), rd, [T_])
        dve(lambda h: h.tensor_tensor(out=t4, in0=are, in1=are, op=ALU.mult), rd, [T_])
        dve(lambda h: h.tensor_tensor(out=t5, in0=aim, in1=aim, op=ALU.mult), rd, [T_])
        dve(lambda h: h.tensor_tensor(out=t4, in0=t4, in1=t5, op=ALU.add), rd, [T_])
        dve(lambda h: h.reciprocal(out=t4, in_=t4), rd, [T_])
        dve(lambda h: h.tensor_tensor(out=t5, in0=t3, in1=are, op=ALU.mult), rd, [T_])
        dve(lambda h: h.tensor_tensor(out=t6, in0=li, in1=aim, op=ALU.mult), rd, [T_])
        dve(lambda h: h.tensor_tensor(out=t5, in0=t5, in1=t6, op=ALU.add), rd, [T_])
        dve(lambda h: h.tensor_tensor(out=t5, in0=t5, in1=t4, op=ALU.mult), rd, [T_])
        dve(lambda h: h.tensor_tensor(out=t6, in0=li, in1=are, op=ALU.mult), rd, [T_])
        dve(lambda h: h.tensor_tensor(out=t7, in0=t3, in1=aim, op=ALU.mult), rd, [T_])
        dve(lambda h: h.tensor_tensor(out=t6, in0=t6, in1=t7, op=ALU.subtract), rd, [T_])
        dve(lambda h: h.tensor_tensor(out=t6, in0=t6, in1=t4, op=ALU.mult), rd, [T_])
        for (dst, src_d) in [(Bre, b_re_d), (Bim, b_im_d)]:
            src = src_d[l].rearrange("(gp two) p n -> (two p) gp n", two=2)
            P.dma("sp", "misc", lambda h, dst=dst, src=src: h.dma_start(out=dst[:], in_=src), writes=[T_])
        crb = bass.AP(sA, 5, [[128, 128], [8, 16], [0, 16]])
        cib = bass.AP(sA, 6, [[128, 128], [8, 16], [0, 16]])
        dve(lambda h: h.tensor_tensor(out=bbr[:], in0=Bre[:], in1=crb, op=ALU.mult), rd, [T_])
        dve(lambda h: h.tensor_tensor(out=bbi[:], in0=Bim[:], in1=cib, op=ALU.mult), rd, [T_])
        dve(lambda h: h.tensor_tensor(out=bbr[:], in0=bbr[:], in1=bbi[:], op=ALU.subtract), rd, [T_])
        dve(lambda h: h.tensor_tensor(out=bbi[:], in0=Bim[:], in1=crb, op=ALU.mult), rd, [T_])
        dve(lambda h: h.tensor_tensor(out=Bre[:], in0=Bre[:], in1=cib, op=ALU.mult), rd, [T_])
        dve(lambda h: h.tensor_tensor(out=bbi[:], in0=bbi[:], in1=Bre[:], op=ALU.add), rd, [T_])
        for ri, src_d in enumerate([c_re_d, c_im_d]):
            for j in range(4):
                src = src_d[l, 8 * j:8 * j + 8].rearrange("g n p -> (g n) p")
                P.dma("sp", "misc", lambda h, src=src: h.dma_start(out=cstage[:], in_=src), writes=[bf("cstage")])
                b = bank()
                tr(ps[b][0:64, 0:128], cstage[:, :], identf[:], [bf("cstage"), bf("identf")], [psB[b]])
                if ri == 0:
                    dve(lambda h, b=b, j=j: h.tensor_copy(out=CT[0:64, 0, j, :], in_=ps[b][0:64, 0:128]), [psB[b]], [T_])
                else:
                    dve(lambda h, b=b, j=j: h.tensor_scalar(out=CT[0:64, 1, j, :], in0=ps[b][0:64, 0:128],
                                                           scalar1=-1.0, scalar2=None, op0=ALU.mult), [psB[b]], [T_])
        for j in range(4):
            dc = pcol("ssm_d", l * 4 + j)
            dve(lambda h, j=j, dc=dc: h.tensor_scalar(out=Dd[:, j, :], in0=identf[:], scalar1=dc, scalar2=None,
                                                    op0=ALU.mult), [bf("identf"), bf("prm")], [T_])

    def layer_norm(l, gname, bname):
        for t in range(4):
            sl = slice(t * 512, (t + 1) * 512)
            b1 = bank()
            b2 = bank()
            for c in range(8):
                act(lambda h, c=c, sl=sl: h.copy(out=tbf2[:], in_=h32[:, c, sl]), [hB[c][t]], [bf("tbf2")])
                mm(ps[b1][:], onesb[:], tbf2[:], c == 0, c == 7, [bf("tbf2"), bf("onesb")], [psB[b1]], sig=True)
                act(lambda h, c=c, sl=sl: h.activation(out=qk_tm[:], in_=h32[:, c, sl], func=AF.Square),
                    [hB[c][t]], [bf("qk_tm")])
                mm(ps[b2][:], onesb[:], qk_tm[:], c == 0, c == 7, [bf("qk_tm"), bf("onesb")], [psB[b2]], sig=True)
            act(lambda h, b1=b1: h.activation(out=tmpB[:], in_=ps[b1][:], func=AF.Identity, scale=1.0 / D), [psB[b1]], [bf("tmpB")])
            dve(lambda h: h.tensor_tensor(out=tmpC[:], in0=tmpB[:], in1=tmpB[:], op=ALU.mult), [bf("tmpB")], [bf("tmpC")])
            dve(lambda h, b2=b2: h.scalar_tensor_tensor(out=tmpC[:], in0=ps[b2][:], scalar=1.0 / D, in1=tmpC[:],
                                                 op0=ALU.mult, op1=ALU.subtract), [psB[b2], bf("tmpC")], [bf("tmpC")])
            act(lambda h: h.activation(out=tmpC[:], in_=tmpC[:], func=AF.Sqrt, bias=LN_EPS, scale=1.0), [bf("tmpC")], [bf("tmpC")])
            dve(lambda h: h.reciprocal(out=tmpC[:], in_=tmpC[:]), [bf("tmpC")], [bf("tmpC")])
            for c in range(8):
                g_ = pcol(gname, l * 8 + c)
                b_ = pcol(bname, l * 8 + c)
                dve(lambda h, c=c, sl=sl: h.tensor_tensor(out=h32[:, c, sl], in0=h32[:, c, sl], in1=tmpB[:], op=ALU.subtract),
                    [hB[c][t], bf("tmpB")], [hB[c][t]])
                dve(lambda h, c=c, sl=sl: h.tensor_tensor(out=h32[:, c, sl], in0=h32[:, c, sl], in1=tmpC[:], op=ALU.mult),
                    [hB[c][t], bf("tmpC")], [hB[c][t]])
                dve(lambda h, c=c, g_=g_, b_=b_, sl=sl: h.tensor_scalar(out=h32[:, c, sl], in0=h32[:, c, sl], scalar1=g_, scalar2=b_,
                                                                  op0=ALU.mult, op1=ALU.add), [hB[c][t], bf("prm")], [hB[c][t]])
                act(lambda h, c=c, sl=sl: h.copy(out=hb[:, c, sl], in_=h32[:, c, sl]), [hB[c][t]], [hbB[c][t]])

    def rms_half(l, c0, gname):
        bs = [bank() for _ in range(4)]
        for ci in range(4):
            c = c0 + ci
            for t in range(4):
                sl = slice(t * 512, (t + 1) * 512)
                dve(lambda h, c=c, sl=sl: h.tensor_tensor(out=tbf2[:], in0=hb[:, c, sl], in1=hb[:, c, sl], op=ALU.mult),
                    [hbB[c][t]], [bf("tbf2")])
                mm(ps[bs[t]][:], onesb[:], tbf2[:], ci == 0, ci == 3, [bf("tbf2"), bf("onesb")], [psB[bs[t]]], sig=True)
        for t in range(4):
            act(lambda h, t=t: h.activation(out=tmpA[:, t * 512:(t + 1) * 512], in_=ps[bs[t]][:], func=AF.Sqrt,
                                            bias=RMS_EPS, scale=1.0 / 512), [psB[bs[t]]], [bf("tmpA")])
        dve(lambda h: h.reciprocal(out=tmpA[:], in_=tmpA[:]), [bf("tmpA")], [bf("tmpA")])
        for ci in range(4):
            c = c0 + ci
            g_ = pcol(gname, l * 4 + ci)
            rd = [hbB[c][t] for t in range(4)]
            dve(lambda h, c=c, g_=g_: h.scalar_tensor_tensor(out=hb[:, c, :], in0=hb[:, c, :], scalar=g_, in1=tmpA[:],
                                                             op0=ALU.mult, op1=ALU.mult), rd + [bf("tmpA"), bf("prm")], rd)

    def block(s, l, first):
        if STAGE < 1:
            return
        for piece in range(2):
            slot = 0
            load_w(wb[:, slot, :].rearrange("p (a n) -> p a n", a=8),
                   w_in_d[l][:, piece * 512:(piece + 1) * 512].rearrange("(a p) n -> p a n", p=128), bf("wb%d" % slot))
            wv = wb[:, slot, :].rearrange("p (a n) -> p a n", a=8)
            dstT = qT if piece == 0 else kT
            qkbufs = [(qk_tm, bf("qk_tm")), (tbf2, bf("tbf2"))]

            def qk_tail(tt, dstT=dstT):
                qb, qB = qkbufs[tt % 2]
                b2 = bank()
                for c in range(4):
                    tr(psb[b2][:, c * 128:(c + 1) * 128], qb[:, c * 128:(c + 1) * 128], identb[:],
                       [qB, bf("identb")], [psB[b2]])
                act(lambda h, b2=b2, tt=tt, dstT=dstT: h.copy(
                    out=dstT[:, :, tt * 128:(tt + 1) * 128],
                    in_=psb[b2][:, 0:512].rearrange("p (c n) -> p c n", c=4)), [psB[b2]], [R1B])

            for tt in range(16):
                t4 = tt // 4
                b = bank()
                for dk in range(8):
                    mm(ps[b][:], hb[:, dk, tt * 128:(tt + 1) * 128], wv[:, dk, :], dk == 0, dk == 7,
                       [hbB[dk][t4], bf("wb%d" % slot)], [psB[b]])
                if tt > 0:
                    qk_tail(tt - 1)
                qb, qB = qkbufs[tt % 2]
                pv = ps[b][:].rearrange("p (h e) -> p h e", h=8)
                q1, q2 = pv[:, :, 0:32], pv[:, :, 32:64]
                cb = bass.AP(cosT, tt * 32, [[512, 128], [0, 8], [1, 32]])
                sbb = bass.AP(sinT, tt * 32, [[512, 128], [0, 8], [1, 32]])
                ta = tmpB[:, 0:256].rearrange("p (h e) -> p h e", h=8)
                tb_ = tmpB[:, 256:512].rearrange("p (h e) -> p h e", h=8)
                ov = qb[:].rearrange("p (h e) -> p h e", h=8)
                rdc = [psB[b], bf("rope"), bf("ropes")]
                dve(lambda h, q1=q1, cb=cb, ta=ta: h.tensor_tensor(out=ta, in0=q1, in1=cb, op=ALU.mult), rdc, [bf("tmpB")])
                dve(lambda h, q2=q2, sbb=sbb, tb_=tb_: h.tensor_tensor(out=tb_, in0=q2, in1=sbb, op=ALU.mult), rdc, [bf("tmpB")])
                dve(lambda h, ta=ta, tb_=tb_, ov=ov: h.tensor_tensor(out=ov[:, :, 0:32], in0=ta, in1=tb_, op=ALU.subtract),
                    [bf("tmpB")], [qB])
                dve(lambda h, q1=q1, sbb=sbb, ta=ta: h.tensor_tensor(out=ta, in0=q1, in1=sbb, op=ALU.mult), rdc, [bf("tmpB")])
                dve(lambda h, q2=q2, cb=cb, tb_=tb_: h.tensor_tensor(out=tb_, in0=q2, in1=cb, op=ALU.mult), rdc, [bf("tmpB")])
                dve(lambda h, ta=ta, tb_=tb_, ov=ov: h.tensor_tensor(out=ov[:, :, 32:64], in0=ta, in1=tb_, op=ALU.add),
                    [bf("tmpB")], [qB])
            qk_tail(15)
        load_w(wb[:, 0, :].rearrange("p (a n) -> p a n", a=8),
               w_in_d[l][:, 1024:1536].rearrange("(a p) n -> p a n", p=128), bf("wb0"))
        wv = wb[:, 0, :].rearrange("p (a n) -> p a n", a=8)
        for tt in range(16):
            t4 = tt // 4
            b = bank()
            for dk in range(8):
                mm(ps[b][:], hb[:, dk, tt * 128:(tt + 1) * 128], wv[:, dk, :], dk == 0, dk == 7,
                   [hbB[dk][t4], bf("wb0")], [psB[b]])
            vb, vB = (qk_tm, bf("qk_tm")) if tt % 2 == 0 else (tbf2, bf("tbf2"))
            act(lambda h, b=b, vb=vb: h.copy(out=vb[:], in_=ps[b][:]), [psB[b]], [vB])
            P.dma("sp", "v", lambda h, tt=tt, vb=vb: h.dma_start(out=vscr[tt * 128:(tt + 1) * 128, :], in_=vb[:]),
                  reads=[vB], writes=[bf("vscr")])
        load_w(wb[:, 0, :].rearrange("p (a n) -> p a n", a=8),
               w_in_d[l][:, 1536:2048].rearrange("(a p) n -> p a n", p=128), bf("wb0"))
        wv = wb[:, 0, :].rearrange("p (a n) -> p a n", a=8)
        for c in range(4):
            for t in range(4):
                b = bank()
                for dk in range(8):
                    mm(ps[b][:], wv[:, dk, c * 128:(c + 1) * 128], hb[:, dk, t * 512:(t + 1) * 512], dk == 0, dk == 7,
                       [hbB[dk][t], bf("wb0")], [psB[b]])
                act(lambda h, b=b, c=c, t=t: h.copy(out=uT[:, c, t * 512:(t + 1) * 512], in_=ps[b][:]), [psB[b]], [uTB[c][t]])

        if STAGE < 2:
            return
        def tokset(br, i):
            if br == 0:
                return slice(128 * i, 128 * i + 128)
            if br == 1:
                r, J = i // 4, i % 4
                return slice(512 * J + r, 512 * J + 512, 4)
            return slice(i, S, 16)

        def vslot(br, i):
            if br == 0:
                return i
            if br == 1:
                r, J = i // 4, i % 4
                return 4 * J + r
            return i

        subseqs = [
            (0, [list(range(16))]),
            (1, [[r * 4 + J for J in range(4)] for r in range(4)]),
            (2, [[r] for r in range(16)]),
        ]
        for hh in range(8):
            hs = slice((hh % 2) * 64, (hh % 2) * 64 + 64)
            cch = hh // 2
            vcol = vscr[:, 64 * hh:64 * hh + 64]
            srcs = [vcol.rearrange("(j p) e -> p j e", p=128),
                    vcol.rearrange("(J p r) e -> p J r e", J=4, r=4),
                    vcol.rearrange("(p r) e -> p r e", r=16)]
            items = []
            for (br, seqs) in subseqs:
                for seq in seqs:
                    for ii, i in enumerate(seq):
                        items.append((br, i, seq[ii + 1] if ii + 1 < len(seq) else None, ii > 0))
            st_pt = {}

            def stage_a(n):
                br, i, nxt, _ = items[n]
                ks = tokset(br, i)
                b = bank()
                mm(ps[b][:, 0:128], kT[hs, cch, ks], qT[hs, cch, ks], True, True, [R1B], [psB[b]], sig=nxt is None)
                if nxt is not None:
                    mm(ps[b][:, 128:256], kT[hs, cch, ks], qT[hs, cch, tokset(br, nxt)], True, True,
                       [R1B], [psB[b]], sig=True)
                w_ = 256 if nxt is not None else 128
                pt = PT[n % 3]
                ptB = bf("PT%d" % (n % 3))
                act(lambda h, b=b, pt=pt, w_=w_: h.activation(out=pt[:, 0:w_], in_=ps[b][:, 0:w_], func=AF.Exp, scale=0.125),
                    [psB[b]], [ptB])
                (dve if os.environ.get("KMASK", "dve") == "dve" else pool)(lambda h, pt=pt, w_=w_: h.tensor_tensor(out=pt[:, 0:w_], in0=pt[:, 0:w_], in1=maskb[:, 0:w_], op=ALU.mult),
                     [ptB, bf("maskb")], [ptB])
                st_pt[n] = (pt, ptB)

            def stage_b(n):
                br, i, nxt, has_prev = items[n]
                if n == 0 or items[n - 1][0] != br:
                    if br == 1:
                        pairs = [(vh[:, 0, 4 * J:4 * J + 4, 0:64], srcs[1][:, J]) for J in range(4)]
                    else:
                        pairs = [(vh[:, 0, 8 * hf:8 * hf + 8, 0:64], srcs[br][:, 8 * hf:8 * hf + 8]) for hf in range(2)]
                    for (dst, src) in pairs:
                        P.dma("sp", "vh", lambda h, dst=dst, src=src: h.dma_start(out=dst, in_=src),
                              reads=[bf("vscr")], writes=[bf("vh0")])
                ks = tokset(br, i)
                pt, ptB = st_pt[n]
                bo = bank()
                if has_prev:
                    ppt, pptB = st_pt[n - 1]
                    mm(ps[bo][:, 0:128], vh[:, 0, vslot(br, items[n - 1][1]), :], ppt[:, 128:256], True, False,
                       [bf("vh0"), pptB], [psB[bo]], sig=False)
                mm(ps[bo][:, 0:128], vh[:, 0, vslot(br, i), :], pt[:, 0:128], not has_prev, True,
                   [bf("vh0"), ptB], [psB[bo]], sig=True)
                if br == 0:
                    dve(lambda h, bo=bo, ks=ks: h.tensor_copy(out=Oacc[:, ks], in_=ps[bo][:, 0:128]), [psB[bo]], [R2B])
                else:
                    dve(lambda h, bo=bo, ks=ks: h.tensor_tensor(out=Oacc[:, ks], in0=Oacc[:, ks], in1=ps[bo][:, 0:128], op=ALU.add),
                        [psB[bo], R2B], [R2B])

            stage_a(0)
            for n in range(len(items)):
                if n + 1 < len(items):
                    stage_a(n + 1)
                stage_b(n)
            dve(lambda h: h.tensor_copy(out=tmpA[0:64, :], in_=Oacc[64:128, :]), [R2B], [bf("tmpA")])
            dve(lambda h: h.reciprocal(out=tmpA[0:64, :], in_=tmpA[0:64, :]), [bf("tmpA")], [bf("tmpA")])
            wr = [hbB[cch][t] for t in range(4)]
            dve(lambda h, hs=hs, cch=cch: h.tensor_tensor(out=hb[hs, cch, :], in0=Oacc[0:64, :], in1=tmpA[0:64, :], op=ALU.mult),
                [R2B, bf("tmpA")], wr)
        rms_half(l, 0, "attn_gain")

        if STAGE < 3:
            return
        ssm_setup(l)
        T_ = bf("ssmtab")
        bank_pool[0] = [4, 5, 6, 7]
        Xs = [X0, X1]
        XqB = [R1B, bf("X1B")]

        def ssm_bbt_bu(gp, q):
            j = gp // 4
            gl0 = (2 * gp) % 8
            Xq = Xs[q]
            wrs = [XqB[q]] if q == 0 else [XqB[1], R1B]
            for ri, srcB in enumerate([bbr, bbi]):
                dve(lambda h: h.memset(Zf[:, 0, :], 0.0), [], [bf("Zf")])
                dve(lambda h, srcB=srcB: h.tensor_copy(out=Zf[0:64, 0, 16 * gl0:16 * gl0 + 16], in_=srcB[0:64, gp, :]), [T_], [bf("Zf")])
                dve(lambda h, srcB=srcB: h.tensor_copy(out=Zf[64:128, 0, 16 * gl0 + 16:16 * gl0 + 32], in_=srcB[64:128, gp, :]), [T_], [bf("Zf")])
                b = bank()
                tr(ps[b][:, 0:128], Zf[:, 0, :], identf[:], [bf("Zf"), bf("identf")], [psB[b]])
                act(lambda h, b=b, ri=ri: h.copy(out=BbT[:, ri, :], in_=ps[b][:, 0:128]), [psB[b]], [bf("BbT")])
            for t in range(4):
                sl = slice(t * 512, (t + 1) * 512)
                for ri in range(2):
                    b = bank()
                    mm(ps[b][:], BbT[:, ri, :], uT[:, j, sl], True, True, [bf("BbT"), uTB[j][t]], [psB[b]])
                    act(lambda h, b=b, sl=sl, ri=ri: h.copy(out=Xq[:, ri, sl], in_=ps[b][:]), [psB[b]], wrs)

        def ssm_combine(gp, q, T, Sx, k):
            Xq = Xs[q]
            ar = LPr[:, k, gp:gp + 1]
            ai = LPi[:, k, gp:gp + 1]
            an = LPn[:, k, gp:gp + 1]
            rdx = [XqB[q], T_]
            wr = [XqB[q]]
            dve(lambda h: h.scalar_tensor_tensor(
                out=Xq[:, :, T], in0=Xq[:, :, Sx], scalar=ar, in1=Xq[:, :, T], op0=ALU.mult, op1=ALU.add), rdx, wr)
            dve(lambda h: h.scalar_tensor_tensor(
                out=Xq[:, 0, T], in0=Xq[:, 1, Sx], scalar=an, in1=Xq[:, 0, T], op0=ALU.mult, op1=ALU.add), rdx, wr)
            dve(lambda h: h.scalar_tensor_tensor(
                out=Xq[:, 1, T], in0=Xq[:, 0, Sx], scalar=ai, in1=Xq[:, 1, T], op0=ALU.mult, op1=ALU.add), rdx, wr)

        def ssm_cm_y(gp, q):
            j = gp // 4
            gl0 = (2 * gp) % 8
            Xq = Xs[q]
            dve(lambda h: h.memset(Zf[:], 0.0), [], [bf("Zf")])
            for ri in range(2):
                dve(lambda h, ri=ri: h.tensor_copy(out=Zf[0:64, ri, 16 * gl0:16 * gl0 + 16], in_=CT[0:64, ri, j, 16 * gl0:16 * gl0 + 16]), [T_], [bf("Zf")])
                dve(lambda h, ri=ri: h.tensor_copy(out=Zf[64:128, ri, 16 * gl0 + 16:16 * gl0 + 32], in_=CT[0:64, ri, j, 16 * gl0 + 16:16 * gl0 + 32]), [T_], [bf("Zf")])
            dve(lambda h: h.tensor_copy(out=Cm[:], in_=Zf[:]), [bf("Zf")], [bf("Cm")])
            act(lambda h: h.copy(out=XB[:], in_=Xq[:]), [XqB[q]], [R2B])
            for t in range(4):
                sl = slice(t * 512, (t + 1) * 512)
                first_gp = (gp % 4 == 0)
                last_gp = (gp % 4 == 3)
                mm(ps[t][:], Cm[:, 0, :], XB[:, 0, sl], first_gp, False, [bf("Cm"), R2B], [psB[t]], sig=False)
                mm(ps[t][:], Cm[:, 1, :], XB[:, 1, sl], False, False, [bf("Cm"), R2B], [psB[t]], sig=not last_gp)
                if last_gp:
                    mm(ps[t][:], Dd[:, j, :], uT[:, j, sl], False, True, [T_, uTB[j][t]], [psB[t]], sig=True)
                    act(lambda h, t=t: h.activation(out=tmpB[:], in_=ps[t][:], func=AF.Square), [psB[t]], [bf("tmpB")])
                    dve(lambda h: h.tensor_scalar(out=tmpB[:], in0=tmpB[:], scalar1=0.044715, scalar2=1.0, op0=ALU.mult, op1=ALU.add),
                        [bf("tmpB")], [bf("tmpB")])
                    dve(lambda h, t=t: h.tensor_tensor(out=tmpB[:], in0=tmpB[:], in1=ps[t][:], op=ALU.mult), [bf("tmpB"), psB[t]], [bf("tmpB")])
                    act(lambda h: h.activation(out=tmpB[:], in_=tmpB[:], func=AF.Sigmoid, scale=1.5957691216057308), [bf("tmpB")], [bf("tmpB")])
                    dve(lambda h, t=t, sl=sl: h.tensor_tensor(out=uT[:, j, sl], in0=tmpB[:], in1=ps[t][:], op=ALU.mult),
                        [bf("tmpB"), psB[t]], [uTB[j][t]])

        levels = []
        for k in range(11):
            st_, d = 1 << (k + 1), 1 << k
            levels.append((slice(st_ - 1, S, st_), slice(d - 1, S, st_), k))
        for k in range(9, -1, -1):
            st_, d = 1 << (k + 1), 1 << k
            levels.append((slice(st_ + d - 1, S, st_), slice(st_ - 1, S - d, st_), k))
        for gp0 in range(0, 16, 2):
            for q in range(2):
                ssm_bbt_bu(gp0 + q, q)
            for (T, Sx, k) in levels:
                for q in range(2):
                    ssm_combine(gp0 + q, q, T, Sx, k)
            for q in range(2):
                ssm_cm_y(gp0 + q, q)
        bank_pool[0] = list(range(8))
        load_w(wb[:, 0, 0:2048].rearrange("p (a n) -> p a n", a=4),
               w_glu_d[l].rearrange("(a p) n -> p a n", p=128), bf("wb0"))
        wg = wb[:, 0, 0:2048].rearrange("p (a n) -> p a n", a=4)
        for co in range(4):
            bg = pcol("b_glu", l * 4 + co)
            for t in range(4):
                sl = slice(t * 512, (t + 1) * 512)
                b = bank()
                for ci in range(4):
                    mm(ps[b][:], wg[:, ci, co * 128:(co + 1) * 128], uT[:, ci, sl], ci == 0, ci == 3, [bf("wb0"), uTB[ci][t]], [psB[b]])
                act(lambda h, b=b, bg=bg: h.activation(out=tmpB[:], in_=ps[b][:], func=AF.Sigmoid, bias=bg, scale=1.0),
                    [psB[b], bf("prm")], [bf("tmpB")])
                dve(lambda h, co=co, sl=sl: h.tensor_tensor(out=hb[:, 4 + co, sl], in0=uT[:, co, sl], in1=tmpB[:], op=ALU.mult),
                    [bf("tmpB"), uTB[co][t]], [hbB[4 + co][t]])
        rms_half(l, 4, "ssm_gain")

        if dbg == "mixed" and s == 0 and l == 0:
            for c in range(8):
                dve(lambda h, c=c: h.tensor_copy(out=tmpA[:], in_=hb[:, c, :]), [hbB[c][t] for t in range(4)], [bf("tmpA")])
                P.dma("sp", "st", lambda h, c=c: h.dma_start(out=dbg_d[:, c, :], in_=tmpA[:]), reads=[bf("tmpA")])

        if STAGE < 4:
            return
        for co in range(8):
            if co % 4 == 0:
                half = co // 4
                load_w(wb[:, 0, :].rearrange("p (a n) -> p a n", a=8),
                       w_out_d[l][:, half * 512:(half + 1) * 512].rearrange("(a p) n -> p a n", p=128), bf("wb0"))
            wv = wb[:, 0, :].rearrange("p (a n) -> p a n", a=8)
            cl = co % 4
            bo_ = pcol("b_out", l * 8 + co)
            for t in range(4):
                sl = slice(t * 512, (t + 1) * 512)
                b = bank()
                for dk in range(8):
                    mm(ps[b][:], wv[:, dk, cl * 128:(cl + 1) * 128], hb[:, dk, sl], dk == 0, dk == 7,
                       [bf("wb0"), hbB[dk][t]], [psB[b]])
                act(lambda h, b=b, bo_=bo_: h.activation(out=tmpB[:], in_=ps[b][:], func=AF.Identity, bias=bo_, scale=1.0),
                    [psB[b], bf("prm")], [bf("tmpB")])
                dve(lambda h, co=co, sl=sl: h.scalar_tensor_tensor(out=h32[:, co, sl], in0=h32[:, co, sl], scalar=ALPHA, in1=tmpB[:],
                                                                 op0=ALU.mult, op1=ALU.add), [hB[co][t], bf("tmpB")], [hB[co][t]])
        layer_norm(l, "ln1_g", "ln1_b")
        if dbg == "ln1" and s == 0 and l == 0:
            for c in range(8):
                P.dma("sp", "st", lambda h, c=c: h.dma_start(out=dbg_d[:, c, :], in_=h32[:, c, :]), reads=[hB[c][t] for t in range(4)])

        if STAGE < 5:
            return
        P.barrier()
        W1e = [R1b[:, 0:4096].rearrange("p (c n) -> p c n", c=8), R1b[:, 4096:8192].rearrange("p (c n) -> p c n", c=8)]
        W2e = [R1b[:, 8192:12288].rearrange("p (c n) -> p c n", c=4), R1b[:, 12288:16384].rearrange("p (c n) -> p c n", c=4)]

        def load_pass(pp):
            bi = pp % 2
            load_w(W1e[bi], w_ff1_d[l][:, pp * 512:(pp + 1) * 512].rearrange("(a p) n -> p a n", p=128), bf("W1e%d" % bi))
            load_w(W2e[bi], w_ff2_d[l][pp * 512:(pp + 1) * 512, :].rearrange("(a p) n -> p a n", p=128), bf("W2e%d" % bi))

        load_pass(0)
        tile_ctr = 0
        for pp in range(8):
            if pp + 1 < 8:
                load_pass(pp + 1)
            bi = pp % 2
            for t in range(4):
                sl = slice(t * 512, (t + 1) * 512)
                ab = (tile_ctr % 2) * 4
                tile_ctr += 1
                for fc in range(4):
                    b = bank()
                    b1_ = pcol("b_ff1", l * 32 + pp * 4 + fc)
                    for dk in range(8):
                        mm(ps[b][:], W1e[bi][:, dk, fc * 128:(fc + 1) * 128], hb[:, dk, sl], dk == 0, dk == 7,
                           [bf("W1e%d" % bi), hbB[dk][t]], [psB[b]])
                    act(lambda h, b=b, b1_=b1_: h.activation(out=tbf2[:], in_=ps[b][:], func=AF.Relu, bias=b1_, scale=1.0),
                        [psB[b], bf("prm")], [bf("tbf2")])
                    dve(lambda h, fc=fc, ab=ab: h.tensor_tensor(out=aT[:, ab + fc, :], in0=tbf2[:], in1=tbf2[:], op=ALU.mult),
                        [bf("tbf2")], [bf("aT%d" % (ab + fc))])
                for co in range(8):
                    b = bank()
                    for fc in range(4):
                        mm(ps[b][:], W2e[bi][:, fc, co * 128:(co + 1) * 128], aT[:, ab + fc, :], fc == 0, fc == 3,
                           [bf("W2e%d" % bi), bf("aT%d" % (ab + fc))], [psB[b]])
                    if pp == 0:
                        b2_ = pcol("b_ff2", l * 8 + co)
                        act(lambda h, b=b, b2_=b2_: h.activation(out=tmpB[:], in_=ps[b][:], func=AF.Identity, bias=b2_, scale=1.0),
                            [psB[b], bf("prm")], [bf("tmpB")])
                        dve(lambda h, co=co, sl=sl: h.scalar_tensor_tensor(out=h32[:, co, sl], in0=h32[:, co, sl], scalar=ALPHA, in1=tmpB[:],
                                                                         op0=ALU.mult, op1=ALU.add), [hB[co][t], bf("tmpB")], [hB[co][t]])
                    else:
                        dve(lambda h, b=b, co=co, sl=sl: h.tensor_tensor(out=h32[:, co, sl], in0=h32[:, co, sl], in1=ps[b][:], op=ALU.add),
                            [hB[co][t], psB[b]], [hB[co][t]])
        layer_norm(l, "ln2_g", "ln2_b")
        P.barrier()
        P.new_epoch()

    for s in range(NSEQ):
        if "rope" not in KSKIP: P.dma("sp", "misc", lambda h, s=s: h.dma_start(out=posi[:], in_=pos_d[s].rearrange("(j p) -> p j", p=128)),
              writes=[bf("posi")])
        if "rope" not in KSKIP:
            dve(lambda h: h.tensor_copy(out=posf[:], in_=posi[:]), [bf("posi")], [bf("posf")])
        for tt in (range(16) if "rope" not in KSKIP else []):
            pc = posf[:, tt:tt + 1]
            dve(lambda h, tt=tt, pc=pc: h.tensor_scalar(out=cosT[:, tt, :], in0=invf[:], scalar1=pc, scalar2=None, op0=ALU.mult),
                [bf("posf"), bf("invf")], [bf("rope")])
        rr = [bf("rope")]
        if "rope" not in KSKIP:
            sin_of(sinT[:], cosT[:], 0.0, 512, rr, [bf("ropes")])
            sin_of(cosT[:], cosT[:], 0.5 * math.pi, 512, rr + [bf("ropes")], rr)
        for tt in range(int(os.environ.get("KXL", "16"))):
            t4 = tt // 4
            tq = 0 if os.environ.get("KXOFF") else tt
            P.dma("sp", "ld", lambda h, s=s, tt=tq: h.dma_start(out=xtm, in_=x_d[s, tt * 128:(tt + 1) * 128, :]),
                  writes=[bf("tmpA")])
            for g in range(2):
                b = bank()
                for c4 in range(4):
                    c = g * 4 + c4
                    tr(ps[b][:, c4 * 128:(c4 + 1) * 128], xtm[:, c * 128:(c + 1) * 128], identf[:], [bf("tmpA"), bf("identf")], [psB[b]])
                wr = [hB[g * 4 + c4][t4] for c4 in range(4)]
                wr2 = [hbB[g * 4 + c4][t4] for c4 in range(4)]
                pv = ps[b][:].rearrange("p (c n) -> p c n", c=4)
                if "h32copy" not in KSKIP:
                    dve(lambda h, pv=pv, g=g, tt=tt: h.tensor_copy(out=h32[:, g * 4:g * 4 + 4, tt * 128:(tt + 1) * 128], in_=pv), [psB[b]], wr)
                if "hbcopy" not in KSKIP:
                    act(lambda h, pv=pv, g=g, tt=tt: h.copy(out=hb[:, g * 4:g * 4 + 4, tt * 128:(tt + 1) * 128], in_=pv), [psB[b]], wr2)
        for l in range(NL):
            block(s, l, False)
        for tt in range(int(os.environ.get("KXS", "16"))):
            t4 = tt // 4
            for g in range(2):
                b = bank()
                for c4 in range(4):
                    c = g * 4 + c4
                    tr(ps[b][:, c4 * 128:(c4 + 1) * 128], h32[:, c, tt * 128:(tt + 1) * 128], identf[:], [hB[c][t4], bf("identf")], [psB[b]])
                dve(lambda h, b=b, g=g: h.tensor_copy(out=xtm[:, g * 512:(g + 1) * 512], in_=ps[b][:]), [psB[b]], [bf("tmpA")])
            P.dma("sp", "st", lambda h, s=s, tt=tt: h.dma_start(out=out_d[s, tt * 128:(tt + 1) * 128, :], in_=xtm),
                  reads=[bf("tmpA")])
        P.barrier()
    P.barrier()
    with nc.allow_non_contiguous_dma(reason="small strided param loads"), nc.Block() as blk:
        P.emit_all(blk)
    es.close()
    return nc, P


def consts():
    ident = np.eye(128, dtype=np.float32)
    k = np.arange(128)[:, None]
    q = np.arange(128)[None, :]
    mask = np.concatenate([(k <= q), (k >= q)], axis=1).astype(np.float32)
    invf = (10000.0 ** (-np.arange(32, dtype=np.float32) * 2.0 / 64.0)).astype(np.float32)
    invf = np.broadcast_to(invf[None, :], (128, 32)).copy()
    return {"c_ident": ident, "c_mask": mask, "c_invf": invf}


WNAMES = ["w_in", "attn_gain", "ssm_gain", "ssm_a_re", "ssm_a_im", "ssm_log_dt", "ssm_b_re", "ssm_b_im",
          "ssm_c_re", "ssm_c_im", "ssm_d", "w_glu", "b_glu", "w_out", "b_out", "ln1_g", "ln1_b",
          "w_ff1", "b_ff1", "w_ff2", "b_ff2", "ln2_g", "ln2_b"]

_CACHE = {}

LAUNCH_NL = 4
LAUNCH_NSEQ = 4


def kernel(**inputs):
    x = np.ascontiguousarray(inputs["x"], dtype=np.float32)
    pos = np.ascontiguousarray(inputs["positions"], dtype=np.int32)
    per_core = x.shape[0] // NCORES
    key = (LAUNCH_NL, LAUNCH_NSEQ)
    if key not in _CACHE:
        _CACHE[key] = build(LAUNCH_NL, LAUNCH_NSEQ)[0]
    nc = _CACHE[key]
    cs = consts()
    h = x
    for l0 in range(0, DEPTH, LAUNCH_NL):
        base = {n: np.ascontiguousarray(inputs[n][l0:l0 + LAUNCH_NL], dtype=np.float32) for n in WNAMES}
        base.update(cs)
        newh = np.empty_like(h)
        for s0 in range(0, per_core, LAUNCH_NSEQ):
            in_maps = []
            for c in range(NCORES):
                m = dict(base)
                lo = c * per_core + s0
                m["x"] = np.ascontiguousarray(h[lo:lo + LAUNCH_NSEQ])
                m["positions"] = np.ascontiguousarray(pos[lo:lo + LAUNCH_NSEQ])
                in_maps.append(m)
            res = run_bass_kernel_spmd(nc, in_maps, core_ids=list(range(NCORES)))
            for c in range(NCORES):
                lo = c * per_core + s0
                newh[lo:lo + LAUNCH_NSEQ] = res.results[c]["out"]
        h = newh
    return h
```
